# Optimizing a Trainium2 kernel written in Bass

```python
import jax, jax.numpy as jnp
from jax import lax
import numpy as np

D_MODEL = 2048
BATCH = 4
SEQ = 4096
DEPTH = 1
DEC_BATCH = 1
DEC_SEQ = 16384
PAST_LEN = 128

N_MEM = 256
D_INNER = 2 * D_MODEL
D_SSD = D_INNER // 2
SSD_HEADDIM = 64
SSD_HEADS = D_SSD // SSD_HEADDIM
SSD_GROUPS = 4
SSD_HPG = SSD_HEADS // SSD_GROUPS
SSD_STATE = 128
SSD_CHUNK = 128
CONV_WIDTH = 5
D_CONV = D_SSD + 2 * SSD_GROUPS * SSD_STATE
D_GLA_V = D_INNER - D_SSD
D_GLA_K = D_GLA_V // 2
GLA_HEADS = 4
GLA_HEAD_K = D_GLA_K // GLA_HEADS
GLA_HEAD_V = D_GLA_V // GLA_HEADS
GLA_RANK = 16
GLA_NORMALIZER = 16.0
GLA_CHUNK = 64
XATTN_HEADS = 4
XATTN_HEAD_DIM = D_MODEL // XATTN_HEADS
D_FF = ((8 * D_MODEL // 3 + 255) // 256) * 256
EPS = 1e-6
IN_SPLITS = (D_SSD,
             D_CONV,
             2 * SSD_HEADS,
             D_GLA_K,
             D_GLA_K,
             D_GLA_V,
             2 * GLA_RANK,
             D_GLA_V)
D_IN_PROJ = D_SSD + D_CONV + 2 * SSD_HEADS + 2 * D_GLA_K + D_GLA_V + 2 * GLA_RANK + D_GLA_V

kernel_name = "hymba_ssd_gla_macaron_memory_encoder"


def rmsnorm(x, w):
    xf = x.astype(jnp.float32)
    y = xf * lax.rsqrt(jnp.mean(xf * xf, axis=-1, keepdims=True) + EPS)
    return (y * w.astype(jnp.float32)).astype(x.dtype)


def rev(t):
    return jnp.flip(t, axis=1)


def swiglu(x, w1, w3, w2):
    return (jax.nn.silu(x @ w1) * (x @ w3)) @ w2


def centred_depthwise_conv(x, w, b):
    pad = CONV_WIDTH // 2
    y = lax.conv_general_dilated(x, w[:, None, :].astype(x.dtype), window_strides=(1,),
                                 padding=[(pad, pad)], dimension_numbers=('NWC', 'WIO', 'NWC'),
                                 feature_group_count=x.shape[-1])
    return y + b.astype(x.dtype)


def ssd_causal(x, dt, A, bm, cm):
    b, L, H, P = x.shape
    Q, G, R, N = SSD_CHUNK, SSD_GROUPS, SSD_HPG, SSD_STATE
    nc = L // Q
    f32 = jnp.float32
    x = x.astype(f32).reshape(b, nc, Q, G, R, P)
    dt = dt.astype(f32).reshape(b, nc, Q, G, R)
    bm = bm.astype(f32).reshape(b, nc, Q, G, N)
    cm = cm.astype(f32).reshape(b, nc, Q, G, N)
    acum = jnp.cumsum(dt * A.astype(f32).reshape(G, R), axis=2)
    causal = jnp.tril(jnp.ones((Q, Q), dtype=bool))
    seg = acum[:, :, :, None] - acum[:, :, None]
    decay = jnp.exp(jnp.where(causal[:, :, None, None], seg, -jnp.inf))
    cb = jnp.einsum('bcign,bcjgn->bcijg', cm, bm)
    xdt = x * dt[..., None]
    y_diag = jnp.einsum('bcijgr,bcjgrp->bcigrp', cb[..., None] * decay, xdt)
    x_end = xdt * jnp.exp(acum[:, :, -1:] - acum)[..., None]
    chunk_decay = jnp.exp(acum[:, :, -1])

    def step(S, inp):
        b_c, x_c, c_c, ea_c, dec_c = inp
        y = jnp.einsum('bign,bgrnp->bigrp', c_c, S) * ea_c[..., None]
        S = S * dec_c[..., None, None] + jnp.einsum('bjgn,bjgrp->bgrnp', b_c, x_c)
        return S, y

    S0 = jnp.zeros((b, G, R, N, P), f32)
    xs = (jnp.moveaxis(bm, 1, 0), jnp.moveaxis(x_end, 1, 0), jnp.moveaxis(cm, 1, 0),
          jnp.moveaxis(jnp.exp(acum), 1, 0), jnp.moveaxis(chunk_decay, 1, 0))
    _, y_off = lax.scan(step, S0, xs)
    y = y_diag + jnp.moveaxis(y_off, 0, 1)
    return y.reshape(b, L, H, P)


def gla_causal(q, k, v, gk):
    b, L, H, K = q.shape
    V = v.shape[-1]
    Q = GLA_CHUNK
    nc = L // Q
    f32 = jnp.float32
    q = q.astype(f32).reshape(b, nc, Q, H, K)
    k = k.astype(f32).reshape(b, nc, Q, H, K)
    v = v.astype(f32).reshape(b, nc, Q, H, V)
    G = jnp.cumsum(gk.astype(f32).reshape(b, nc, Q, H, K), axis=2)
    g_mid = G[:, :, Q // 2 - 1:Q // 2]
    a = jnp.einsum('bcihk,bcjhk->bchij', q * jnp.exp(G - g_mid), k * jnp.exp(g_mid - G))
    causal = jnp.tril(jnp.ones((Q, Q), dtype=bool))
    a = jnp.where(causal, a, 0.0)
    o_intra = jnp.einsum('bchij,bcjhv->bcihv', a, v)
    q_in = q * jnp.exp(G)
    k_end = k * jnp.exp(G[:, :, -1:] - G)
    dec = jnp.exp(G[:, :, -1])

    def step(S, inp):
        qc, kc, vc, dc = inp
        o = jnp.einsum('bihk,bhkv->bihv', qc, S)
        S = S * dc[..., None] + jnp.einsum('bjhk,bjhv->bhkv', kc, vc)
        return S, o

    S0 = jnp.zeros((b, H, K, V), f32)
    xs = (jnp.moveaxis(q_in, 1, 0), jnp.moveaxis(k_end, 1, 0), jnp.moveaxis(v, 1, 0),
          jnp.moveaxis(dec, 1, 0))
    _, o_inter = lax.scan(step, S0, xs)
    o = o_intra + jnp.moveaxis(o_inter, 0, 1)
    return o.reshape(b, L, H, V)


def hybrid_mixer(h, w_in, conv_w, conv_b, dt_bias_fwd, dt_bias_bwd, a_log_fwd, a_log_bwd,
                 d_skip, ssd_norm, gla_gate_w_fwd, gla_gate_b_fwd, gla_gate_w_bwd, gla_gate_b_bwd,
                 gla_norm, w_out):
    b, L, _ = h.shape
    f32 = jnp.float32
    split_points = np.cumsum(IN_SPLITS)[:-1].tolist()
    z, xbc, dt_raw, q, k, v, g_low, g_out = jnp.split(h @ w_in, split_points, axis=-1)

    xbc = jax.nn.silu(centred_depthwise_conv(xbc, conv_w, conv_b))
    xs, bm, cm = jnp.split(xbc, [D_SSD, D_SSD + SSD_GROUPS * SSD_STATE], axis=-1)
    xs = xs.reshape(b, L, SSD_HEADS, SSD_HEADDIM)
    bm = bm.reshape(b, L, SSD_GROUPS, SSD_STATE)
    cm = cm.reshape(b, L, SSD_GROUPS, SSD_STATE)
    dt_raw = dt_raw.astype(f32)
    dt_f = jax.nn.softplus(dt_raw[..., :SSD_HEADS] + dt_bias_fwd.astype(f32))
    dt_b = jax.nn.softplus(dt_raw[..., SSD_HEADS:] + dt_bias_bwd.astype(f32))
    y = (ssd_causal(xs, dt_f, -jnp.exp(a_log_fwd.astype(f32)), bm, cm)
         + rev(ssd_causal(rev(xs), rev(dt_b), -jnp.exp(a_log_bwd.astype(f32)), rev(bm), rev(cm))))
    y = y + d_skip.astype(f32)[:, None] * xs.astype(f32)
    y = y.reshape(b, L, D_SSD) * jax.nn.silu(z.astype(f32))
    y = rmsnorm(y.reshape(b, L, SSD_GROUPS, D_SSD // SSD_GROUPS),
                ssd_norm.reshape(SSD_GROUPS, D_SSD // SSD_GROUPS)).reshape(b, L, D_SSD)

    q = q.reshape(b, L, GLA_HEADS, GLA_HEAD_K) * GLA_HEAD_K ** -0.5
    k = k.reshape(b, L, GLA_HEADS, GLA_HEAD_K)
    v = v.reshape(b, L, GLA_HEADS, GLA_HEAD_V)
    low_f, low_b = jnp.split(g_low, 2, axis=-1)
    gk_f = jax.nn.log_sigmoid((low_f @ gla_gate_w_fwd + gla_gate_b_fwd).astype(f32)) / GLA_NORMALIZER
    gk_b = jax.nn.log_sigmoid((low_b @ gla_gate_w_bwd + gla_gate_b_bwd).astype(f32)) / GLA_NORMALIZER
    gk_f = gk_f.reshape(b, L, GLA_HEADS, GLA_HEAD_K)
    gk_b = gk_b.reshape(b, L, GLA_HEADS, GLA_HEAD_K)
    o = gla_causal(q, k, v, gk_f) + rev(gla_causal(rev(q), rev(k), rev(v), rev(gk_b)))
    o = rmsnorm(o, gla_norm).reshape(b, L, D_GLA_V) * jax.nn.silu(g_out.astype(f32))

    mixed = jnp.concatenate([y, o], axis=-1).astype(h.dtype)
    return mixed @ w_out


def memory_cross_attention(h, mem, mem_norm, w_cq, w_ckv, w_co):
    b, L, _ = h.shape
    m = rmsnorm(mem, mem_norm)
    q = (h @ w_cq).reshape(b, L, XATTN_HEADS, XATTN_HEAD_DIM)
    k, v = jnp.split(m @ w_ckv, 2, axis=-1)
    k = k.reshape(b, N_MEM, XATTN_HEADS, XATTN_HEAD_DIM)
    v = v.reshape(b, N_MEM, XATTN_HEADS, XATTN_HEAD_DIM)
    s = jnp.einsum('blhd,bmhd->bhlm', q, k).astype(jnp.float32) * XATTN_HEAD_DIM ** -0.5
    p = jax.nn.softmax(s, axis=-1).astype(v.dtype)
    o = jnp.einsum('bhlm,bmhd->blhd', p, v).reshape(b, L, D_MODEL)
    return o @ w_co


def encoder_layer(x, mem, ffn1_norm, ffn1_w1, ffn1_w3, ffn1_w2,
                  mix_norm, w_in, conv_w, conv_b, dt_bias_fwd, dt_bias_bwd, a_log_fwd, a_log_bwd,
                  d_skip, ssd_norm, gla_gate_w_fwd, gla_gate_b_fwd, gla_gate_w_bwd, gla_gate_b_bwd,
                  gla_norm, w_out, xattn_norm, mem_norm, w_cq, w_ckv, w_co,
                  ffn2_norm, ffn2_w1, ffn2_w3, ffn2_w2):
    x = x + 0.5 * swiglu(rmsnorm(x, ffn1_norm), ffn1_w1, ffn1_w3, ffn1_w2)
    x = x + hybrid_mixer(rmsnorm(x, mix_norm), w_in, conv_w, conv_b, dt_bias_fwd, dt_bias_bwd,
                         a_log_fwd, a_log_bwd, d_skip, ssd_norm, gla_gate_w_fwd, gla_gate_b_fwd,
                         gla_gate_w_bwd, gla_gate_b_bwd, gla_norm, w_out)
    x = x + memory_cross_attention(rmsnorm(x, xattn_norm), mem, mem_norm, w_cq, w_ckv, w_co)
    x = x + 0.5 * swiglu(rmsnorm(x, ffn2_norm), ffn2_w1, ffn2_w3, ffn2_w2)
    return x


def trunk(x, mem, layer_stack, final_norm):
    for l in range(DEPTH):
        x = encoder_layer(x, mem, *[p[l] for p in layer_stack])
    return rmsnorm(x, final_norm)


def setup_inputs(seed: int = 0) -> dict:
    key = jax.random.key(seed)
    ks = iter(jax.random.split(key, 48))
    f32 = jnp.float32

    def nrm(shape, scale):
        return jax.random.normal(next(ks), shape, f32) * scale

    def gain(shape):
        return 1.0 + nrm(shape, 0.05)

    def dt_bias(shape):
        u = jax.random.uniform(next(ks), shape, f32, minval=float(np.log(1e-3)), maxval=float(np.log(1e-1)))
        dt = jnp.exp(u)
        return dt + jnp.log(-jnp.expm1(-dt))

    def a_log(shape):
        return jnp.log(jax.random.uniform(next(ks), shape, f32, minval=1.0, maxval=16.0))

    Lr = DEPTH
    return {
        'x_prompt': nrm((BATCH, SEQ, D_MODEL), 1.0),
        'x_sample': nrm((DEC_BATCH, DEC_SEQ, D_MODEL), 1.0),
        'mem_prompt': nrm((BATCH, N_MEM, D_MODEL), 1.0),
        'mem_sample': nrm((DEC_BATCH, N_MEM, D_MODEL), 1.0),
        'ffn1_norm': gain((Lr, D_MODEL)),
        'ffn1_w1': nrm((Lr, D_MODEL, D_FF), D_MODEL ** -0.5),
        'ffn1_w3': nrm((Lr, D_MODEL, D_FF), D_MODEL ** -0.5),
        'ffn1_w2': nrm((Lr, D_FF, D_MODEL), D_FF ** -0.5),
        'mix_norm': gain((Lr, D_MODEL)),
        'w_in': nrm((Lr, D_MODEL, D_IN_PROJ), D_MODEL ** -0.5),
        'conv_w': nrm((Lr, CONV_WIDTH, D_CONV), CONV_WIDTH ** -0.5),
        'conv_b': nrm((Lr, D_CONV), 0.02),
        'dt_bias_fwd': dt_bias((Lr, SSD_HEADS)),
        'dt_bias_bwd': dt_bias((Lr, SSD_HEADS)),
        'a_log_fwd': a_log((Lr, SSD_HEADS)),
        'a_log_bwd': a_log((Lr, SSD_HEADS)),
        'd_skip': gain((Lr, SSD_HEADS)),
        'ssd_norm': gain((Lr, D_SSD)),
        'gla_gate_w_fwd': nrm((Lr, GLA_RANK, D_GLA_K), GLA_RANK ** -0.5),
        'gla_gate_b_fwd': nrm((Lr, D_GLA_K), 0.1),
        'gla_gate_w_bwd': nrm((Lr, GLA_RANK, D_GLA_K), GLA_RANK ** -0.5),
        'gla_gate_b_bwd': nrm((Lr, D_GLA_K), 0.1),
        'gla_norm': gain((Lr, GLA_HEAD_V)),
        'w_out': nrm((Lr, D_INNER, D_MODEL), D_INNER ** -0.5),
        'xattn_norm': gain((Lr, D_MODEL)),
        'mem_norm': gain((Lr, D_MODEL)),
        'w_cq': nrm((Lr, D_MODEL, D_MODEL), D_MODEL ** -0.5),
        'w_ckv': nrm((Lr, D_MODEL, 2 * D_MODEL), D_MODEL ** -0.5),
        'w_co': nrm((Lr, D_MODEL, D_MODEL), D_MODEL ** -0.5),
        'ffn2_norm': gain((Lr, D_MODEL)),
        'ffn2_w1': nrm((Lr, D_MODEL, D_FF), D_MODEL ** -0.5),
        'ffn2_w3': nrm((Lr, D_MODEL, D_FF), D_MODEL ** -0.5),
        'ffn2_w2': nrm((Lr, D_FF, D_MODEL), D_FF ** -0.5),
        'final_norm': gain((D_MODEL,)),
    }


def reference(x_prompt, x_sample, mem_prompt, mem_sample,
              ffn1_norm, ffn1_w1, ffn1_w3, ffn1_w2,
              mix_norm, w_in, conv_w, conv_b, dt_bias_fwd, dt_bias_bwd, a_log_fwd, a_log_bwd,
              d_skip, ssd_norm, gla_gate_w_fwd, gla_gate_b_fwd, gla_gate_w_bwd, gla_gate_b_bwd,
              gla_norm, w_out, xattn_norm, mem_norm, w_cq, w_ckv, w_co,
              ffn2_norm, ffn2_w1, ffn2_w3, ffn2_w2, final_norm):
    layer_stack = (ffn1_norm, ffn1_w1, ffn1_w3, ffn1_w2,
                   mix_norm, w_in, conv_w, conv_b, dt_bias_fwd, dt_bias_bwd, a_log_fwd, a_log_bwd,
                   d_skip, ssd_norm, gla_gate_w_fwd, gla_gate_b_fwd, gla_gate_w_bwd, gla_gate_b_bwd,
                   gla_norm, w_out, xattn_norm, mem_norm, w_cq, w_ckv, w_co,
                   ffn2_norm, ffn2_w1, ffn2_w3, ffn2_w2)
    y_prompt = trunk(x_prompt, mem_prompt, layer_stack, final_norm)
    y_sample = trunk(x_sample, mem_sample, layer_stack, final_norm)
    return (y_prompt, y_sample)
```

```python
import contextlib
import numpy as np
import concourse.bass as bass
import concourse.mybir as mybir
from concourse.bass_utils import run_bass_kernel_spmd

F32 = mybir.dt.float32
BF16 = mybir.dt.bfloat16
AF = mybir.ActivationFunctionType
ALU = mybir.AluOpType

D = 2048
DFF = 5632
DIN = 11360
NMEM = 256
EPS = 1e-6
OFF_Z, OFF_XBC, OFF_DT, OFF_Q, OFF_K, OFF_V, OFF_LOW, OFF_GO = 0, 2048, 5120, 5184, 6208, 7232, 9280, 9312
ENGS = ("pe", "act", "dve", "pool", "sp")
DEBUG = ()
NRING = 24
NROWS = 64 + 64 + 32 + 2048 + 512


class Trk:
    __slots__ = ("w", "r")

    def __init__(self):
        self.w = None
        self.r = {}


class Tile:
    def __init__(self, ap=None, n=1):
        self.ap = ap
        self.ts = [Trk() for _ in range(n)]


class KB:
    def __init__(self, nc, stack):
        self.nc = nc
        self.q = {e: [] for e in ENGS}
        self.cnt = {e: 0 for e in ENGS}
        self.seen = {e: {} for e in ENGS}
        self.esem = {e: stack.enter_context(nc.semaphore("es_" + e)) for e in ENGS}
        self.ring = {qn: [stack.enter_context(nc.semaphore(f"dq_{qn}{i}")) for i in range(NRING)] for qn in ("sp", "pool", "act")}
        self.ringcnt = {qn: [0] * NRING for qn in self.ring}
        self.ringpos = {qn: 0 for qn in self.ring}
        self.semid = {}
        self.collsem = [stack.enter_context(nc.semaphore(f"coll{i}")) for i in range(12)]
        self.ncoll = 0
        self.colltoks = []
        self.arena = nc.alloc_sbuf_tensor("arena", [128, 50 * 1024], F32)
        self.aoff = 0
        self.persist = 0
        self.ps = [Tile(nc.alloc_psum_tensor(f"ps{i}", [128, 512], F32)[:, :]) for i in range(8)]
        self.pp = {}

    def sb(self, shape, dtype, n=1):
        esz = 4 if dtype == F32 else 2
        free = int(np.prod(shape[1:]))
        nbytes = (free * esz + 31) // 32 * 32
        assert self.aoff + nbytes <= 200 * 1024, ("SBUF arena overflow", self.aoff, nbytes)
        ap = self.arena[0:shape[0], self.aoff // 4:(self.aoff + nbytes) // 4]
        if dtype != F32:
            ap = ap.bitcast(dtype)
        ap = ap[:, 0:free]
        if len(shape) == 3:
            ap = ap.rearrange("p (a b) -> p a b", a=shape[1])
        elif len(shape) == 4:
            ap = ap.rearrange("p (a b c) -> p a b c", a=shape[1], b=shape[2])
        self.aoff += nbytes
        return Tile(ap, n)

    def reset(self):
        self.aoff = self.persist

    def bank(self, pool):
        lst, i = self.pp[pool]
        self.pp[pool][1] = (i + 1) % len(lst)
        return self.ps[lst[i]]

    def key(self, sem):
        return id(sem)

    def wait(self, e, tok):
        if tok is None:
            return
        sem, val, src = tok
        if src == "pe" and e == "pe":
            return
        k = id(sem)
        if self.seen[e].get(k, 0) >= val:
            return
        self.seen[e][k] = val
        self.q[e].append(("w", sem, val))

    def _deps(self, e, reads, writes):
        for t in reads:
            self.wait(e, t.w)
        for t in writes:
            if t.w is not None and t.w[2] != e:
                self.wait(e, t.w)
            for src, r in t.r.items():
                if src != e:
                    self.wait(e, r)

    def _mark(self, e, tok, reads, writes):
        for t in reads:
            t.r[tok[2] if tok[2] != "dma" else ("dma", id(tok[0]))] = tok
        for t in writes:
            t.w = tok
            t.r = {}

    @staticmethod
    def _trks(lst):
        out = []
        for x in lst:
            if isinstance(x, Tile):
                out.extend(x.ts)
            elif isinstance(x, (list, tuple)):
                out.extend(x)
            else:
                out.append(x)
        return out

    def op(self, e, fn, reads=(), writes=()):
        reads = self._trks(reads)
        writes = self._trks(writes)
        self._deps(e, reads, writes)
        self.cnt[e] += 1
        tok = (self.esem[e], self.cnt[e], e)
        self.q[e].append(("i", fn, self.esem[e], 1))
        self._mark(e, tok, reads, writes)
        return tok

    def dma(self, qn, out, in_, reads=(), writes=()):
        reads = self._trks(reads)
        writes = self._trks(writes)
        for t in reads:
            self.wait(qn, t.w)
        for t in writes:
            self.wait(qn, t.w)
            for r in t.r.values():
                self.wait(qn, r)
        i = self.ringpos[qn]
        self.ringpos[qn] = (i + 1) % NRING
        sem = self.ring[qn][i]
        c = self.ringcnt[qn][i]
        if c:
            self.wait(qn, (sem, c, "dma"))
        self.ringcnt[qn][i] = c + 16
        tok = (sem, c + 16, "dma")
        self.q[qn].append(("i", (lambda e, o=out, s=in_: e.dma_start(out=o, in_=s)), sem, 16))
        self._mark(qn, tok, reads, writes)
        return tok

    def coll(self, kind, rg, in_ap, out_ap, reads=(), writes=()):
        reads = self._trks(reads)
        writes = self._trks(writes)
        qn = "pool"
        for t in reads:
            self.wait(qn, t.w)
        for t in writes:
            self.wait(qn, t.w)
            for r in t.r.values():
                self.wait(qn, r)
        sem = self.collsem[self.ncoll]
        self.ncoll += 1
        tok = (sem, 1, "dma")
        self.q[qn].append(("i", (lambda e: e.collective_compute(kind, ALU.bypass, replica_groups=rg, ins=[in_ap.opt()], outs=[out_ap.opt()])), sem, 1))
        self._mark(qn, tok, reads, writes)
        self.colltoks.append(tok)
        return tok

    def barrier(self):
        for e in ENGS:
            for e2 in ENGS:
                if e2 != e and self.cnt[e2]:
                    self.wait(e, (self.esem[e2], self.cnt[e2], e2))
            for qn in self.ring:
                for i in range(NRING):
                    if self.ringcnt[qn][i]:
                        self.wait(e, (self.ring[qn][i], self.ringcnt[qn][i], "dma"))
            for tok in self.colltoks:
                self.wait(e, tok)

    def replay(self, e, eng):
        for it in self.q[e]:
            if it[0] == "w":
                eng.wait_ge(it[1], it[2])
            else:
                it[1](eng).then_inc(it[2], it[3])

    def mm(self, out, lhsT, rhs, start, stop, reads, writes):
        return self.op("pe", lambda e: e.matmul(out, lhsT, rhs, start=start, stop=stop), reads, writes)

    def tr(self, out, in_, ident, reads, writes):
        return self.op("pe", lambda e: e.transpose(out, in_, ident), reads, writes)

    def act(self, out, in_, func, reads, writes, bias=None, scale=1.0, accum=None, eng="act"):
        kw = {}
        if bias is not None:
            kw["bias"] = bias
        if accum is not None:
            kw["accum_out"] = accum
        return self.op("act", lambda e: e.activation(out, in_, func, scale=scale, **kw), reads, writes)

    def tt(self, e, out, in0, in1, op, reads, writes):
        return self.op(e, lambda g: g.tensor_tensor(out, in0, in1, op), reads, writes)

    def ts(self, e, out, in0, s1, s2, op0, op1, reads, writes):
        if s2 is None:
            return self.op(e, lambda g: g.tensor_scalar(out, in0, s1, None, op0), reads, writes)
        return self.op(e, lambda g: g.tensor_scalar(out, in0, s1, s2, op0, op1), reads, writes)

    def stt(self, e, out, in0, sc, in1, op0, op1, reads, writes):
        return self.op(e, lambda g: g.scalar_tensor_tensor(out, in0, sc, in1, op0, op1), reads, writes)

    def cp(self, e, out, in_, reads, writes):
        if e == "act":
            return self.op("act", lambda g: g.copy(out, in_), reads, writes)
        return self.op(e, lambda g: g.tensor_copy(out, in_), reads, writes)


def build(T, ncores=8):
    NT = T // 512
    NCH = T // 128
    nc = bass.Bass("TRN2", target_bir_lowering=False)
    stack = contextlib.ExitStack()
    kb = KB(nc, stack)

    def din(name, shape, dt=F32):
        return nc.dram_tensor(name, list(shape), dt, kind="ExternalInput").ap()

    def dscr(name, shape, dt):
        return nc.dram_tensor(name, list(shape), dt, kind=("ExternalOutput" if (DEBUG and name in DEBUG) else "Internal")).ap()

    x_in = din("x", [T, D])
    mem_in = din("mem", [NMEM, D])
    consts_in = din("consts", [128, 10 * 128])
    cols_in = din("cols", [128, 6 * 16 + 24 * 6])
    rows_in = din("rows", [1, NROWS])
    wg_in = din("wg", [2, 17, 1024])
    tmask_in = din("tmask", [128, NCH])
    lmask_in = din("lmask", [128, 32])
    GR = min(ncores, 4)
    RG = [list(range(g0, g0 + GR)) for g0 in range(0, ncores, GR)]
    XF = 2 * 6184
    edge_d = dscr("edge_d", [128, 96], BF16)
    edges_g = dscr("edges_g", [GR * 128, 96], BF16)
    PW = 1546
    xg_dp = [dscr(f"xg_d{k}", [128, PW], F32) for k in range(8)]
    xg_gp = [dscr(f"xg_g{k}", [GR * 128, PW], F32) for k in range(8)]

    def segs(lo, hi):
        out = []
        q = lo
        while q < hi:
            k = q // PW
            a = q - k * PW
            b = min(PW, a + (hi - q))
            out.append((k, a, b, q - lo))
            q += b - a
        return out
    sin_d = dscr("sin_d", [2, 128, 6144], F32)
    wsrc = {
        "w1a": din("ffn1_w1", [D, DFF]), "w3a": din("ffn1_w3", [D, DFF]), "w2a": din("ffn1_w2", [DFF, D]),
        "win": din("w_in", [D, DIN]), "wout": din("w_out", [2 * D, D]),
        "wcq": din("w_cq", [D, D]), "wckv": din("w_ckv", [D, 2 * D]), "wco": din("w_co", [D, D]),
        "w1b": din("ffn2_w1", [D, DFF]), "w3b": din("ffn2_w3", [D, DFF]), "w2b": din("ffn2_w2", [DFF, D]),
    }
    y_out = nc.dram_tensor("y", [T, D], F32, kind="ExternalOutput").ap()

    wb = {k: dscr("wb_" + k, v.shape, BF16) for k, v in wsrc.items()}
    x1T = dscr("x1T", [16, 128, T], F32)
    z_tok = dscr("z_tok", [T, 2048], BF16)
    dt_tok = dscr("dt_tok", [T, 64], F32)
    k_tok = dscr("k_tok", [T, 1024], BF16)
    v_tok = dscr("v_tok", [T, 2048], BF16)
    go_tok = dscr("go_tok", [T, 2048], BF16)
    xbcT = dscr("xbcT", [24, 128, T], BF16)
    qT_d = dscr("qT", [8, 128, T], BF16)
    kT_d = dscr("kT", [8, 128, T], BF16)
    lowT = dscr("lowT", [2, 16, T], F32)
    xs_tok = dscr("xs_tok", [T, 2048], BF16)
    B_tok = dscr("B_tok", [T, 512], BF16)
    BT_d = dscr("BT", [4, 128, T], BF16)
    CT_d = dscr("CT", [4, 128, T], BF16)
    yf_tok = dscr("yf_tok", [T, 4096], F32)
    mix_tok = dscr("mix_tok", [T, 4096], BF16)

    cst = kb.sb([128, 10, 128], F32)
    cstb = kb.sb([128, 10, 128], BF16)
    cols = kb.sb([128, 6 * 16 + 24 * 6], F32)
    epsc = kb.sb([128, 1], F32)
    kb.dma("sp", cst.ap, consts_in.rearrange("p (a b) -> p a b", a=10), [], [cst])
    kb.dma("sp", cols.ap, cols_in, [], [cols])
    kb.cp("dve", cstb.ap, cst.ap, [cst], [cstb])
    kb.op("dve", lambda g: g.memset(epsc.ap, EPS), [], [epsc])
    IDF = cst.ap[:, 0, :]
    IDB = cstb.ap[:, 0, :]
    ONESF = cst.ap[:, 5, :]

    def ncol(i):
        return cols.ap[:, 16 * i:16 * i + 16]
    N_FFN1, N_MIX, N_XA, N_MEM, N_FFN2, N_FIN = range(6)
    convw = cols.ap[:, 96:96 + 120].rearrange("p (c k) -> p c k", c=24)
    convb = cols.ap[:, 216:240]
    kb.persist = kb.aoff

    CW = 4096
    wtmp = [kb.sb([128, CW], F32) for _ in range(2)]
    wtb = [kb.sb([128, CW], BF16) for _ in range(2)]
    it = 0
    cast_engs = ("dve", "pool", "act")
    for name, src in wsrc.items():
        K, N = src.shape
        tot = K * N // 128
        s2 = src.rearrange("(p a) n -> p (a n)", p=128)
        d2 = wb[name].rearrange("(p a) n -> p (a n)", p=128)
        for c0 in range(0, tot, CW):
            cw = min(CW, tot - c0)
            a, b = wtmp[it % 2], wtb[it % 2]
            kb.dma("sp", a.ap[:, 0:cw], s2[:, c0:c0 + cw], [], [a])
            kb.cp(cast_engs[it % 3], b.ap[:, 0:cw], a.ap[:, 0:cw], [a], [b])
            kb.dma("pool", d2[:, c0:c0 + cw], b.ap[:, 0:cw], [b], [])
            it += 1
    kb.barrier()
    kb.reset()

    def rmsnorm_fm(xT, hT, ncols, sqb, rstd, pbank):
        W = xT.ap.shape[2]
        for kc in range(16):
            s = sqb[kc % 2]
            kb.act(s.ap[:, 0:W], xT.ap[:, kc, :], AF.Square, [xT.ts[kc]], [s])
            kb.mm(pbank.ap[:, 0:W], ONESF, s.ap[:, 0:W], kc == 0, kc == 15, [s, cst], [pbank])
        kb.act(rstd.ap[:, 0:W], pbank.ap[:, 0:W], AF.Sqrt, [pbank, epsc], [rstd], bias=epsc.ap, scale=1.0 / D)
        kb.op("dve", lambda g: g.reciprocal(rstd.ap[:, 0:W], rstd.ap[:, 0:W]), [rstd], [rstd])
        for kc in range(16):
            kb.stt("dve", hT.ap[:, kc, :], xT.ap[:, kc, :], ncols[:, kc:kc + 1], rstd.ap[:, 0:W], ALU.mult, ALU.mult,
                   [xT.ts[kc], rstd, cols], [hT.ts[kc]])

    def load_panel(wname, KC, c0, w, slot):
        src = wb[wname].rearrange("(kc p) n -> p kc n", p=128)[:, :, c0:c0 + w]
        flat = slot.ap.rearrange("p a b -> p (a b)") if len(slot.ap.shape) == 3 else slot.ap
        dst = flat[:, 0:KC * w].rearrange("p (a b) -> p a b", a=KC)
        kb.dma("sp", dst, src, [], [slot])
        return dst

    class PanelStream:
        def __init__(self, slots, reqs):
            self.slots = slots
            self.reqs = reqs
            self.views = {}
            for i in range(min(len(slots), len(reqs))):
                self._issue(i)

        def _issue(self, i):
            slot = self.slots[i % len(self.slots)]
            self.views[i] = (load_panel(*self.reqs[i], slot), slot)

        def get(self, i):
            return self.views[i]

        def done(self, i):
            del self.views[i]
            j = i + len(self.slots)
            if j < len(self.reqs):
                self._issue(j)

    def ffn(xT, hT, hid, w1, w3, w2, ws8, ws11, sa, tile_tag):
        W = xT.ap.shape[2]
        reqs = []
        for fp in range(DFF // 256):
            reqs.append((w1, 16, fp * 256, 256))
            reqs.append((w3, 16, fp * 256, 256))
        st = PanelStream(ws8, reqs)
        for fp in range(DFF // 256):
            p1, s1 = st.get(2 * fp)
            p3, s3 = st.get(2 * fp + 1)
            for j in range(2):
                fc = fp * 2 + j
                pa = kb.bank("ab")
                pb = kb.bank("ab")
                for kc in range(16):
                    kb.mm(pa.ap[:, 0:W], p1[:, kc, j * 128:(j + 1) * 128], hT.ap[:, kc, :], kc == 0, kc == 15, [s1, hT.ts[kc]], [pa])
                for kc in range(16):
                    kb.mm(pb.ap[:, 0:W], p3[:, kc, j * 128:(j + 1) * 128], hT.ap[:, kc, :], kc == 0, kc == 15, [s3, hT.ts[kc]], [pb])
                s = sa[fc % 2]
                kb.act(s.ap[:, 0:W], pa.ap[:, 0:W], AF.Silu, [pa], [s])
                kb.tt("dve", hid.ap[:, fc, :], s.ap[:, 0:W], pb.ap[:, 0:W], ALU.mult, [s, pb], [hid.ts[fc]])
            st.done(2 * fp)
            st.done(2 * fp + 1)
        reqs2 = [(w2, 44, dc * 128, 128) for dc in range(16)]
        st2 = PanelStream(ws11, reqs2)
        for dc in range(16):
            p2, s2 = st2.get(dc)
            po = kb.bank("o")
            for fc in range(44):
                kb.mm(po.ap[:, 0:W], p2[:, fc, :], hid.ap[:, fc, :], fc == 0, fc == 43, [s2, hid.ts[fc]], [po])
            kb.stt("dve", xT.ap[:, dc, :], po.ap[:, 0:W], 0.5, xT.ap[:, dc, :], ALU.mult, ALU.add, [po, xT.ts[dc]], [xT.ts[dc]])
            st2.done(dc)

    kb.pp = {"ab": [[0, 1, 2, 3], 0], "o": [[4, 5], 0], "m": [[6, 7], 0]}
    xtok = [kb.sb([128, 2048], F32) for _ in range(2)]
    xT = kb.sb([128, 16, 512], F32, n=16)
    hT = kb.sb([128, 16, 512], BF16, n=16)
    hid = kb.sb([128, 44, 512], BF16, n=44)
    ws8 = [kb.sb([128, 16, 256], BF16) for _ in range(4)]
    ws11 = [kb.sb([128, 44, 128], BF16) for _ in range(2)]
    sqb = [kb.sb([128, 512], F32) for _ in range(2)]
    rstd = kb.sb([128, 512], F32)
    sa = [kb.sb([128, 512], F32) for _ in range(2)]
    ev = [kb.sb([128, 512], BF16) for _ in range(4)]
    evf = [kb.sb([128, 512], F32) for _ in range(2)]
    evi = [0]

    def nev():
        evi[0] += 1
        return ev[evi[0] % 4]

    for ti in range(NT):
        t0 = ti * 512
        for sub in range(4):
            xt = xtok[sub % 2]
            kb.dma("sp", xt.ap, x_in[t0 + sub * 128:t0 + (sub + 1) * 128, :], [], [xt])
            for k4 in range(4):
                pm = kb.bank("m")
                for u in range(4):
                    kc = k4 * 4 + u
                    kb.tr(pm.ap[:, u * 128:(u + 1) * 128], xt.ap[:, kc * 128:(kc + 1) * 128], IDF, [xt, cst], [pm])
                e = "dve" if k4 % 2 == 0 else "act"
                kb.cp(e, xT.ap[:, k4 * 4:k4 * 4 + 4, sub * 128:(sub + 1) * 128],
                      pm.ap.rearrange("p (a b) -> p a b", a=4), [pm], xT.ts[k4 * 4:k4 * 4 + 4])
        rmsnorm_fm(xT, hT, ncol(N_FFN1), sqb, rstd, kb.bank("m"))
        ffn(xT, hT, hid, "w1a", "w3a", "w2a", ws8, ws11, sa, ti)
        kb.dma("pool", x1T[:, :, t0:t0 + 512].rearrange("k p t -> p k t"), xT.ap, [xT], [])
        rmsnorm_fm(xT, hT, ncol(N_MIX), sqb, rstd, kb.bank("m"))
        tm_jobs = [(z_tok, OFF_Z, 2048), (k_tok, OFF_K, 1024), (v_tok, OFF_V, 2048), (go_tok, OFF_GO, 2048)]
        fm_jobs = [(xbcT, OFF_XBC, 24, 1.0), (qT_d, OFF_Q, 8, 0.0625), (kT_d, OFF_K, 8, 1.0)]
        reqs = []
        for dst, c0, ncs in tm_jobs:
            for pi in range(ncs // 256):
                reqs.append(("win", 16, c0 + pi * 256, 256))
        for dst, c0, nch, sc in fm_jobs:
            for pi in range(nch // 2):
                reqs.append(("win", 16, c0 + pi * 256, 256))
        reqs.append(("win", 16, OFF_DT, 64))
        reqs.append(("win", 16, OFF_LOW, 32))
        st = PanelStream(ws8, reqs)
        ri = 0
        for dst, c0, ncs in tm_jobs:
            for pi in range(ncs // 256):
                pv, ps_ = st.get(ri)
                ri += 1
                for sub in range(4):
                    po = kb.bank("ab")
                    for kc in range(16):
                        kb.mm(po.ap[:, 0:256], hT.ap[:, kc, sub * 128:(sub + 1) * 128], pv[:, kc, :], kc == 0, kc == 15,
                              [ps_, hT.ts[kc]], [po])
                    o = nev()
                    kb.cp("act" if sub % 2 else "dve", o.ap[:, 0:256], po.ap[:, 0:256], [po], [o])
                    kb.dma("pool", dst[t0 + sub * 128:t0 + (sub + 1) * 128, pi * 256:(pi + 1) * 256], o.ap[:, 0:256], [o], [])
                st.done(ri - 1)
        for dst, c0, nch, sc in fm_jobs:
            for pi in range(nch // 2):
                pv, ps_ = st.get(ri)
                ri += 1
                for j in range(2):
                    po = kb.bank("ab")
                    for kc in range(16):
                        kb.mm(po.ap, pv[:, kc, j * 128:(j + 1) * 128], hT.ap[:, kc, :], kc == 0, kc == 15, [ps_, hT.ts[kc]], [po])
                    o = nev()
                    if sc == 1.0:
                        kb.cp("act" if j % 2 else "dve", o.ap, po.ap, [po], [o])
                    else:
                        kb.ts("dve", o.ap, po.ap, sc, None, ALU.mult, None, [po], [o])
                    kb.dma("pool", dst[pi * 2 + j, :, t0:t0 + 512], o.ap, [o], [])
                st.done(ri - 1)
        pv, ps_ = st.get(ri)
        ri += 1
        for sub in range(4):
            po = kb.bank("ab")
            for kc in range(16):
                kb.mm(po.ap[:, 0:64], hT.ap[:, kc, sub * 128:(sub + 1) * 128], pv[:, kc, :], kc == 0, kc == 15, [ps_, hT.ts[kc]], [po])
            o = evf[sub % 2]
            kb.cp("dve", o.ap[:, 0:64], po.ap[:, 0:64], [po], [o])
            kb.dma("pool", dt_tok[t0 + sub * 128:t0 + (sub + 1) * 128, :], o.ap[:, 0:64], [o], [])
        st.done(ri - 1)
        pv, ps_ = st.get(ri)
        ri += 1
        for j in range(2):
            po = kb.bank("ab")
            for kc in range(16):
                kb.mm(po.ap[0:16, :], pv[:, kc, j * 16:(j + 1) * 16], hT.ap[:, kc, :], kc == 0, kc == 15, [ps_, hT.ts[kc]], [po])
            o = evf[j % 2]
            kb.cp("dve", o.ap[0:16, :], po.ap[0:16, :], [po], [o])
            kb.dma("pool", lowT[j, :, t0:t0 + 512], o.ap[0:16, :], [o], [])
    kb.barrier()
    kb.reset()
    if STOP_AFTER == "A":
        return finalize(nc, kb, stack)

    kb.pp = {"a": [[0, 1], 0], "b": [[2, 3], 0], "c": [[4, 5], 0], "d": [[6, 7], 0]}
    lmk = kb.sb([128, 32], F32)
    kb.dma("sp", lmk.ap, lmask_in, [], [lmk])
    xsrc = xbcT.rearrange("c p t -> p c t")
    agE = Tile(None)
    e0 = kb.sb([128, 24, 2], BF16)
    e1 = kb.sb([128, 24, 2], BF16)
    with nc.allow_non_contiguous_dma(reason="tiny conv halo"):
        pass
    kb.dma("sp", e0.ap, xsrc[:, :, 0:2], [], [e0])
    kb.dma("sp", e1.ap, xsrc[:, :, T - 2:T], [], [e1])
    kb.dma("sp", edge_d[:, 0:48], e0.ap.rearrange("p a b -> p (a b)"), [e0], [agE])
    kb.dma("sp", edge_d[:, 48:96], e1.ap.rearrange("p a b -> p (a b)"), [e1], [agE])
    kb.coll("AllGather", RG, edge_d[:, :], edges_g[:, :], [agE], [agE])
    eg = kb.sb([128, GR, 96], BF16)
    kb.dma("sp", eg.ap, edges_g.rearrange("(r p) f -> p r f", p=128), [agE], [eg])
    hl = kb.sb([128, 48], F32)
    hr = kb.sb([128, 48], F32)
    kb.op("dve", lambda g: g.memset(hl.ap, 0.0), [], [hl])
    kb.op("dve", lambda g: g.memset(hr.ap, 0.0), [], [hr])
    for r in range(GR):
        kb.stt("dve", hl.ap, eg.ap[:, r, 48:96], lmk.ap[:, 16 + r:17 + r], hl.ap, ALU.mult, ALU.add, [eg, lmk, hl], [hl])
        kb.stt("dve", hr.ap, eg.ap[:, r, 0:48], lmk.ap[:, 24 + r:25 + r], hr.ap, ALU.mult, ALU.add, [eg, lmk, hr], [hr])
    xh = [kb.sb([128, 24, 516], BF16) for _ in range(2)]
    cacc = [kb.sb([128, 512], F32) for _ in range(3)]
    xc = kb.sb([128, 24, 512], BF16, n=24)
    tko = [kb.sb([128, 2048], BF16) for _ in range(2)]
    tkb = [kb.sb([128, 512], BF16) for _ in range(2)]
    for ti in range(NT):
        t0 = ti * 512
        xt = xh[ti % 2]
        src = xbcT.rearrange("c p t -> p c t")
        lo = 2 if ti == 0 else 0
        hi = 514 if ti == NT - 1 else 516
        if ti == 0:
            kb.cp("dve", xt.ap[:, :, 0:2], hl.ap.rearrange("p (a b) -> p a b", a=24), [hl], [xt])
        if ti == NT - 1:
            kb.cp("dve", xt.ap[:, :, 514:516], hr.ap.rearrange("p (a b) -> p a b", a=24), [hr], [xt])
        kb.dma("sp", xt.ap[:, :, lo:hi], src[:, :, t0 - 2 + lo:t0 - 2 + hi], [], [xt])
        for c in range(24):
            acc = cacc[c % 3]
            e = "dve"
            kb.ts(e, acc.ap, xt.ap[:, c, 0:512], convw[:, c, 0:1], convb[:, c:c + 1], ALU.mult, ALU.add, [xt, cols], [acc])
            for k in range(1, 5):
                kb.stt(e, acc.ap, xt.ap[:, c, k:k + 512], convw[:, c, k:k + 1], acc.ap, ALU.mult, ALU.add, [xt, cols, acc], [acc])
            kb.act(xc.ap[:, c, :], acc.ap, AF.Silu, [acc], [xc.ts[c]])
        for sub in range(4):
            o = tko[sub % 2]
            for c4 in range(4):
                pm = kb.bank("a" if c4 % 2 else "b")
                pmb = pm.ap[:, 0:256].bitcast(BF16)
                for u in range(4):
                    c = c4 * 4 + u
                    kb.tr(pmb[:, u * 128:(u + 1) * 128], xc.ap[:, c, sub * 128:(sub + 1) * 128], IDB, [xc.ts[c], cstb], [pm])
                kb.cp("act" if c4 % 2 else "dve", o.ap[:, c4 * 512:(c4 + 1) * 512], pmb, [pm], [o])
            kb.dma("pool", xs_tok[t0 + sub * 128:t0 + (sub + 1) * 128, :], o.ap, [o], [])
            ob = tkb[sub % 2]
            pm = kb.bank("c")
            pmb = pm.ap[:, 0:256].bitcast(BF16)
            for u in range(4):
                kb.tr(pmb[:, u * 128:(u + 1) * 128], xc.ap[:, 16 + u, sub * 128:(sub + 1) * 128], IDB, [xc.ts[16 + u], cstb], [pm])
            kb.cp("dve", ob.ap, pmb, [pm], [ob])
            kb.dma("pool", B_tok[t0 + sub * 128:t0 + (sub + 1) * 128, :], ob.ap, [ob], [])
        kb.dma("pool", BT_d[:, :, t0:t0 + 512].rearrange("c p t -> p c t"), xc.ap[:, 16:20, :], [xc.ts[16:20]], [])
        kb.dma("pool", CT_d[:, :, t0:t0 + 512].rearrange("c p t -> p c t"), xc.ap[:, 20:24, :], [xc.ts[20:24]], [])
    kb.barrier()
    kb.reset()

    if STOP_AFTER == "B0":
        return finalize(nc, kb, stack)
    rows = kb.sb([128, NROWS], F32)
    kb.dma("sp", rows.ap, rows_in.partition_broadcast(128).rearrange("p a b -> p (a b)"), [], [rows])
    arow = kb.sb([128, 64], F32)
    kb.act(arow.ap, rows.ap[:, 64:128], AF.Exp, [rows], [arow])
    kb.ts("dve", arow.ap, arow.ap, -1.0, None, ALU.mult, None, [arow], [arow])
    tmk = kb.sb([128, NCH], F32)
    kb.dma("sp", tmk.ap, tmask_in, [], [tmk])
    onec = kb.sb([128, 1], F32)
    kb.op("dve", lambda g: g.memset(onec.ap, 1.0), [], [onec])
    R_DSK, R_SSDN, R_GLAN = 128, 160, 160 + 2048
    wgt = kb.sb([17, 1024], F32)
    lowa = [kb.sb([17, 64], F32) for _ in range(2)]
    for l in lowa:
        kb.op("dve", lambda g, a=l.ap: g.memset(a, 1.0), [], [l])
    lmk = kb.sb([128, 32], F32)
    kb.dma("sp", lmk.ap, lmask_in, [], [lmk])
    Dt_s = kb.sb([128, 32], F32)
    Dt_g = kb.sb([128, 8], F32)
    aoff_keep = kb.aoff
    S_ssd = kb.sb([128, 4, 512], F32, n=4)
    S_ssdb = kb.sb([128, 4, 512], BF16, n=4)
    S_gla = kb.sb([128, 8, 512], F32, n=8)
    S_glab = kb.sb([128, 8, 512], BF16, n=8)
    db = lambda shape, dt: [kb.sb(shape, dt) for _ in range(2)]
    xs_b, Bt_b, BT_b, CT_b, dtr_b = db([128, 2048], BF16), db([128, 512], BF16), db([128, 4, 128], BF16), db([128, 4, 128], BF16), db([128, 64], F32)
    kt_b, vt_b, qT_b, kT_b = db([64, 1024], BF16), db([64, 2048], BF16), db([128, 8, 64], BF16), db([128, 8, 64], BF16)
    dts = kb.sb([128, 32], F32)
    av = kb.sb([128, 32], F32)
    acs = kb.sb([128, 64], F32)
    dd = kb.sb([128, 32], F32)
    ea = kb.sb([128, 32], F32)
    dec = kb.sb([128, 32], F32)
    xes = kb.sb([128, 32], F32)
    xdt = kb.sb([128, 2048], BF16)
    xend = kb.sb([128, 2048], BF16)
    Lm = kb.sb([128, 8, 128], F32)
    Em = kb.sb([128, 8, 128], F32)
    Mh = kb.sb([128, 8, 128], BF16)
    cbm = kb.sb([128, 128], F32)
    ysb = kb.sb([128, 2048], F32, n=4)
    ytmp = kb.sb([128, 512], F32)
    sp_ = kb.sb([64, 1024], F32)
    EG = kb.sb([128, 8, 64], F32)
    Gs = kb.sb([128, 8, 64], F32)
    E1 = kb.sb([128, 8, 64], F32)
    Gm = kb.sb([128, 8, 64], F32)
    E2 = kb.sb([128, 8, 64], F32)
    qp = kb.sb([128, 8, 64], BF16)
    kp = kb.sb([128, 8, 64], BF16)
    qin = kb.sb([128, 8, 64], BF16)
    Ek = kb.sb([64, 1024], F32)
    kend = kb.sb([64, 1024], BF16)
    Am = kb.sb([64, 4, 64], BF16)
    osb = kb.sb([64, 2048], F32, n=4)
    yfb = kb.sb([128, 2048], F32)
    zb = kb.sb([128, 2048], BF16)
    szb = kb.sb([128, 2048], F32)
    mixb = db([128, 2048], BF16)
    ssq = kb.sb([128, 8], F32)
    junk = kb.sb([128, 512], F32)

    def bc(ap2, n):
        return ap2.unsqueeze(2).to_broadcast([ap2.shape[0], ap2.shape[1], n])

    def scan_pass(dirn, full, final, init=False, save=False):
        fwd = dirn == 0
        M_INC = 1 if fwd else 2
        M_STR = 3 if fwd else 4
        G_INC = 6 if fwd else 7
        G_STR = 8 if fwd else 9
        mid = 31 if fwd else 32
        last = 63 if fwd else 0
        kb.dma("sp", wgt.ap, wg_in[dirn], [], [wgt])
        if init:
            kb.dma("sp", S_ssd.ap.rearrange("p a b -> p (a b)"), sin_d[dirn, :, 0:2048], [], [S_ssd])
            kb.dma("sp", S_gla.ap.rearrange("p a b -> p (a b)"), sin_d[dirn, :, 2048:6144], [], [S_gla])
            for g in range(4):
                kb.cp("pool", S_ssdb.ap[:, g, :], S_ssd.ap[:, g, :], [S_ssd.ts[g]], [S_ssdb.ts[g]])
            for g in range(8):
                kb.cp("pool", S_glab.ap[:, g, :], S_gla.ap[:, g, :], [S_gla.ts[g]], [S_glab.ts[g]])
        else:
            for g in range(4):
                kb.op("pool", lambda e, a=S_ssd.ap[:, g, :]: e.memset(a, 0.0), [], [S_ssd.ts[g]])
                kb.op("pool", lambda e, a=S_ssdb.ap[:, g, :]: e.memset(a, 0.0), [], [S_ssdb.ts[g]])
            for g in range(8):
                kb.op("pool", lambda e, a=S_gla.ap[:, g, :]: e.memset(a, 0.0), [], [S_gla.ts[g]])
                kb.op("pool", lambda e, a=S_glab.ap[:, g, :]: e.memset(a, 0.0), [], [S_glab.ts[g]])
        if save:
            kb.op("dve", lambda e: e.memset(Dt_s.ap, 1.0), [], [Dt_s])
            kb.op("dve", lambda e: e.memset(Dt_g.ap, 1.0), [], [Dt_g])
        order = list(range(NCH)) if fwd else list(range(NCH - 1, -1, -1))
        for n_, ci in enumerate(order):
            t0 = ci * 128
            pb_ = n_ % 2
            xs, Bt, BT, CT, dtr = xs_b[pb_], Bt_b[pb_], BT_b[pb_], CT_b[pb_], dtr_b[pb_]
            kb.dma("sp", xs.ap, xs_tok[t0:t0 + 128, :], [], [xs])
            kb.dma("sp", Bt.ap, B_tok[t0:t0 + 128, :], [], [Bt])
            kb.dma("sp", dtr.ap, dt_tok[t0:t0 + 128, :], [], [dtr])
            if full:
                kb.dma("sp", BT.ap, BT_d[:, :, t0:t0 + 128].rearrange("c p t -> p c t"), [], [BT])
                kb.dma("sp", CT.ap, CT_d[:, :, t0:t0 + 128].rearrange("c p t -> p c t"), [], [CT])
            c0 = 32 * dirn
            kb.tt("dve", dts.ap, dtr.ap[:, c0:c0 + 32], rows.ap[:, c0:c0 + 32], ALU.add, [dtr, rows], [dts])
            kb.act(dts.ap, dts.ap, AF.Exp, [dts], [dts])
            kb.act(dts.ap, dts.ap, AF.Ln, [dts, onec], [dts], bias=onec.ap)
            kb.tt("dve", av.ap, dts.ap, arow.ap[:, c0:c0 + 32], ALU.mult, [dts, arow], [av])
            pm = kb.bank("a")
            kb.mm(pm.ap[:, 0:32], cst.ap[:, M_INC, :], av.ap, True, True, [cst, av], [pm])
            kb.mm(pm.ap[:, 32:64], ONESF, av.ap, True, True, [cst, av], [pm])
            kb.cp("dve", acs.ap, pm.ap[:, 0:64], [pm], [acs])
            kb.tt("dve", dd.ap, acs.ap[:, 32:64], acs.ap[:, 0:32], ALU.subtract, [acs], [dd])
            kb.act(xes.ap, dd.ap, AF.Exp, [dd], [xes])
            kb.act(ea.ap, acs.ap[:, 0:32], AF.Exp, [acs], [ea])
            kb.act(dec.ap, acs.ap[:, 32:64], AF.Exp, [acs], [dec])
            if save:
                kb.tt("dve", Dt_s.ap, Dt_s.ap, dec.ap, ALU.mult, [Dt_s, dec], [Dt_s])
            kb.ts("dve", dts.ap, dts.ap, tmk.ap[:, ci:ci + 1], None, ALU.mult, None, [dts, tmk], [dts])
            kb.tt("dve", xes.ap, xes.ap, dts.ap, ALU.mult, [xes, dts], [xes])
            x3 = xs.ap.rearrange("p (h d) -> p h d", h=32)
            kb.tt("dve", xend.ap.rearrange("p (h d) -> p h d", h=32), x3, bc(xes.ap, 64), ALU.mult, [xs, xes], [xend])
            if full:
                kb.tt("pool", xdt.ap.rearrange("p (h d) -> p h d", h=32), x3, bc(dts.ap, 64), ALU.mult, [xs, dts], [xdt])
            for g in range(4):
                gs = slice(g * 512, (g + 1) * 512)
                if full and BIS != 4:
                    pyo = kb.bank("b")
                    kb.mm(pyo.ap, CT.ap[:, g, :], S_ssdb.ap[:, g, :], True, True, [CT, S_ssdb.ts[g]], [pyo])
                    pcb = kb.bank("a")
                    kb.mm(pcb.ap[:, 0:128], BT.ap[:, g, :], CT.ap[:, g, :], True, True, [BT, CT], [pcb])
                    kb.tt("dve", cbm.ap, pcb.ap[:, 0:128], cst.ap[:, M_INC, :], ALU.mult, [pcb, cst], [cbm])
                    kb.tt("pool", Lm.ap, cst.ap[:, M_STR:M_STR + 1, :].to_broadcast([128, 8, 128]),
                          bc(av.ap[:, g * 8:(g + 1) * 8], 128), ALU.mult, [cst, av], [Lm])
                    pys = kb.bank("d")
                    for hh in range(2):
                        psg = kb.bank("c")
                        for u in range(4):
                            kb.mm(psg.ap[:, u * 128:(u + 1) * 128], Lm.ap[:, hh * 4 + u, :], cst.ap[:, M_INC, :], True, True, [Lm, cst], [psg])
                        kb.act(Em.ap[:, hh * 4:hh * 4 + 4, :], psg.ap.rearrange("p (a b) -> p a b", a=4), AF.Exp, [psg], [Em])
                    kb.tt("dve", Mh.ap, Em.ap, cbm.ap.unsqueeze(1).to_broadcast([128, 8, 128]), ALU.mult, [Em, cbm], [Mh])
                    for h in range(8):
                        hd = (g * 8 + h) * 64
                        kb.mm(pys.ap[:, h * 64:(h + 1) * 64], Mh.ap[:, h, :], xdt.ap[:, hd:hd + 64], True, True, [Mh, xdt], [pys])
                    kb.tt("dve", ytmp.ap.rearrange("p (h d) -> p h d", h=8), pyo.ap.rearrange("p (h d) -> p h d", h=8),
                          bc(ea.ap[:, g * 8:(g + 1) * 8], 64), ALU.mult, [pyo, ea], [ytmp])
                    kb.tt("dve", ysb.ap[:, gs], ytmp.ap, pys.ap, ALU.add, [ytmp, pys], [ysb.ts[g]])
                pu = kb.bank("b")
                kb.mm(pu.ap, Bt.ap[:, g * 128:(g + 1) * 128], xend.ap[:, gs], True, True, [Bt, xend], [pu])
                S3 = S_ssd.ap[:, g, :].rearrange("p (h d) -> p h d", h=8)
                kb.tt("pool", S3, S3, bc(dec.ap[:, g * 8:(g + 1) * 8], 64), ALU.mult, [S_ssd.ts[g], dec], [S_ssd.ts[g]])
                kb.tt("dve", S_ssd.ap[:, g, :], S_ssd.ap[:, g, :], pu.ap, ALU.add, [S_ssd.ts[g], pu], [S_ssd.ts[g]])
                kb.cp("act", S_ssdb.ap[:, g, :], S_ssd.ap[:, g, :], [S_ssd.ts[g]], [S_ssdb.ts[g]])
            if full and not final:
                kb.dma("pool", yf_tok[t0:t0 + 128, 0:2048], ysb.ap, [ysb], [])
            if final:
                kb.dma("sp", yfb.ap, yf_tok[t0:t0 + 128, 0:2048], [], [yfb])
                kb.dma("sp", zb.ap, z_tok[t0:t0 + 128, :], [], [zb])
                kb.tt("dve", ysb.ap, ysb.ap, yfb.ap, ALU.add, [ysb, yfb], [ysb])
                kb.tt("pool", yfb.ap.rearrange("p (h d) -> p h d", h=32), x3, bc(rows.ap[:, R_DSK:R_DSK + 32], 64), ALU.mult,
                      [xs, rows], [yfb])
                kb.tt("dve", ysb.ap, ysb.ap, yfb.ap, ALU.add, [ysb, yfb], [ysb])
                kb.act(szb.ap, zb.ap, AF.Silu, [zb], [szb])
                kb.tt("dve", ysb.ap, ysb.ap, szb.ap, ALU.mult, [ysb, szb], [ysb])
                for g in range(4):
                    kb.act(junk.ap, ysb.ap[:, g * 512:(g + 1) * 512], AF.Square, [ysb], [junk, ssq], accum=ssq.ap[:, g:g + 1])
                kb.act(ssq.ap[:, 0:4], ssq.ap[:, 0:4], AF.Sqrt, [ssq, epsc], [ssq], bias=epsc.ap, scale=1.0 / 512)
                kb.op("dve", lambda e: e.reciprocal(ssq.ap[:, 0:4], ssq.ap[:, 0:4]), [ssq], [ssq])
                mo = mixb[n_ % 2]
                for g in range(4):
                    kb.stt("dve", mo.ap[:, g * 512:(g + 1) * 512], ysb.ap[:, g * 512:(g + 1) * 512], ssq.ap[:, g:g + 1],
                           rows.ap[:, R_SSDN + g * 512:R_SSDN + (g + 1) * 512], ALU.mult, ALU.mult, [ysb, ssq, rows], [mo])
                kb.dma("pool", mix_tok[t0:t0 + 128, 0:2048], mo.ap, [mo], [])
            for sn, sub in enumerate((0, 1) if fwd else (1, 0)):
                s0 = t0 + sub * 64
                qb_ = (n_ * 2 + sn) % 2
                kt, vt, qT, kT, la = kt_b[qb_], vt_b[qb_], qT_b[qb_], kT_b[qb_], lowa[qb_]
                kb.dma("sp", kt.ap, k_tok[s0:s0 + 64, :], [], [kt])
                kb.dma("sp", vt.ap, v_tok[s0:s0 + 64, :], [], [vt])
                kb.dma("sp", la.ap[0:16, :], lowT[dirn, :, s0:s0 + 64], [], [la])
                if full:
                    kb.dma("sp", qT.ap, qT_d[:, :, s0:s0 + 64].rearrange("c p t -> p c t"), [], [qT])
                    kb.dma("sp", kT.ap, kT_d[:, :, s0:s0 + 64].rearrange("c p t -> p c t"), [], [kT])
                for hf in range(2):
                    pg = kb.bank("c")
                    kb.mm(pg.ap[0:64, :], la.ap[0:17, :], wgt.ap[0:17, hf * 512:(hf + 1) * 512], True, True, [la, wgt], [pg])
                    kb.act(sp_.ap[:, hf * 512:(hf + 1) * 512], pg.ap[0:64, :], AF.Exp, [pg], [sp_], scale=-1.0)
                kb.act(sp_.ap, sp_.ap, AF.Ln, [sp_, onec], [sp_], bias=onec.ap[0:64, :])
                pG = kb.bank("a")
                for kc in range(8):
                    kb.mm(pG.ap[:, kc * 64:(kc + 1) * 64], sp_.ap[0:64, kc * 128:(kc + 1) * 128], cst.ap[0:64, G_INC, 0:64], True, True,
                          [sp_, cst], [pG])
                pG3 = pG.ap.rearrange("p (a b) -> p a b", a=8)
                kb.cp("dve", Gs.ap, pG3, [pG], [Gs])
                kb.act(EG.ap, Gs.ap, AF.Exp, [Gs], [EG])
                if save:
                    kb.tt("dve", Dt_g.ap, Dt_g.ap, EG.ap[:, :, last], ALU.mult, [Dt_g, EG], [Dt_g])
                for hf in range(2):
                    pD = kb.bank("d")
                    kb.mm(pD.ap[0:64, :], cst.ap[0:64, G_STR, 0:64], sp_.ap[0:64, hf * 512:(hf + 1) * 512], True, True, [sp_, cst], [pD])
                    kb.act(Ek.ap[:, hf * 512:(hf + 1) * 512], pD.ap[0:64, :], AF.Exp, [pD], [Ek])
                kb.tt("pool", kend.ap, kt.ap, Ek.ap, ALU.mult, [kt, Ek], [kend])
                if full and BIS != 3:
                    kb.tt("dve", Gm.ap, Gs.ap, Gs.ap[:, :, mid:mid + 1].to_broadcast([128, 8, 64]), ALU.subtract, [Gs], [Gm])
                    kb.act(E1.ap, Gm.ap, AF.Exp, [Gm], [E1])
                    kb.act(E2.ap, Gm.ap, AF.Exp, [Gm], [E2], scale=-1.0)
                    kb.tt("dve", qp.ap, qT.ap, E1.ap, ALU.mult, [qT, E1], [qp])
                    kb.tt("pool", kp.ap, kT.ap, E2.ap, ALU.mult, [kT, E2], [kp])
                    kb.tt("dve", qin.ap, qT.ap, EG.ap, ALU.mult, [qT, EG], [qin])
                    pA = kb.bank("a")
                    for h in range(4):
                        for kk in range(2):
                            kb.mm(pA.ap[0:64, h * 64:(h + 1) * 64], kp.ap[:, 2 * h + kk, :], qp.ap[:, 2 * h + kk, :], kk == 0, kk == 1,
                                  [kp, qp], [pA])
                    kb.tt("dve", Am.ap, pA.ap[0:64, 0:256].rearrange("p (a b) -> p a b", a=4),
                          cst.ap[0:64, M_INC:M_INC + 1, 0:64].to_broadcast([64, 4, 64]), ALU.mult, [pA, cst], [Am])
                    for h in range(4 if BIS != 5 else 0):
                        po = kb.bank("b")
                        kb.mm(po.ap[0:64, :], Am.ap[:, h, :], vt.ap[:, h * 512:(h + 1) * 512], True, False, [Am, vt], [po])
                        for kk in range(2):
                            kb.mm(po.ap[0:64, :], qin.ap[:, 2 * h + kk, :], S_glab.ap[:, 2 * h + kk, :], False, kk == 1,
                                  [qin, S_glab.ts[2 * h + kk]], [po])
                        kb.cp("act", osb.ap[:, h * 512:(h + 1) * 512], po.ap[0:64, :], [po], [osb.ts[h]])
                for h in range(4):
                    for kk in range(2):
                        j = 2 * h + kk
                        pu = kb.bank("d" if kk else "c")
                        kb.mm(pu.ap, kend.ap[:, j * 128:(j + 1) * 128], vt.ap[:, h * 512:(h + 1) * 512], True, True, [kend, vt], [pu])
                        kb.stt("dve", S_gla.ap[:, j, :], S_gla.ap[:, j, :], EG.ap[:, j, last:last + 1], pu.ap, ALU.mult, ALU.add,
                               [S_gla.ts[j], EG, pu], [S_gla.ts[j]])
                        kb.cp("pool", S_glab.ap[:, j, :], S_gla.ap[:, j, :], [S_gla.ts[j]], [S_glab.ts[j]])
                if full and not final:
                    kb.dma("pool", yf_tok[s0:s0 + 64, 2048:4096], osb.ap, [osb], [])
                if final:
                    kb.dma("sp", yfb.ap[0:64, :], yf_tok[s0:s0 + 64, 2048:4096], [], [yfb])
                    kb.dma("sp", zb.ap[0:64, :], go_tok[s0:s0 + 64, :], [], [zb])
                    kb.tt("dve", osb.ap, osb.ap, yfb.ap[0:64, :], ALU.add, [osb, yfb], [osb])
                    for h in range(4):
                        kb.act(junk.ap[0:64, :], osb.ap[:, h * 512:(h + 1) * 512], AF.Square, [osb], [junk, ssq],
                               accum=ssq.ap[0:64, 4 + h:5 + h])
                    kb.act(ssq.ap[0:64, 4:8], ssq.ap[0:64, 4:8], AF.Sqrt, [ssq, epsc], [ssq], bias=epsc.ap[0:64, :], scale=1.0 / 512)
                    kb.op("dve", lambda e: e.reciprocal(ssq.ap[0:64, 4:8], ssq.ap[0:64, 4:8]), [ssq], [ssq])
                    for h in range(4):
                        kb.stt("dve", osb.ap[:, h * 512:(h + 1) * 512], osb.ap[:, h * 512:(h + 1) * 512], ssq.ap[0:64, 4 + h:5 + h],
                               rows.ap[0:64, R_GLAN:R_GLAN + 512], ALU.mult, ALU.mult, [osb, ssq, rows], [osb])
                    kb.act(szb.ap[0:64, :], zb.ap[0:64, :], AF.Silu, [zb], [szb])
                    mo = mixb[(n_ * 2 + sn) % 2]
                    kb.tt("dve", mo.ap[0:64, :], osb.ap, szb.ap[0:64, :], ALU.mult, [osb, szb], [mo])
                    kb.dma("pool", mix_tok[s0:s0 + 64, 2048:4096], mo.ap[0:64, :], [mo], [])

    agS = Tile(None)

    def save_states(dirn):
        o = dirn * 6184
        for tl, flat, lo, n in ((S_ssd, S_ssd.ap.rearrange("p a b -> p (a b)"), 0, 2048),
                                (S_gla, S_gla.ap.rearrange("p a b -> p (a b)"), 2048, 4096),
                                (Dt_s, Dt_s.ap, 6144, 32), (Dt_g, Dt_g.ap, 6176, 8)):
            for k, a, b, off in segs(o + lo, o + lo + n):
                kb.dma("pool", xg_dp[k][:, a:b], flat[:, off:off + (b - a)], [tl], [agS])

    scan_pass(0, False, False, save=True)
    save_states(0)
    scan_pass(1, False, False, save=True)
    save_states(1)
    kb.barrier()
    for k in range(8):
        kb.coll("AllGather", RG, xg_dp[k][:, :], xg_gp[k][:, :], [agS], [agS])
    aoff_B = kb.aoff
    kb.aoff = aoff_keep
    acc = kb.sb([128, 6144], F32)
    Sr = [kb.sb([128, 6184], F32) for _ in range(2)]
    Deff = [kb.sb([128, 40], F32) for _ in range(2)]
    tmpc = kb.sb([128, 512], F32)
    for dirn in range(2):
        kb.op("dve", lambda e: e.memset(acc.ap, 0.0), [], [acc])
        rr = list(range(GR)) if dirn == 0 else list(range(GR - 1, -1, -1))
        for n_, r in enumerate(rr):
            sr, de = Sr[n_ % 2], Deff[n_ % 2]
            o = dirn * 6184
            for kk in range(4):
                kb.dma("sp", sr.ap[:, kk * PW:(kk + 1) * PW], xg_gp[dirn * 4 + kk][r * 128:(r + 1) * 128, :], [agS], [sr])
            b = lmk.ap[:, dirn * 8 + r:dirn * 8 + r + 1]
            kb.ts("dve", de.ap, sr.ap[:, 6144:6184], -1.0, b, ALU.add, ALU.mult, [sr, lmk], [de])
            kb.ts("dve", de.ap, de.ap, 1.0, None, ALU.add, None, [de], [de])
            a3 = acc.ap[:, 0:2048].rearrange("p (h d) -> p h d", h=32)
            kb.tt("dve", a3, a3, bc(de.ap[:, 0:32], 64), ALU.mult, [acc, de], [acc])
            kb.stt("dve", acc.ap[:, 0:2048], sr.ap[:, 0:2048], b, acc.ap[:, 0:2048], ALU.mult, ALU.add, [sr, lmk, acc], [acc])
            for j in range(8):
                sl = slice(2048 + j * 512, 2048 + (j + 1) * 512)
                kb.ts("pool", tmpc.ap, sr.ap[:, sl], b, None, ALU.mult, None, [sr, lmk], [tmpc])
                kb.stt("dve", acc.ap[:, sl], acc.ap[:, sl], de.ap[:, 32 + j:33 + j], tmpc.ap, ALU.mult, ALU.add, [acc, de, tmpc], [acc])
        kb.dma("pool", sin_d[dirn], acc.ap, [acc], [])
    kb.barrier()
    kb.aoff = aoff_B
    scan_pass(0, True, False, init=True)
    kb.barrier()
    scan_pass(1, True, True, init=True)
    kb.barrier()
    kb.reset()
    if STOP_AFTER == "B":
        return finalize(nc, kb, stack)
    WC = 256
    NTC = T // WC
    kb.pp = {"ab": [[0, 1, 2, 3], 0], "o": [[4, 5], 0], "m": [[6, 7], 0]}
    xT = kb.sb([128, 16, WC], F32, n=16)
    hT = kb.sb([128, 16, WC], BF16, n=16)
    oT = kb.sb([128, 16, WC], BF16, n=16)
    hid = kb.sb([128, 44, WC], BF16, n=44)
    mixT = Tile(hid.ap.rearrange("p a b -> p (a b)")[:, 0:32 * WC].rearrange("p (a b) -> p a b", a=32), 32)
    yT = Tile(hid.ap.rearrange("p a b -> p (a b)")[:, 0:32 * WC].bitcast(F32).rearrange("p (a b) -> p a b", a=16), 16)
    ws8 = [kb.sb([128, 16, 256], BF16) for _ in range(4)]
    ws11 = [kb.sb([128, 44, 128], BF16) for _ in range(2)]
    sqb = [kb.sb([128, 512], F32) for _ in range(2)]
    rstd = kb.sb([128, 512], F32)
    sa = [kb.sb([128, 512], F32) for _ in range(2)]
    kTm = kb.sb([128, 16, 256], BF16)
    vm = kb.sb([128, 2, 2048], BF16)
    mtok = [kb.sb([128, 2048], F32) for _ in range(2)]
    qTh = kb.sb([128, 4, WC], BF16)
    PT = kb.sb([128, 2, WC], BF16)
    Pf = kb.sb([128, 256], F32)
    Pb = kb.sb([128, 256], BF16)
    st4 = kb.sb([128, 4], F32)

    mT = Tile(hid.ap.rearrange("p a b -> p (a b)")[:, 0:16 * 256].rearrange("p (a b) -> p a b", a=16), 1)
    for mt in range(2):
        mk = mtok[mt]
        kb.dma("sp", mk.ap, mem_in[mt * 128:(mt + 1) * 128, :], [], [mk])
        for q4 in range(4):
            kb.act(sa[q4 % 2].ap, mk.ap[:, q4 * 512:(q4 + 1) * 512], AF.Square, [mk], [sa[q4 % 2], st4], accum=st4.ap[:, q4:q4 + 1])
        kb.op("dve", lambda e, o=rstd.ap[:, mt:mt + 1], i=st4.ap: e.reduce_sum(o, i, mybir.AxisListType.X), [st4], [rstd])
        kb.act(rstd.ap[:, mt:mt + 1], rstd.ap[:, mt:mt + 1], AF.Sqrt, [rstd, epsc], [rstd], bias=epsc.ap, scale=1.0 / D)
        kb.op("dve", lambda e, a=rstd.ap[:, mt:mt + 1]: e.reciprocal(a, a), [rstd], [rstd])
        kb.ts("dve", mk.ap, mk.ap, rstd.ap[:, mt:mt + 1], None, ALU.mult, None, [mk, rstd], [mk])
        for k4 in range(4):
            pm = kb.bank("m")
            for u in range(4):
                kc = k4 * 4 + u
                kb.tr(pm.ap[:, u * 128:(u + 1) * 128], mk.ap[:, kc * 128:(kc + 1) * 128], IDF, [mk, cst], [pm])
            for u in range(4):
                kc = k4 * 4 + u
                kb.ts("dve", mT.ap[:, kc, mt * 128:(mt + 1) * 128], pm.ap[:, u * 128:(u + 1) * 128], ncol(N_MEM)[:, kc:kc + 1], None,
                      ALU.mult, None, [pm, cols], [mT])
    reqs = [("wckv", 16, pi * 256, 256) for pi in range(16)]
    st = PanelStream(ws8, reqs)
    for pi in range(8):
        pv, ps_ = st.get(pi)
        for j in range(2):
            po = kb.bank("ab")
            for kc in range(16):
                kb.mm(po.ap[:, 0:256], pv[:, kc, j * 128:(j + 1) * 128], mT.ap[:, kc, :], kc == 0, kc == 15, [ps_, mT], [po])
            kb.cp("act" if j else "dve", kTm.ap[:, pi * 2 + j, :], po.ap[:, 0:256], [po], [kTm])
        st.done(pi)
    for pi in range(8, 16):
        pv, ps_ = st.get(pi)
        for mt in range(2):
            po = kb.bank("ab")
            for kc in range(16):
                kb.mm(po.ap[:, 0:256], mT.ap[:, kc, mt * 128:(mt + 1) * 128], pv[:, kc, :], kc == 0, kc == 15, [ps_, mT], [po])
            kb.cp("act" if mt else "dve", vm.ap[:, mt, (pi - 8) * 256:(pi - 7) * 256], po.ap[:, 0:256], [po], [vm])
        st.done(pi)

    SCL = 512.0 ** -0.5
    for ti in range(NTC):
        t0 = ti * WC
        kb.dma("sp", xT.ap, x1T[:, :, t0:t0 + WC].rearrange("k p t -> p k t"), [], [xT])
        for sub in range(WC // 128):
            mk = mtok[sub % 2]
            mkb = mk.ap.bitcast(BF16)
            kb.dma("sp", mkb, mix_tok[t0 + sub * 128:t0 + (sub + 1) * 128, :], [], [mk])
            for c4 in range(8):
                pm = kb.bank("m")
                pmb = pm.ap[:, 0:256].bitcast(BF16)
                for u in range(4):
                    c = c4 * 4 + u
                    kb.tr(pmb[:, u * 128:(u + 1) * 128], mkb[:, c * 128:(c + 1) * 128], IDB, [mk, cstb], [pm])
                kb.cp("act" if c4 % 2 else "dve", mixT.ap[:, c4 * 4:c4 * 4 + 4, sub * 128:(sub + 1) * 128],
                      pmb.rearrange("p (a b) -> p a b", a=4), [pm], mixT.ts[c4 * 4:c4 * 4 + 4])
        st = PanelStream(ws8, [("wout", 32, dc * 128, 128) for dc in range(16)])
        for dc in range(16):
            pv, ps_ = st.get(dc)
            po = kb.bank("o")
            for cc in range(32):
                kb.mm(po.ap[:, 0:WC], pv[:, cc, :], mixT.ap[:, cc, :], cc == 0, cc == 31, [ps_, mixT.ts[cc]], [po])
            kb.tt("dve", xT.ap[:, dc, :], xT.ap[:, dc, :], po.ap[:, 0:WC], ALU.add, [po, xT.ts[dc]], [xT.ts[dc]])
            st.done(dc)
        rmsnorm_fm(xT, hT, ncol(N_XA), sqb, rstd, kb.bank("m"))
        st = PanelStream(ws8, [("wcq", 16, pi * 256, 256) for pi in range(8)])
        for h in range(4):
            for pj in range(2):
                pv, ps_ = st.get(h * 2 + pj)
                for j in range(2):
                    po = kb.bank("ab")
                    for kc in range(16):
                        kb.mm(po.ap[:, 0:WC], pv[:, kc, j * 128:(j + 1) * 128], hT.ap[:, kc, :], kc == 0, kc == 15, [ps_, hT.ts[kc]], [po])
                    kb.cp("act" if j else "dve", qTh.ap[:, pj * 2 + j, :], po.ap[:, 0:WC], [po], [qTh])
                st.done(h * 2 + pj)
            for sub in range(WC // 128):
                pss = kb.bank("o")
                for j in range(4):
                    kb.mm(pss.ap[:, 0:256], qTh.ap[:, j, sub * 128:(sub + 1) * 128], kTm.ap[:, h * 4 + j, :], j == 0, j == 3, [qTh, kTm], [pss])
                kb.op("dve", lambda e, o=st4.ap[:, 0:1], i=pss.ap[:, 0:256]: e.reduce_max(o, i, mybir.AxisListType.X), [pss], [st4])
                kb.ts("dve", st4.ap[:, 1:2], st4.ap[:, 0:1], -SCL, None, ALU.mult, None, [st4], [st4])
                kb.act(Pf.ap, pss.ap[:, 0:256], AF.Exp, [pss, st4], [Pf, st4], bias=st4.ap[:, 1:2], scale=SCL, accum=st4.ap[:, 2:3])
                kb.op("dve", lambda e, o=st4.ap[:, 3:4], i=st4.ap[:, 2:3]: e.reciprocal(o, i), [st4], [st4])
                kb.ts("dve", Pb.ap, Pf.ap, st4.ap[:, 3:4], None, ALU.mult, None, [Pf, st4], [Pb])
                pm = kb.bank("m")
                pmb = pm.ap[:, 0:128].bitcast(BF16)
                for mc in range(2):
                    kb.tr(pmb[:, mc * 128:(mc + 1) * 128], Pb.ap[:, mc * 128:(mc + 1) * 128], IDB, [Pb, cstb], [pm])
                kb.cp("act", PT.ap[:, :, sub * 128:(sub + 1) * 128], pmb.rearrange("p (a b) -> p a b", a=2), [pm], [PT])
            for j in range(4):
                po = kb.bank("ab")
                for mc in range(2):
                    kb.mm(po.ap[:, 0:WC], vm.ap[:, mc, (h * 4 + j) * 128:(h * 4 + j + 1) * 128], PT.ap[:, mc, :], mc == 0, mc == 1, [vm, PT], [po])
                kb.cp("act" if j % 2 else "dve", oT.ap[:, h * 4 + j, :], po.ap[:, 0:WC], [po], [oT.ts[h * 4 + j]])
        st = PanelStream(ws8, [("wco", 16, pi * 256, 256) for pi in range(8)])
        for pi in range(8):
            pv, ps_ = st.get(pi)
            for j in range(2):
                dc = pi * 2 + j
                po = kb.bank("o")
                for kc in range(16):
                    kb.mm(po.ap[:, 0:WC], pv[:, kc, j * 128:(j + 1) * 128], oT.ap[:, kc, :], kc == 0, kc == 15, [ps_, oT.ts[kc]], [po])
                kb.tt("dve", xT.ap[:, dc, :], xT.ap[:, dc, :], po.ap[:, 0:WC], ALU.add, [po, xT.ts[dc]], [xT.ts[dc]])
            st.done(pi)
        rmsnorm_fm(xT, hT, ncol(N_FFN2), sqb, rstd, kb.bank("m"))
        ffn(xT, hT, hid, "w1b", "w3b", "w2b", ws8, ws11, sa, ti)
        rmsnorm_fm(xT, yT, ncol(N_FIN), sqb, rstd, kb.bank("m"))
        for sub in range(WC // 128):
            yo = mtok[sub % 2]
            for k4 in range(4):
                pm = kb.bank("m")
                for u in range(4):
                    kc = k4 * 4 + u
                    kb.tr(pm.ap[:, u * 128:(u + 1) * 128], yT.ap[:, kc, sub * 128:(sub + 1) * 128], IDF, [yT.ts[kc], cst], [pm])
                kb.cp("act" if k4 % 2 else "dve", yo.ap[:, k4 * 512:(k4 + 1) * 512], pm.ap, [pm], [yo])
            kb.dma("pool", y_out[t0 + sub * 128:t0 + (sub + 1) * 128, :], yo.ap, [yo], [])
    return finalize(nc, kb, stack)


STOP_AFTER = None
BIS = 0


def finalize(nc, kb, stack):
    kb.barrier()
    with nc.Block() as block:
        @block.tensor
        def _(eng):
            kb.replay("pe", eng)

        @block.scalar
        def _(eng):
            kb.replay("act", eng)

        @block.vector
        def _(eng):
            kb.replay("dve", eng)

        @block.gpsimd
        def _(eng):
            kb.replay("pool", eng)

        @block.sync
        def _(eng):
            kb.replay("sp", eng)
    stack.close()
    return nc


def make_consts():
    p = np.arange(128)[:, None]
    f = np.arange(128)[None, :]
    mats = [p == f, p <= f, p >= f, p > f, p < f, np.ones((128, 128), bool)]
    c = [m.astype(np.float32) for m in mats]
    c += [m.astype(np.float32) * (-1.0 / 16.0) for m in mats[1:5]]
    return np.ascontiguousarray(np.concatenate(c, axis=1))


def colify(v):
    return np.asarray(v, np.float32).reshape(-1, 128).T


def host_inputs(inp, x, mem):
    g = lambda k: np.asarray(inp[k], np.float32)
    cols = np.concatenate(
        [colify(g(k).reshape(-1)) for k in ("ffn1_norm", "mix_norm", "xattn_norm", "mem_norm", "ffn2_norm", "final_norm")]
        + [g("conv_w")[0].T.reshape(24, 128, 5).transpose(1, 0, 2).reshape(128, 120),
           colify(g("conv_b").reshape(-1))], axis=1)
    rows = np.concatenate([
        g("dt_bias_fwd").reshape(-1), g("dt_bias_bwd").reshape(-1),
        g("a_log_fwd").reshape(-1), g("a_log_bwd").reshape(-1),
        g("d_skip").reshape(-1), g("ssd_norm").reshape(-1), g("gla_norm").reshape(-1)])[None, :]
    rows = np.ascontiguousarray(rows, np.float32)
    wg = np.stack([np.concatenate([g("gla_gate_w_fwd")[0], g("gla_gate_b_fwd")], 0),
                   np.concatenate([g("gla_gate_w_bwd")[0], g("gla_gate_b_bwd")], 0)], 0)
    m = {"x": np.ascontiguousarray(x, np.float32), "mem": np.ascontiguousarray(mem, np.float32),
         "consts": make_consts(), "cols": np.ascontiguousarray(cols, np.float32), "rows": rows,
         "wg": np.ascontiguousarray(wg, np.float32)}
    for k in ("ffn1_w1", "ffn1_w3", "ffn1_w2", "w_in", "w_out", "w_cq", "w_ckv", "w_co", "ffn2_w1", "ffn2_w3", "ffn2_w2"):
        m[k] = np.ascontiguousarray(g(k)[0])
    return m


T_CORE = 4096
_CACHE = {}


def tmask_for(T, valid):
    nch = T // 128
    m = (np.arange(nch) * 128 < valid).astype(np.float32)
    return np.ascontiguousarray(np.broadcast_to(m[None, :], (128, nch)))


def lmask_for(c, chain):
    m = np.zeros((128, 32), np.float32)
    i = chain.index(c)
    for r in chain[:i]:
        m[:, r] = 1.0
    for r in chain[i + 1:]:
        m[:, 8 + r] = 1.0
    if i > 0:
        m[:, 16 + chain[i - 1]] = 1.0
    if i + 1 < len(chain):
        m[:, 24 + chain[i + 1]] = 1.0
    return m


def kernel(**inputs):
    inp = {k: np.asarray(v) for k, v in inputs.items()}
    xp, xs_, mp, ms = inp["x_prompt"], inp["x_sample"], inp["mem_prompt"], inp["mem_sample"]
    T = T_CORE
    if T not in _CACHE:
        _CACHE[T] = build(T)
    nc = _CACHE[T]
    in_maps = []
    base = None
    for c in range(8):
        if c < 4:
            x, mem, chain, cl = xp[c], mp[c], [c], c
        else:
            x, mem, chain, cl = xs_[0, (c - 4) * T:(c - 3) * T], ms[0], [0, 1, 2, 3], c - 4
        if base is None:
            base = host_inputs(inp, x, mem)
        m = dict(base)
        m["x"] = np.ascontiguousarray(x, np.float32)
        m["mem"] = np.ascontiguousarray(mem, np.float32)
        m["tmask"] = tmask_for(T, T)
        m["lmask"] = lmask_for(cl, chain)
        in_maps.append(m)
    res = run_bass_kernel_spmd(nc, in_maps, core_ids=list(range(8)))
    ys = [r["y"] for r in res.results]
    y_prompt = np.stack(ys[0:4], 0).astype(np.float32)
    y_sample = np.concatenate(ys[4:8], 0)[None].astype(np.float32)
    return (y_prompt, y_sample)
```

```python
import contextlib
import numpy as np
import concourse.bass as bass
import concourse.mybir as mybir
from concourse.bass_utils import run_bass_kernel_spmd

F32 = mybir.dt.float32
BF16 = mybir.dt.bfloat16
AF = mybir.ActivationFunctionType
ALU = mybir.AluOpType

D = 2048
DFF = 5632
DIN = 11360
NMEM = 256
EPS = 1e-6
OFF_Z, OFF_XBC, OFF_DT, OFF_Q, OFF_K, OFF_V, OFF_LOW, OFF_GO = 0, 2048, 5120, 5184, 6208, 7232, 9280, 9312
ENGS = ("pe", "act", "dve", "pool", "sp")
DEBUG = ()
NRING = 24
NROWS = 64 + 64 + 32 + 2048 + 512


class Trk:
    __slots__ = ("w", "r")

    def __init__(self):
        self.w = None
        self.r = {}


class Tile:
    def __init__(self, ap=None, n=1):
        self.ap = ap
        self.ts = [Trk() for _ in range(n)]


class KB:
    def __init__(self, nc, stack):
        self.nc = nc
        self.q = {e: [] for e in ENGS}
        self.cnt = {e: 0 for e in ENGS}
        self.seen = {e: {} for e in ENGS}
        self.esem = {e: stack.enter_context(nc.semaphore("es_" + e)) for e in ENGS}
        self.ring = {qn: [stack.enter_context(nc.semaphore(f"dq_{qn}{i}")) for i in range(NRING)] for qn in ("sp", "pool", "act")}
        self.ringcnt = {qn: [0] * NRING for qn in self.ring}
        self.ringpos = {qn: 0 for qn in self.ring}
        self.semid = {}
        self.collsem = [stack.enter_context(nc.semaphore(f"coll{i}")) for i in range(12)]
        self.ncoll = 0
        self.colltoks = []
        self.arena = nc.alloc_sbuf_tensor("arena", [128, 52992], F32)
        self.aoff = 0
        self.persist = 0
        self.ps = [Tile(nc.alloc_psum_tensor(f"ps{i}", [128, 512], F32)[:, :]) for i in range(8)]
        self.pp = {}

    def sb(self, shape, dtype, n=1):
        esz = 4 if dtype == F32 else 2
        free = int(np.prod(shape[1:]))
        nbytes = (free * esz + 31) // 32 * 32
        assert self.aoff + nbytes <= 52992 * 4, ("SBUF arena overflow", self.aoff, nbytes)
        ap = self.arena[0:shape[0], self.aoff // 4:(self.aoff + nbytes) // 4]
        if dtype != F32:
            ap = ap.bitcast(dtype)
        ap = ap[:, 0:free]
        if len(shape) == 3:
            ap = ap.rearrange("p (a b) -> p a b", a=shape[1])
        elif len(shape) == 4:
            ap = ap.rearrange("p (a b c) -> p a b c", a=shape[1], b=shape[2])
        self.aoff += nbytes
        return Tile(ap, n)

    def reset(self):
        self.aoff = self.persist

    def bank(self, pool):
        lst, i = self.pp[pool]
        self.pp[pool][1] = (i + 1) % len(lst)
        return self.ps[lst[i]]

    def key(self, sem):
        return id(sem)

    def wait(self, e, tok):
        if tok is None:
            return
        sem, val, src = tok
        if src == "pe" and e == "pe":
            return
        k = id(sem)
        if self.seen[e].get(k, 0) >= val:
            return
        self.seen[e][k] = val
        self.q[e].append(("w", sem, val))

    def _deps(self, e, reads, writes):
        for t in reads:
            self.wait(e, t.w)
        for t in writes:
            if t.w is not None and t.w[2] != e:
                self.wait(e, t.w)
            for src, r in t.r.items():
                if src != e:
                    self.wait(e, r)

    def _mark(self, e, tok, reads, writes):
        for t in reads:
            t.r[tok[2] if tok[2] != "dma" else ("dma", id(tok[0]))] = tok
        for t in writes:
            t.w = tok
            t.r = {}

    @staticmethod
    def _trks(lst):
        out = []
        for x in lst:
            if isinstance(x, Tile):
                out.extend(x.ts)
            elif isinstance(x, (list, tuple)):
                out.extend(x)
            else:
                out.append(x)
        return out

    def op(self, e, fn, reads=(), writes=()):
        reads = self._trks(reads)
        writes = self._trks(writes)
        self._deps(e, reads, writes)
        self.cnt[e] += 1
        tok = (self.esem[e], self.cnt[e], e)
        self.q[e].append(("i", fn, self.esem[e], 1))
        self._mark(e, tok, reads, writes)
        return tok

    def dma(self, qn, out, in_, reads=(), writes=()):
        reads = self._trks(reads)
        writes = self._trks(writes)
        for t in reads:
            self.wait(qn, t.w)
        for t in writes:
            self.wait(qn, t.w)
            for r in t.r.values():
                self.wait(qn, r)
        i = self.ringpos[qn]
        self.ringpos[qn] = (i + 1) % NRING
        sem = self.ring[qn][i]
        c = self.ringcnt[qn][i]
        if c:
            self.wait(qn, (sem, c, "dma"))
        self.ringcnt[qn][i] = c + 16
        tok = (sem, c + 16, "dma")
        self.q[qn].append(("i", (lambda e, o=out, s=in_: e.dma_start(out=o, in_=s)), sem, 16))
        self._mark(qn, tok, reads, writes)
        return tok

    def coll(self, kind, rg, in_ap, out_ap, reads=(), writes=()):
        reads = self._trks(reads)
        writes = self._trks(writes)
        qn = "pool"
        for t in reads:
            self.wait(qn, t.w)
        for t in writes:
            self.wait(qn, t.w)
            for r in t.r.values():
                self.wait(qn, r)
        sem = self.collsem[self.ncoll]
        self.ncoll += 1
        tok = (sem, 1, "dma")
        self.q[qn].append(("i", (lambda e: e.collective_compute(kind, ALU.bypass, replica_groups=rg, ins=[in_ap.opt()], outs=[out_ap.opt()])), sem, 1))
        self._mark(qn, tok, reads, writes)
        self.colltoks.append(tok)
        return tok

    def barrier(self):
        for e in ENGS:
            for e2 in ENGS:
                if e2 != e and self.cnt[e2]:
                    self.wait(e, (self.esem[e2], self.cnt[e2], e2))
            for qn in self.ring:
                for i in range(NRING):
                    if self.ringcnt[qn][i]:
                        self.wait(e, (self.ring[qn][i], self.ringcnt[qn][i], "dma"))
            for tok in self.colltoks:
                self.wait(e, tok)

    def replay(self, e, eng):
        for it in self.q[e]:
            if it[0] == "w":
                eng.wait_ge(it[1], it[2])
            else:
                it[1](eng).then_inc(it[2], it[3])

    def mm(self, out, lhsT, rhs, start, stop, reads, writes):
        return self.op("pe", lambda e: e.matmul(out, lhsT, rhs, start=start, stop=stop), reads, writes)

    def tr(self, out, in_, ident, reads, writes):
        return self.op("pe", lambda e: e.transpose(out, in_, ident), reads, writes)

    def act(self, out, in_, func, reads, writes, bias=None, scale=1.0, accum=None, eng="act"):
        kw = {}
        if bias is not None:
            kw["bias"] = bias
        if accum is not None:
            kw["accum_out"] = accum
        return self.op("act", lambda e: e.activation(out, in_, func, scale=scale, **kw), reads, writes)

    def tt(self, e, out, in0, in1, op, reads, writes):
        return self.op(e, lambda g: g.tensor_tensor(out, in0, in1, op), reads, writes)

    def ts(self, e, out, in0, s1, s2, op0, op1, reads, writes):
        if s2 is None:
            return self.op(e, lambda g: g.tensor_scalar(out, in0, s1, None, op0), reads, writes)
        return self.op(e, lambda g: g.tensor_scalar(out, in0, s1, s2, op0, op1), reads, writes)

    def stt(self, e, out, in0, sc, in1, op0, op1, reads, writes):
        return self.op(e, lambda g: g.scalar_tensor_tensor(out, in0, sc, in1, op0, op1), reads, writes)

    def cp(self, e, out, in_, reads, writes):
        if e == "act":
            return self.op("act", lambda g: g.copy(out, in_), reads, writes)
        return self.op(e, lambda g: g.tensor_copy(out, in_), reads, writes)


def build(T, ncores=8):
    NT = T // 512
    NCH = T // 128
    nc = bass.Bass("TRN2", target_bir_lowering=False)
    stack = contextlib.ExitStack()
    kb = KB(nc, stack)

    def din(name, shape, dt=F32):
        return nc.dram_tensor(name, list(shape), dt, kind="ExternalInput").ap()

    def dscr(name, shape, dt):
        return nc.dram_tensor(name, list(shape), dt, kind=("ExternalOutput" if (DEBUG and name in DEBUG) else "Internal")).ap()

    x_in = din("x", [T, D])
    mem_in = din("mem", [NMEM, D])
    consts_in = din("consts", [128, 10 * 128])
    cols_in = din("cols", [128, 6 * 16 + 24 * 6])
    rows_in = din("rows", [1, NROWS])
    wg_in = din("wg", [2, 17, 1024])
    tmask_in = din("tmask", [128, NCH])
    lmask_in = din("lmask", [128, 32])
    GR = min(ncores, 4)
    RG = [list(range(g0, g0 + GR)) for g0 in range(0, ncores, GR)]
    XF = 2 * 6184
    edge_d = dscr("edge_d", [128, 96], BF16)
    edges_g = dscr("edges_g", [GR * 128, 96], BF16)
    PW = 1546
    xg_dp = [dscr(f"xg_d{k}", [128, PW], F32) for k in range(8)]
    xg_gp = [dscr(f"xg_g{k}", [GR * 128, PW], F32) for k in range(8)]

    def segs(lo, hi):
        out = []
        q = lo
        while q < hi:
            k = q // PW
            a = q - k * PW
            b = min(PW, a + (hi - q))
            out.append((k, a, b, q - lo))
            q += b - a
        return out
    sin_d = dscr("sin_d", [2, 128, 6144], F32)
    wsrc = {
        "w1a": din("ffn1_w1", [D, DFF]), "w3a": din("ffn1_w3", [D, DFF]), "w2a": din("ffn1_w2", [DFF, D]),
        "win": din("w_in", [D, DIN]), "wout": din("w_out", [2 * D, D]),
        "wcq": din("w_cq", [D, D]), "wckv": din("w_ckv", [D, 2 * D]), "wco": din("w_co", [D, D]),
        "w1b": din("ffn2_w1", [D, DFF]), "w3b": din("ffn2_w3", [D, DFF]), "w2b": din("ffn2_w2", [DFF, D]),
    }
    y_out = nc.dram_tensor("y", [T, D], F32, kind="ExternalOutput").ap()

    wb = {k: dscr("wb_" + k, v.shape, BF16) for k, v in wsrc.items()}
    x1T = dscr("x1T", [16, 128, T], F32)
    z_tok = dscr("z_tok", [T, 2048], BF16)
    dt_tok = dscr("dt_tok", [T, 64], F32)
    k_tok = dscr("k_tok", [T, 1024], BF16)
    v_tok = dscr("v_tok", [T, 2048], BF16)
    go_tok = dscr("go_tok", [T, 2048], BF16)
    xbcT = dscr("xbcT", [24, 128, T], BF16)
    qT_d = dscr("qT", [8, 128, T], BF16)
    kT_d = dscr("kT", [8, 128, T], BF16)
    lowT = dscr("lowT", [2, 16, T], F32)
    xs_tok = dscr("xs_tok", [T, 2048], BF16)
    B_tok = dscr("B_tok", [T, 512], BF16)
    BT_d = dscr("BT", [4, 128, T], BF16)
    CT_d = dscr("CT", [4, 128, T], BF16)
    yf_tok = dscr("yf_tok", [T, 4096], F32)
    mix_tok = dscr("mix_tok", [T, 4096], BF16)

    cst = kb.sb([128, 10, 128], F32)
    cstb = kb.sb([128, 10, 128], BF16)
    cols = kb.sb([128, 6 * 16 + 24 * 6], F32)
    epsc = kb.sb([128, 1], F32)
    kb.dma("sp", cst.ap, consts_in.rearrange("p (a b) -> p a b", a=10), [], [cst])
    kb.dma("sp", cols.ap, cols_in, [], [cols])
    kb.cp("dve", cstb.ap, cst.ap, [cst], [cstb])
    kb.op("dve", lambda g: g.memset(epsc.ap, EPS), [], [epsc])
    IDF = cst.ap[:, 0, :]
    IDB = cstb.ap[:, 0, :]
    ONESF = cst.ap[:, 5, :]

    def ncol(i):
        return cols.ap[:, 16 * i:16 * i + 16]
    N_FFN1, N_MIX, N_XA, N_MEM, N_FFN2, N_FIN = range(6)
    convw = cols.ap[:, 96:96 + 120].rearrange("p (c k) -> p c k", c=24)
    convb = cols.ap[:, 216:240]
    kb.persist = kb.aoff

    CW = 4096
    wtmp = [kb.sb([128, CW], F32) for _ in range(2)]
    wtb = [kb.sb([128, CW], BF16) for _ in range(2)]
    it = 0
    cast_engs = ("dve", "pool", "act")
    for name, src in wsrc.items():
        K, N = src.shape
        tot = K * N // 128
        s2 = src.rearrange("(p a) n -> p (a n)", p=128)
        d2 = wb[name].rearrange("(p a) n -> p (a n)", p=128)
        for c0 in range(0, tot, CW):
            cw = min(CW, tot - c0)
            a, b = wtmp[it % 2], wtb[it % 2]
            kb.dma("sp", a.ap[:, 0:cw], s2[:, c0:c0 + cw], [], [a])
            kb.cp(cast_engs[it % 3], b.ap[:, 0:cw], a.ap[:, 0:cw], [a], [b])
            kb.dma("pool", d2[:, c0:c0 + cw], b.ap[:, 0:cw], [b], [])
            it += 1
    kb.barrier()
    kb.reset()

    def rmsnorm_fm(xT, hT, ncols, sqb, rstd, pbank):
        W = xT.ap.shape[2]
        for kc in range(16):
            s = sqb[kc % 2]
            kb.act(s.ap[:, 0:W], xT.ap[:, kc, :], AF.Square, [xT.ts[kc]], [s])
            kb.mm(pbank.ap[:, 0:W], ONESF, s.ap[:, 0:W], kc == 0, kc == 15, [s, cst], [pbank])
        kb.act(rstd.ap[:, 0:W], pbank.ap[:, 0:W], AF.Sqrt, [pbank, epsc], [rstd], bias=epsc.ap, scale=1.0 / D)
        kb.op("dve", lambda g: g.reciprocal(rstd.ap[:, 0:W], rstd.ap[:, 0:W]), [rstd], [rstd])
        for kc in range(16):
            kb.stt("dve", hT.ap[:, kc, :], xT.ap[:, kc, :], ncols[:, kc:kc + 1], rstd.ap[:, 0:W], ALU.mult, ALU.mult,
                   [xT.ts[kc], rstd, cols], [hT.ts[kc]])

    def load_panel(wname, KC, c0, w, slot):
        src = wb[wname].rearrange("(kc p) n -> p kc n", p=128)[:, :, c0:c0 + w]
        flat = slot.ap.rearrange("p a b -> p (a b)") if len(slot.ap.shape) == 3 else slot.ap
        dst = flat[:, 0:KC * w].rearrange("p (a b) -> p a b", a=KC)
        kb.dma("sp", dst, src, [], [slot])
        return dst

    class PanelStream:
        def __init__(self, slots, reqs):
            self.slots = slots
            self.reqs = reqs
            self.views = {}
            for i in range(min(len(slots), len(reqs))):
                self._issue(i)

        def _issue(self, i):
            slot = self.slots[i % len(self.slots)]
            self.views[i] = (load_panel(*self.reqs[i], slot), slot)

        def get(self, i):
            return self.views[i]

        def done(self, i):
            del self.views[i]
            j = i + len(self.slots)
            if j < len(self.reqs):
                self._issue(j)

    def ffn(xT, hT, hid, w1, w3, w2, ws8, ws11, sa, tile_tag):
        W = xT.ap.shape[2]
        reqs = []
        for fp in range(DFF // 256):
            reqs.append((w1, 16, fp * 256, 256))
            reqs.append((w3, 16, fp * 256, 256))
        st = PanelStream(ws8, reqs)
        for fp in range(DFF // 256):
            p1, s1 = st.get(2 * fp)
            p3, s3 = st.get(2 * fp + 1)
            for j in range(2):
                fc = fp * 2 + j
                pa = kb.bank("ab")
                pb = kb.bank("ab")
                for kc in range(16):
                    kb.mm(pa.ap[:, 0:W], p1[:, kc, j * 128:(j + 1) * 128], hT.ap[:, kc, :], kc == 0, kc == 15, [s1, hT.ts[kc]], [pa])
                for kc in range(16):
                    kb.mm(pb.ap[:, 0:W], p3[:, kc, j * 128:(j + 1) * 128], hT.ap[:, kc, :], kc == 0, kc == 15, [s3, hT.ts[kc]], [pb])
                s = sa[fc % 2]
                kb.act(s.ap[:, 0:W], pa.ap[:, 0:W], AF.Silu, [pa], [s])
                kb.tt("dve", hid.ap[:, fc, :], s.ap[:, 0:W], pb.ap[:, 0:W], ALU.mult, [s, pb], [hid.ts[fc]])
            st.done(2 * fp)
            st.done(2 * fp + 1)
        reqs2 = [(w2, 44, dc * 128, 128) for dc in range(16)]
        st2 = PanelStream(ws11, reqs2)
        for dc in range(16):
            p2, s2 = st2.get(dc)
            po = kb.bank("o")
            for fc in range(44):
                kb.mm(po.ap[:, 0:W], p2[:, fc, :], hid.ap[:, fc, :], fc == 0, fc == 43, [s2, hid.ts[fc]], [po])
            kb.stt("dve", xT.ap[:, dc, :], po.ap[:, 0:W], 0.5, xT.ap[:, dc, :], ALU.mult, ALU.add, [po, xT.ts[dc]], [xT.ts[dc]])
            st2.done(dc)

    kb.pp = {"ab": [[0, 1, 2, 3], 0], "o": [[4, 5], 0], "m": [[6, 7], 0]}
    xtok = [kb.sb([128, 2048], F32) for _ in range(2)]
    xT = kb.sb([128, 16, 512], F32, n=16)
    hT = kb.sb([128, 16, 512], BF16, n=16)
    hid = kb.sb([128, 44, 512], BF16, n=44)
    ws8 = [kb.sb([128, 16, 256], BF16) for _ in range(4)]
    ws11 = [kb.sb([128, 44, 128], BF16) for _ in range(2)]
    sqb = [kb.sb([128, 512], F32) for _ in range(2)]
    rstd = kb.sb([128, 512], F32)
    sa = [kb.sb([128, 512], F32) for _ in range(2)]
    ev = [kb.sb([128, 512], BF16) for _ in range(4)]
    evf = [kb.sb([128, 512], F32) for _ in range(2)]
    evi = [0]

    def nev():
        evi[0] += 1
        return ev[evi[0] % 4]

    for ti in range(NT):
        t0 = ti * 512
        for sub in range(4):
            xt = xtok[sub % 2]
            kb.dma("sp", xt.ap, x_in[t0 + sub * 128:t0 + (sub + 1) * 128, :], [], [xt])
            for k4 in range(4):
                pm = kb.bank("m")
                for u in range(4):
                    kc = k4 * 4 + u
                    kb.tr(pm.ap[:, u * 128:(u + 1) * 128], xt.ap[:, kc * 128:(kc + 1) * 128], IDF, [xt, cst], [pm])
                e = "dve" if k4 % 2 == 0 else "act"
                kb.cp(e, xT.ap[:, k4 * 4:k4 * 4 + 4, sub * 128:(sub + 1) * 128],
                      pm.ap.rearrange("p (a b) -> p a b", a=4), [pm], xT.ts[k4 * 4:k4 * 4 + 4])
        rmsnorm_fm(xT, hT, ncol(N_FFN1), sqb, rstd, kb.bank("m"))
        ffn(xT, hT, hid, "w1a", "w3a", "w2a", ws8, ws11, sa, ti)
        kb.dma("pool", x1T[:, :, t0:t0 + 512].rearrange("k p t -> p k t"), xT.ap, [xT], [])
        rmsnorm_fm(xT, hT, ncol(N_MIX), sqb, rstd, kb.bank("m"))
        tm_jobs = [(z_tok, OFF_Z, 2048), (k_tok, OFF_K, 1024), (v_tok, OFF_V, 2048), (go_tok, OFF_GO, 2048)]
        fm_jobs = [(xbcT, OFF_XBC, 24, 1.0), (qT_d, OFF_Q, 8, 0.0625), (kT_d, OFF_K, 8, 1.0)]
        reqs = []
        for dst, c0, ncs in tm_jobs:
            for pi in range(ncs // 256):
                reqs.append(("win", 16, c0 + pi * 256, 256))
        for dst, c0, nch, sc in fm_jobs:
            for pi in range(nch // 2):
                reqs.append(("win", 16, c0 + pi * 256, 256))
        reqs.append(("win", 16, OFF_DT, 64))
        reqs.append(("win", 16, OFF_LOW, 32))
        st = PanelStream(ws8, reqs)
        ri = 0
        for dst, c0, ncs in tm_jobs:
            for pi in range(ncs // 256):
                pv, ps_ = st.get(ri)
                ri += 1
                for sub in range(4):
                    po = kb.bank("ab")
                    for kc in range(16):
                        kb.mm(po.ap[:, 0:256], hT.ap[:, kc, sub * 128:(sub + 1) * 128], pv[:, kc, :], kc == 0, kc == 15,
                              [ps_, hT.ts[kc]], [po])
                    o = nev()
                    kb.cp("act" if sub % 2 else "dve", o.ap[:, 0:256], po.ap[:, 0:256], [po], [o])
                    kb.dma("pool", dst[t0 + sub * 128:t0 + (sub + 1) * 128, pi * 256:(pi + 1) * 256], o.ap[:, 0:256], [o], [])
                st.done(ri - 1)
        for dst, c0, nch, sc in fm_jobs:
            for pi in range(nch // 2):
                pv, ps_ = st.get(ri)
                ri += 1
                for j in range(2):
                    po = kb.bank("ab")
                    for kc in range(16):
                        kb.mm(po.ap, pv[:, kc, j * 128:(j + 1) * 128], hT.ap[:, kc, :], kc == 0, kc == 15, [ps_, hT.ts[kc]], [po])
                    o = nev()
                    if sc == 1.0:
                        kb.cp("act" if j % 2 else "dve", o.ap, po.ap, [po], [o])
                    else:
                        kb.ts("dve", o.ap, po.ap, sc, None, ALU.mult, None, [po], [o])
                    kb.dma("pool", dst[pi * 2 + j, :, t0:t0 + 512], o.ap, [o], [])
                st.done(ri - 1)
        pv, ps_ = st.get(ri)
        ri += 1
        for sub in range(4):
            po = kb.bank("ab")
            for kc in range(16):
                kb.mm(po.ap[:, 0:64], hT.ap[:, kc, sub * 128:(sub + 1) * 128], pv[:, kc, :], kc == 0, kc == 15, [ps_, hT.ts[kc]], [po])
            o = evf[sub % 2]
            kb.cp("dve", o.ap[:, 0:64], po.ap[:, 0:64], [po], [o])
            kb.dma("pool", dt_tok[t0 + sub * 128:t0 + (sub + 1) * 128, :], o.ap[:, 0:64], [o], [])
        st.done(ri - 1)
        pv, ps_ = st.get(ri)
        ri += 1
        for j in range(2):
            po = kb.bank("ab")
            for kc in range(16):
                kb.mm(po.ap[0:16, :], pv[:, kc, j * 16:(j + 1) * 16], hT.ap[:, kc, :], kc == 0, kc == 15, [ps_, hT.ts[kc]], [po])
            o = evf[j % 2]
            kb.cp("dve", o.ap[0:16, :], po.ap[0:16, :], [po], [o])
            kb.dma("pool", lowT[j, :, t0:t0 + 512], o.ap[0:16, :], [o], [])
    kb.barrier()
    kb.reset()
    if STOP_AFTER == "A":
        return finalize(nc, kb, stack)

    kb.pp = {"a": [[0, 1], 0], "b": [[2, 3], 0], "c": [[4, 5], 0], "d": [[6, 7], 0]}
    lmk = kb.sb([128, 32], F32)
    kb.dma("sp", lmk.ap, lmask_in, [], [lmk])
    xsrc = xbcT.rearrange("c p t -> p c t")
    agE = Tile(None)
    e0 = kb.sb([128, 24, 2], BF16)
    e1 = kb.sb([128, 24, 2], BF16)
    with nc.allow_non_contiguous_dma(reason="tiny conv halo"):
        pass
    kb.dma("sp", e0.ap, xsrc[:, :, 0:2], [], [e0])
    kb.dma("sp", e1.ap, xsrc[:, :, T - 2:T], [], [e1])
    kb.dma("sp", edge_d[:, 0:48], e0.ap.rearrange("p a b -> p (a b)"), [e0], [agE])
    kb.dma("sp", edge_d[:, 48:96], e1.ap.rearrange("p a b -> p (a b)"), [e1], [agE])
    kb.coll("AllGather", RG, edge_d[:, :], edges_g[:, :], [agE], [agE])
    eg = kb.sb([128, GR, 96], BF16)
    kb.dma("sp", eg.ap, edges_g.rearrange("(r p) f -> p r f", p=128), [agE], [eg])
    hl = kb.sb([128, 48], F32)
    hr = kb.sb([128, 48], F32)
    kb.op("dve", lambda g: g.memset(hl.ap, 0.0), [], [hl])
    kb.op("dve", lambda g: g.memset(hr.ap, 0.0), [], [hr])
    for r in range(GR):
        kb.stt("dve", hl.ap, eg.ap[:, r, 48:96], lmk.ap[:, 16 + r:17 + r], hl.ap, ALU.mult, ALU.add, [eg, lmk, hl], [hl])
        kb.stt("dve", hr.ap, eg.ap[:, r, 0:48], lmk.ap[:, 24 + r:25 + r], hr.ap, ALU.mult, ALU.add, [eg, lmk, hr], [hr])
    xh = [kb.sb([128, 24, 516], BF16) for _ in range(2)]
    cacc = [kb.sb([128, 512], F32) for _ in range(3)]
    xc = kb.sb([128, 24, 512], BF16, n=24)
    tko = [kb.sb([128, 2048], BF16) for _ in range(2)]
    tkb = [kb.sb([128, 512], BF16) for _ in range(2)]
    for ti in range(NT):
        t0 = ti * 512
        xt = xh[ti % 2]
        src = xbcT.rearrange("c p t -> p c t")
        lo = 2 if ti == 0 else 0
        hi = 514 if ti == NT - 1 else 516
        if ti == 0:
            kb.cp("dve", xt.ap[:, :, 0:2], hl.ap.rearrange("p (a b) -> p a b", a=24), [hl], [xt])
        if ti == NT - 1:
            kb.cp("dve", xt.ap[:, :, 514:516], hr.ap.rearrange("p (a b) -> p a b", a=24), [hr], [xt])
        kb.dma("sp", xt.ap[:, :, lo:hi], src[:, :, t0 - 2 + lo:t0 - 2 + hi], [], [xt])
        for c in range(24):
            acc = cacc[c % 3]
            e = "dve"
            kb.ts(e, acc.ap, xt.ap[:, c, 0:512], convw[:, c, 0:1], convb[:, c:c + 1], ALU.mult, ALU.add, [xt, cols], [acc])
            for k in range(1, 5):
                kb.stt(e, acc.ap, xt.ap[:, c, k:k + 512], convw[:, c, k:k + 1], acc.ap, ALU.mult, ALU.add, [xt, cols, acc], [acc])
            kb.act(xc.ap[:, c, :], acc.ap, AF.Silu, [acc], [xc.ts[c]])
        for sub in range(4):
            o = tko[sub % 2]
            for c4 in range(4):
                pm = kb.bank("a" if c4 % 2 else "b")
                pmb = pm.ap[:, 0:256].bitcast(BF16)
                for u in range(4):
                    c = c4 * 4 + u
                    kb.tr(pmb[:, u * 128:(u + 1) * 128], xc.ap[:, c, sub * 128:(sub + 1) * 128], IDB, [xc.ts[c], cstb], [pm])
                kb.cp("act" if c4 % 2 else "dve", o.ap[:, c4 * 512:(c4 + 1) * 512], pmb, [pm], [o])
            kb.dma("pool", xs_tok[t0 + sub * 128:t0 + (sub + 1) * 128, :], o.ap, [o], [])
            ob = tkb[sub % 2]
            pm = kb.bank("c")
            pmb = pm.ap[:, 0:256].bitcast(BF16)
            for u in range(4):
                kb.tr(pmb[:, u * 128:(u + 1) * 128], xc.ap[:, 16 + u, sub * 128:(sub + 1) * 128], IDB, [xc.ts[16 + u], cstb], [pm])
            kb.cp("dve", ob.ap, pmb, [pm], [ob])
            kb.dma("pool", B_tok[t0 + sub * 128:t0 + (sub + 1) * 128, :], ob.ap, [ob], [])
        kb.dma("pool", BT_d[:, :, t0:t0 + 512].rearrange("c p t -> p c t"), xc.ap[:, 16:20, :], [xc.ts[16:20]], [])
        kb.dma("pool", CT_d[:, :, t0:t0 + 512].rearrange("c p t -> p c t"), xc.ap[:, 20:24, :], [xc.ts[20:24]], [])
    kb.barrier()
    kb.reset()

    if STOP_AFTER == "B0":
        return finalize(nc, kb, stack)
    rows = kb.sb([128, NROWS], F32)
    kb.dma("sp", rows.ap, rows_in.partition_broadcast(128).rearrange("p a b -> p (a b)"), [], [rows])
    arow = kb.sb([128, 64], F32)
    kb.act(arow.ap, rows.ap[:, 64:128], AF.Exp, [rows], [arow])
    kb.ts("dve", arow.ap, arow.ap, -1.0, None, ALU.mult, None, [arow], [arow])
    tmk = kb.sb([128, NCH], F32)
    kb.dma("sp", tmk.ap, tmask_in, [], [tmk])
    onec = kb.sb([128, 1], F32)
    kb.op("dve", lambda g: g.memset(onec.ap, 1.0), [], [onec])
    R_DSK, R_SSDN, R_GLAN = 128, 160, 160 + 2048
    wgt = kb.sb([17, 1024], F32)
    lowa = [kb.sb([17, 64], F32) for _ in range(2)]
    for l in lowa:
        kb.op("dve", lambda g, a=l.ap: g.memset(a, 1.0), [], [l])
    lmk = kb.sb([128, 32], F32)
    kb.dma("sp", lmk.ap, lmask_in, [], [lmk])
    Dt_s = kb.sb([128, 32], F32)
    Dt_g = kb.sb([128, 8], F32)
    aoff_keep = kb.aoff
    S_ssd = kb.sb([128, 4, 512], F32, n=4)
    S_ssdb = kb.sb([128, 4, 512], BF16, n=4)
    S_gla = kb.sb([128, 8, 512], F32, n=8)
    S_glab = kb.sb([128, 8, 512], BF16, n=8)
    db = lambda shape, dt: [kb.sb(shape, dt) for _ in range(2)]
    xs_b, Bt_b, BT_b, CT_b, dtr_b = db([128, 2048], BF16), db([128, 512], BF16), db([128, 4, 128], BF16), db([128, 4, 128], BF16), db([128, 64], F32)
    kt_b, vt_b, qT_b, kT_b = db([64, 1024], BF16), db([64, 2048], BF16), db([128, 8, 64], BF16), db([128, 8, 64], BF16)
    dts = kb.sb([128, 32], F32)
    av = kb.sb([128, 32], F32)
    acs = kb.sb([128, 64], F32)
    dd = kb.sb([128, 32], F32)
    ea = kb.sb([128, 32], F32)
    dec = kb.sb([128, 32], F32)
    xes = kb.sb([128, 32], F32)
    xdt = kb.sb([128, 2048], BF16)
    xend = kb.sb([128, 2048], BF16)
    Lm = kb.sb([128, 8, 128], F32)
    Em = kb.sb([128, 8, 128], F32)
    Mh = kb.sb([128, 8, 128], BF16)
    cbm = kb.sb([128, 128], F32)
    ysb = kb.sb([128, 2048], F32, n=4)
    ytmp = kb.sb([128, 512], F32)
    sp_ = kb.sb([64, 1024], F32)
    EG = kb.sb([128, 8, 64], F32)
    Gs = kb.sb([128, 8, 64], F32)
    E1 = kb.sb([128, 8, 64], F32)
    Gm = kb.sb([128, 8, 64], F32)
    E2 = kb.sb([128, 8, 64], F32)
    qp = kb.sb([128, 8, 64], BF16)
    kp = kb.sb([128, 8, 64], BF16)
    qin = kb.sb([128, 8, 64], BF16)
    Ek = kb.sb([64, 1024], F32)
    kend = kb.sb([64, 1024], BF16)
    Am = kb.sb([64, 4, 64], BF16)
    osb = kb.sb([64, 2048], F32, n=4)
    yfb = kb.sb([128, 2048], F32)
    zb = kb.sb([128, 2048], BF16)
    szb = kb.sb([128, 2048], F32)
    mixb = db([128, 2048], BF16)
    ssq = kb.sb([128, 8], F32)
    junk = kb.sb([128, 512], F32)

    def bc(ap2, n):
        return ap2.unsqueeze(2).to_broadcast([ap2.shape[0], ap2.shape[1], n])

    def scan_pass(dirn, full, final, init=False, save=False):
        fwd = dirn == 0
        M_INC = 1 if fwd else 2
        M_STR = 3 if fwd else 4
        G_INC = 6 if fwd else 7
        G_STR = 8 if fwd else 9
        mid = 31 if fwd else 32
        last = 63 if fwd else 0
        kb.dma("sp", wgt.ap, wg_in[dirn], [], [wgt])
        if init:
            kb.dma("sp", S_ssd.ap.rearrange("p a b -> p (a b)"), sin_d[dirn, :, 0:2048], [], [S_ssd])
            kb.dma("sp", S_gla.ap.rearrange("p a b -> p (a b)"), sin_d[dirn, :, 2048:6144], [], [S_gla])
            for g in range(4):
                kb.cp("pool", S_ssdb.ap[:, g, :], S_ssd.ap[:, g, :], [S_ssd.ts[g]], [S_ssdb.ts[g]])
            for g in range(8):
                kb.cp("pool", S_glab.ap[:, g, :], S_gla.ap[:, g, :], [S_gla.ts[g]], [S_glab.ts[g]])
        else:
            for g in range(4):
                kb.op("pool", lambda e, a=S_ssd.ap[:, g, :]: e.memset(a, 0.0), [], [S_ssd.ts[g]])
                kb.op("pool", lambda e, a=S_ssdb.ap[:, g, :]: e.memset(a, 0.0), [], [S_ssdb.ts[g]])
            for g in range(8):
                kb.op("pool", lambda e, a=S_gla.ap[:, g, :]: e.memset(a, 0.0), [], [S_gla.ts[g]])
                kb.op("pool", lambda e, a=S_glab.ap[:, g, :]: e.memset(a, 0.0), [], [S_glab.ts[g]])
        if save:
            kb.op("dve", lambda e: e.memset(Dt_s.ap, 1.0), [], [Dt_s])
            kb.op("dve", lambda e: e.memset(Dt_g.ap, 1.0), [], [Dt_g])
        order = list(range(NCH)) if fwd else list(range(NCH - 1, -1, -1))
        for n_, ci in enumerate(order):
            t0 = ci * 128
            pb_ = n_ % 2
            xs, Bt, BT, CT, dtr = xs_b[pb_], Bt_b[pb_], BT_b[pb_], CT_b[pb_], dtr_b[pb_]
            kb.dma("sp", xs.ap, xs_tok[t0:t0 + 128, :], [], [xs])
            kb.dma("sp", Bt.ap, B_tok[t0:t0 + 128, :], [], [Bt])
            kb.dma("sp", dtr.ap, dt_tok[t0:t0 + 128, :], [], [dtr])
            if full:
                kb.dma("sp", BT.ap, BT_d[:, :, t0:t0 + 128].rearrange("c p t -> p c t"), [], [BT])
                kb.dma("sp", CT.ap, CT_d[:, :, t0:t0 + 128].rearrange("c p t -> p c t"), [], [CT])
            c0 = 32 * dirn
            kb.tt("dve", dts.ap, dtr.ap[:, c0:c0 + 32], rows.ap[:, c0:c0 + 32], ALU.add, [dtr, rows], [dts])
            kb.act(dts.ap, dts.ap, AF.Exp, [dts], [dts])
            kb.act(dts.ap, dts.ap, AF.Ln, [dts, onec], [dts], bias=onec.ap)
            kb.tt("dve", av.ap, dts.ap, arow.ap[:, c0:c0 + 32], ALU.mult, [dts, arow], [av])
            pm = kb.bank("a")
            kb.mm(pm.ap[:, 0:32], cst.ap[:, M_INC, :], av.ap, True, True, [cst, av], [pm])
            kb.mm(pm.ap[:, 32:64], ONESF, av.ap, True, True, [cst, av], [pm])
            kb.cp("dve", acs.ap, pm.ap[:, 0:64], [pm], [acs])
            kb.tt("dve", dd.ap, acs.ap[:, 32:64], acs.ap[:, 0:32], ALU.subtract, [acs], [dd])
            kb.act(xes.ap, dd.ap, AF.Exp, [dd], [xes])
            kb.act(ea.ap, acs.ap[:, 0:32], AF.Exp, [acs], [ea])
            kb.act(dec.ap, acs.ap[:, 32:64], AF.Exp, [acs], [dec])
            if save:
                kb.tt("dve", Dt_s.ap, Dt_s.ap, dec.ap, ALU.mult, [Dt_s, dec], [Dt_s])
            kb.ts("dve", dts.ap, dts.ap, tmk.ap[:, ci:ci + 1], None, ALU.mult, None, [dts, tmk], [dts])
            kb.tt("dve", xes.ap, xes.ap, dts.ap, ALU.mult, [xes, dts], [xes])
            x3 = xs.ap.rearrange("p (h d) -> p h d", h=32)
            kb.tt("dve", xend.ap.rearrange("p (h d) -> p h d", h=32), x3, bc(xes.ap, 64), ALU.mult, [xs, xes], [xend])
            if full:
                kb.tt("pool", xdt.ap.rearrange("p (h d) -> p h d", h=32), x3, bc(dts.ap, 64), ALU.mult, [xs, dts], [xdt])
            for g in range(4):
                gs = slice(g * 512, (g + 1) * 512)
                if full and BIS != 4:
                    pyo = kb.bank("b")
                    kb.mm(pyo.ap, CT.ap[:, g, :], S_ssdb.ap[:, g, :], True, True, [CT, S_ssdb.ts[g]], [pyo])
                    pcb = kb.bank("a")
                    kb.mm(pcb.ap[:, 0:128], BT.ap[:, g, :], CT.ap[:, g, :], True, True, [BT, CT], [pcb])
                    kb.tt("dve", cbm.ap, pcb.ap[:, 0:128], cst.ap[:, M_INC, :], ALU.mult, [pcb, cst], [cbm])
                    kb.tt("pool", Lm.ap, cst.ap[:, M_STR:M_STR + 1, :].to_broadcast([128, 8, 128]),
                          bc(av.ap[:, g * 8:(g + 1) * 8], 128), ALU.mult, [cst, av], [Lm])
                    pys = kb.bank("d")
                    for hh in range(2):
                        psg = kb.bank("c")
                        for u in range(4):
                            kb.mm(psg.ap[:, u * 128:(u + 1) * 128], Lm.ap[:, hh * 4 + u, :], cst.ap[:, M_INC, :], True, True, [Lm, cst], [psg])
                        kb.act(Em.ap[:, hh * 4:hh * 4 + 4, :], psg.ap.rearrange("p (a b) -> p a b", a=4), AF.Exp, [psg], [Em])
                    kb.tt("dve", Mh.ap, Em.ap, cbm.ap.unsqueeze(1).to_broadcast([128, 8, 128]), ALU.mult, [Em, cbm], [Mh])
                    for h in range(8):
                        hd = (g * 8 + h) * 64
                        kb.mm(pys.ap[:, h * 64:(h + 1) * 64], Mh.ap[:, h, :], xdt.ap[:, hd:hd + 64], True, True, [Mh, xdt], [pys])
                    kb.tt("dve", ytmp.ap.rearrange("p (h d) -> p h d", h=8), pyo.ap.rearrange("p (h d) -> p h d", h=8),
                          bc(ea.ap[:, g * 8:(g + 1) * 8], 64), ALU.mult, [pyo, ea], [ytmp])
                    kb.tt("dve", ysb.ap[:, gs], ytmp.ap, pys.ap, ALU.add, [ytmp, pys], [ysb.ts[g]])
                pu = kb.bank("b")
                kb.mm(pu.ap, Bt.ap[:, g * 128:(g + 1) * 128], xend.ap[:, gs], True, True, [Bt, xend], [pu])
                S3 = S_ssd.ap[:, g, :].rearrange("p (h d) -> p h d", h=8)
                kb.tt("pool", S3, S3, bc(dec.ap[:, g * 8:(g + 1) * 8], 64), ALU.mult, [S_ssd.ts[g], dec], [S_ssd.ts[g]])
                kb.tt("dve", S_ssd.ap[:, g, :], S_ssd.ap[:, g, :], pu.ap, ALU.add, [S_ssd.ts[g], pu], [S_ssd.ts[g]])
                kb.cp("act", S_ssdb.ap[:, g, :], S_ssd.ap[:, g, :], [S_ssd.ts[g]], [S_ssdb.ts[g]])
            if full and not final:
                kb.dma("pool", yf_tok[t0:t0 + 128, 0:2048], ysb.ap, [ysb], [])
            if final:
                kb.dma("sp", yfb.ap, yf_tok[t0:t0 + 128, 0:2048], [], [yfb])
                kb.dma("sp", zb.ap, z_tok[t0:t0 + 128, :], [], [zb])
                kb.tt("dve", ysb.ap, ysb.ap, yfb.ap, ALU.add, [ysb, yfb], [ysb])
                kb.tt("pool", yfb.ap.rearrange("p (h d) -> p h d", h=32), x3, bc(rows.ap[:, R_DSK:R_DSK + 32], 64), ALU.mult,
                      [xs, rows], [yfb])
                kb.tt("dve", ysb.ap, ysb.ap, yfb.ap, ALU.add, [ysb, yfb], [ysb])
                kb.act(szb.ap, zb.ap, AF.Silu, [zb], [szb])
                kb.tt("dve", ysb.ap, ysb.ap, szb.ap, ALU.mult, [ysb, szb], [ysb])
                for g in range(4):
                    kb.act(junk.ap, ysb.ap[:, g * 512:(g + 1) * 512], AF.Square, [ysb], [junk, ssq], accum=ssq.ap[:, g:g + 1])
                kb.act(ssq.ap[:, 0:4], ssq.ap[:, 0:4], AF.Sqrt, [ssq, epsc], [ssq], bias=epsc.ap, scale=1.0 / 512)
                kb.op("dve", lambda e: e.reciprocal(ssq.ap[:, 0:4], ssq.ap[:, 0:4]), [ssq], [ssq])
                mo = mixb[n_ % 2]
                for g in range(4):
                    kb.stt("dve", mo.ap[:, g * 512:(g + 1) * 512], ysb.ap[:, g * 512:(g + 1) * 512], ssq.ap[:, g:g + 1],
                           rows.ap[:, R_SSDN + g * 512:R_SSDN + (g + 1) * 512], ALU.mult, ALU.mult, [ysb, ssq, rows], [mo])
                kb.dma("pool", mix_tok[t0:t0 + 128, 0:2048], mo.ap, [mo], [])
            for sn, sub in enumerate((0, 1) if fwd else (1, 0)):
                s0 = t0 + sub * 64
                qb_ = (n_ * 2 + sn) % 2
                kt, vt, qT, kT, la = kt_b[qb_], vt_b[qb_], qT_b[qb_], kT_b[qb_], lowa[qb_]
                kb.dma("sp", kt.ap, k_tok[s0:s0 + 64, :], [], [kt])
                kb.dma("sp", vt.ap, v_tok[s0:s0 + 64, :], [], [vt])
                kb.dma("sp", la.ap[0:16, :], lowT[dirn, :, s0:s0 + 64], [], [la])
                if full:
                    kb.dma("sp", qT.ap, qT_d[:, :, s0:s0 + 64].rearrange("c p t -> p c t"), [], [qT])
                    kb.dma("sp", kT.ap, kT_d[:, :, s0:s0 + 64].rearrange("c p t -> p c t"), [], [kT])
                for hf in range(2):
                    pg = kb.bank("c")
                    kb.mm(pg.ap[0:64, :], la.ap[0:17, :], wgt.ap[0:17, hf * 512:(hf + 1) * 512], True, True, [la, wgt], [pg])
                    kb.act(sp_.ap[:, hf * 512:(hf + 1) * 512], pg.ap[0:64, :], AF.Exp, [pg], [sp_], scale=-1.0)
                kb.act(sp_.ap, sp_.ap, AF.Ln, [sp_, onec], [sp_], bias=onec.ap[0:64, :])
                pG = kb.bank("a")
                for kc in range(8):
                    kb.mm(pG.ap[:, kc * 64:(kc + 1) * 64], sp_.ap[0:64, kc * 128:(kc + 1) * 128], cst.ap[0:64, G_INC, 0:64], True, True,
                          [sp_, cst], [pG])
                pG3 = pG.ap.rearrange("p (a b) -> p a b", a=8)
                kb.cp("dve", Gs.ap, pG3, [pG], [Gs])
                kb.act(EG.ap, Gs.ap, AF.Exp, [Gs], [EG])
                if save:
                    kb.tt("dve", Dt_g.ap, Dt_g.ap, EG.ap[:, :, last], ALU.mult, [Dt_g, EG], [Dt_g])
                for hf in range(2):
                    pD = kb.bank("d")
                    kb.mm(pD.ap[0:64, :], cst.ap[0:64, G_STR, 0:64], sp_.ap[0:64, hf * 512:(hf + 1) * 512], True, True, [sp_, cst], [pD])
                    kb.act(Ek.ap[:, hf * 512:(hf + 1) * 512], pD.ap[0:64, :], AF.Exp, [pD], [Ek])
                kb.tt("pool", kend.ap, kt.ap, Ek.ap, ALU.mult, [kt, Ek], [kend])
                if full and BIS != 3:
                    kb.tt("dve", Gm.ap, Gs.ap, Gs.ap[:, :, mid:mid + 1].to_broadcast([128, 8, 64]), ALU.subtract, [Gs], [Gm])
                    kb.act(E1.ap, Gm.ap, AF.Exp, [Gm], [E1])
                    kb.act(E2.ap, Gm.ap, AF.Exp, [Gm], [E2], scale=-1.0)
                    kb.tt("dve", qp.ap, qT.ap, E1.ap, ALU.mult, [qT, E1], [qp])
                    kb.tt("pool", kp.ap, kT.ap, E2.ap, ALU.mult, [kT, E2], [kp])
                    kb.tt("dve", qin.ap, qT.ap, EG.ap, ALU.mult, [qT, EG], [qin])
                    pA = kb.bank("a")
                    for h in range(4):
                        for kk in range(2):
                            kb.mm(pA.ap[0:64, h * 64:(h + 1) * 64], kp.ap[:, 2 * h + kk, :], qp.ap[:, 2 * h + kk, :], kk == 0, kk == 1,
                                  [kp, qp], [pA])
                    kb.tt("dve", Am.ap, pA.ap[0:64, 0:256].rearrange("p (a b) -> p a b", a=4),
                          cst.ap[0:64, M_INC:M_INC + 1, 0:64].to_broadcast([64, 4, 64]), ALU.mult, [pA, cst], [Am])
                    for h in range(4 if BIS != 5 else 0):
                        po = kb.bank("b")
                        kb.mm(po.ap[0:64, :], Am.ap[:, h, :], vt.ap[:, h * 512:(h + 1) * 512], True, False, [Am, vt], [po])
                        for kk in range(2):
                            kb.mm(po.ap[0:64, :], qin.ap[:, 2 * h + kk, :], S_glab.ap[:, 2 * h + kk, :], False, kk == 1,
                                  [qin, S_glab.ts[2 * h + kk]], [po])
                        kb.cp("act", osb.ap[:, h * 512:(h + 1) * 512], po.ap[0:64, :], [po], [osb.ts[h]])
                for h in range(4):
                    for kk in range(2):
                        j = 2 * h + kk
                        pu = kb.bank("d" if kk else "c")
                        kb.mm(pu.ap, kend.ap[:, j * 128:(j + 1) * 128], vt.ap[:, h * 512:(h + 1) * 512], True, True, [kend, vt], [pu])
                        kb.stt("dve", S_gla.ap[:, j, :], S_gla.ap[:, j, :], EG.ap[:, j, last:last + 1], pu.ap, ALU.mult, ALU.add,
                               [S_gla.ts[j], EG, pu], [S_gla.ts[j]])
                        kb.cp("pool", S_glab.ap[:, j, :], S_gla.ap[:, j, :], [S_gla.ts[j]], [S_glab.ts[j]])
                if full and not final:
                    kb.dma("pool", yf_tok[s0:s0 + 64, 2048:4096], osb.ap, [osb], [])
                if final:
                    kb.dma("sp", yfb.ap[0:64, :], yf_tok[s0:s0 + 64, 2048:4096], [], [yfb])
                    kb.dma("sp", zb.ap[0:64, :], go_tok[s0:s0 + 64, :], [], [zb])
                    kb.tt("dve", osb.ap, osb.ap, yfb.ap[0:64, :], ALU.add, [osb, yfb], [osb])
                    for h in range(4):
                        kb.act(junk.ap[0:64, :], osb.ap[:, h * 512:(h + 1) * 512], AF.Square, [osb], [junk, ssq],
                               accum=ssq.ap[0:64, 4 + h:5 + h])
                    kb.act(ssq.ap[0:64, 4:8], ssq.ap[0:64, 4:8], AF.Sqrt, [ssq, epsc], [ssq], bias=epsc.ap[0:64, :], scale=1.0 / 512)
                    kb.op("dve", lambda e: e.reciprocal(ssq.ap[0:64, 4:8], ssq.ap[0:64, 4:8]), [ssq], [ssq])
                    for h in range(4):
                        kb.stt("dve", osb.ap[:, h * 512:(h + 1) * 512], osb.ap[:, h * 512:(h + 1) * 512], ssq.ap[0:64, 4 + h:5 + h],
                               rows.ap[0:64, R_GLAN:R_GLAN + 512], ALU.mult, ALU.mult, [osb, ssq, rows], [osb])
                    kb.act(szb.ap[0:64, :], zb.ap[0:64, :], AF.Silu, [zb], [szb])
                    mo = mixb[(n_ * 2 + sn) % 2]
                    kb.tt("dve", mo.ap[0:64, :], osb.ap, szb.ap[0:64, :], ALU.mult, [osb, szb], [mo])
                    kb.dma("pool", mix_tok[s0:s0 + 64, 2048:4096], mo.ap[0:64, :], [mo], [])

    agS = Tile(None)

    def save_states(dirn):
        o = dirn * 6184
        for tl, flat, lo, n in ((S_ssd, S_ssd.ap.rearrange("p a b -> p (a b)"), 0, 2048),
                                (S_gla, S_gla.ap.rearrange("p a b -> p (a b)"), 2048, 4096),
                                (Dt_s, Dt_s.ap, 6144, 32), (Dt_g, Dt_g.ap, 6176, 8)):
            for k, a, b, off in segs(o + lo, o + lo + n):
                kb.dma("pool", xg_dp[k][:, a:b], flat[:, off:off + (b - a)], [tl], [agS])

    scan_pass(0, False, False, save=True)
    save_states(0)
    scan_pass(1, False, False, save=True)
    save_states(1)
    kb.barrier()
    for k in range(8):
        kb.coll("AllGather", RG, xg_dp[k][:, :], xg_gp[k][:, :], [agS], [agS])
    aoff_B = kb.aoff
    kb.aoff = aoff_keep
    acc = kb.sb([128, 6144], F32)
    Sr = [kb.sb([128, 6184], F32) for _ in range(2)]
    Deff = [kb.sb([128, 40], F32) for _ in range(2)]
    tmpc = kb.sb([128, 512], F32)
    for dirn in range(2):
        kb.op("dve", lambda e: e.memset(acc.ap, 0.0), [], [acc])
        rr = list(range(GR)) if dirn == 0 else list(range(GR - 1, -1, -1))
        for n_, r in enumerate(rr):
            sr, de = Sr[n_ % 2], Deff[n_ % 2]
            o = dirn * 6184
            for kk in range(4):
                kb.dma("sp", sr.ap[:, kk * PW:(kk + 1) * PW], xg_gp[dirn * 4 + kk][r * 128:(r + 1) * 128, :], [agS], [sr])
            b = lmk.ap[:, dirn * 8 + r:dirn * 8 + r + 1]
            kb.ts("dve", de.ap, sr.ap[:, 6144:6184], -1.0, b, ALU.add, ALU.mult, [sr, lmk], [de])
            kb.ts("dve", de.ap, de.ap, 1.0, None, ALU.add, None, [de], [de])
            a3 = acc.ap[:, 0:2048].rearrange("p (h d) -> p h d", h=32)
            kb.tt("dve", a3, a3, bc(de.ap[:, 0:32], 64), ALU.mult, [acc, de], [acc])
            kb.stt("dve", acc.ap[:, 0:2048], sr.ap[:, 0:2048], b, acc.ap[:, 0:2048], ALU.mult, ALU.add, [sr, lmk, acc], [acc])
            for j in range(8):
                sl = slice(2048 + j * 512, 2048 + (j + 1) * 512)
                kb.ts("pool", tmpc.ap, sr.ap[:, sl], b, None, ALU.mult, None, [sr, lmk], [tmpc])
                kb.stt("dve", acc.ap[:, sl], acc.ap[:, sl], de.ap[:, 32 + j:33 + j], tmpc.ap, ALU.mult, ALU.add, [acc, de, tmpc], [acc])
        kb.dma("pool", sin_d[dirn], acc.ap, [acc], [])
    kb.barrier()
    kb.aoff = aoff_B
    scan_pass(0, True, False, init=True)
    kb.barrier()
    scan_pass(1, True, True, init=True)
    kb.barrier()
    kb.reset()
    if STOP_AFTER == "B":
        return finalize(nc, kb, stack)
    WC = 512
    NTC = T // WC
    kb.pp = {"ab": [[0, 1, 2, 3], 0], "o": [[4, 5], 0], "m": [[6, 7], 0]}
    xT = kb.sb([128, 16, WC], F32, n=16)
    hT = kb.sb([128, 16, WC], BF16, n=16)
    oT = kb.sb([128, 16, WC], BF16, n=16)
    hid = kb.sb([128, 44, WC], BF16, n=44)
    mixT = Tile(hid.ap.rearrange("p a b -> p (a b)")[:, 0:32 * WC].rearrange("p (a b) -> p a b", a=32), 32)
    yT = Tile(hid.ap.rearrange("p a b -> p (a b)")[:, 0:32 * WC].bitcast(F32).rearrange("p (a b) -> p a b", a=16), 16)
    ws8 = [kb.sb([128, 16, 256], BF16) for _ in range(3)]
    ws11 = [kb.sb([128, 44, 128], BF16) for _ in range(2)]
    sqb = [kb.sb([128, 512], F32) for _ in range(2)]
    rstd = kb.sb([128, 512], F32)
    sa = [kb.sb([128, 512], F32) for _ in range(2)]
    kTm = kb.sb([128, 16, 256], BF16)
    vm = kb.sb([128, 2, 2048], BF16)
    mtok = [kb.sb([128, 2048], F32)] * 2
    qTh = kb.sb([128, 4, WC], BF16)
    PT = kb.sb([128, 2, WC], BF16)
    Pf = kb.sb([128, 256], F32)
    Pb = kb.sb([128, 256], BF16)
    st4 = kb.sb([128, 4], F32)

    mT = Tile(hid.ap.rearrange("p a b -> p (a b)")[:, 0:16 * 256].rearrange("p (a b) -> p a b", a=16), 1)
    for mt in range(2):
        mk = mtok[mt]
        kb.dma("sp", mk.ap, mem_in[mt * 128:(mt + 1) * 128, :], [], [mk])
        for q4 in range(4):
            kb.act(sa[q4 % 2].ap, mk.ap[:, q4 * 512:(q4 + 1) * 512], AF.Square, [mk], [sa[q4 % 2], st4], accum=st4.ap[:, q4:q4 + 1])
        kb.op("dve", lambda e, o=rstd.ap[:, mt:mt + 1], i=st4.ap: e.reduce_sum(o, i, mybir.AxisListType.X), [st4], [rstd])
        kb.act(rstd.ap[:, mt:mt + 1], rstd.ap[:, mt:mt + 1], AF.Sqrt, [rstd, epsc], [rstd], bias=epsc.ap, scale=1.0 / D)
        kb.op("dve", lambda e, a=rstd.ap[:, mt:mt + 1]: e.reciprocal(a, a), [rstd], [rstd])
        kb.ts("dve", mk.ap, mk.ap, rstd.ap[:, mt:mt + 1], None, ALU.mult, None, [mk, rstd], [mk])
        for k4 in range(4):
            pm = kb.bank("m")
            for u in range(4):
                kc = k4 * 4 + u
                kb.tr(pm.ap[:, u * 128:(u + 1) * 128], mk.ap[:, kc * 128:(kc + 1) * 128], IDF, [mk, cst], [pm])
            for u in range(4):
                kc = k4 * 4 + u
                kb.ts("dve", mT.ap[:, kc, mt * 128:(mt + 1) * 128], pm.ap[:, u * 128:(u + 1) * 128], ncol(N_MEM)[:, kc:kc + 1], None,
                      ALU.mult, None, [pm, cols], [mT])
    reqs = [("wckv", 16, pi * 256, 256) for pi in range(16)]
    st = PanelStream(ws8, reqs)
    for pi in range(8):
        pv, ps_ = st.get(pi)
        for j in range(2):
            po = kb.bank("ab")
            for kc in range(16):
                kb.mm(po.ap[:, 0:256], pv[:, kc, j * 128:(j + 1) * 128], mT.ap[:, kc, :], kc == 0, kc == 15, [ps_, mT], [po])
            kb.cp("act" if j else "dve", kTm.ap[:, pi * 2 + j, :], po.ap[:, 0:256], [po], [kTm])
        st.done(pi)
    for pi in range(8, 16):
        pv, ps_ = st.get(pi)
        for mt in range(2):
            po = kb.bank("ab")
            for kc in range(16):
                kb.mm(po.ap[:, 0:256], mT.ap[:, kc, mt * 128:(mt + 1) * 128], pv[:, kc, :], kc == 0, kc == 15, [ps_, mT], [po])
            kb.cp("act" if mt else "dve", vm.ap[:, mt, (pi - 8) * 256:(pi - 7) * 256], po.ap[:, 0:256], [po], [vm])
        st.done(pi)

    SCL = 512.0 ** -0.5
    for ti in range(NTC):
        t0 = ti * WC
        kb.dma("sp", xT.ap, x1T[:, :, t0:t0 + WC].rearrange("k p t -> p k t"), [], [xT])
        for sub in range(WC // 128):
            mk = mtok[sub % 2]
            mkb = mk.ap.bitcast(BF16)
            kb.dma("sp", mkb, mix_tok[t0 + sub * 128:t0 + (sub + 1) * 128, :], [], [mk])
            for c4 in range(8):
                pm = kb.bank("m")
                pmb = pm.ap[:, 0:256].bitcast(BF16)
                for u in range(4):
                    c = c4 * 4 + u
                    kb.tr(pmb[:, u * 128:(u + 1) * 128], mkb[:, c * 128:(c + 1) * 128], IDB, [mk, cstb], [pm])
                kb.cp("act" if c4 % 2 else "dve", mixT.ap[:, c4 * 4:c4 * 4 + 4, sub * 128:(sub + 1) * 128],
                      pmb.rearrange("p (a b) -> p a b", a=4), [pm], mixT.ts[c4 * 4:c4 * 4 + 4])
        st = PanelStream(ws8, [("wout", 32, dc * 128, 128) for dc in range(16)])
        for dc in range(16):
            pv, ps_ = st.get(dc)
            po = kb.bank("o")
            for cc in range(32):
                kb.mm(po.ap[:, 0:WC], pv[:, cc, :], mixT.ap[:, cc, :], cc == 0, cc == 31, [ps_, mixT.ts[cc]], [po])
            kb.tt("dve", xT.ap[:, dc, :], xT.ap[:, dc, :], po.ap[:, 0:WC], ALU.add, [po, xT.ts[dc]], [xT.ts[dc]])
            st.done(dc)
        rmsnorm_fm(xT, hT, ncol(N_XA), sqb, rstd, kb.bank("m"))
        st = PanelStream(ws8, [("wcq", 16, pi * 256, 256) for pi in range(8)])
        for h in range(4):
            for pj in range(2):
                pv, ps_ = st.get(h * 2 + pj)
                for j in range(2):
                    po = kb.bank("ab")
                    for kc in range(16):
                        kb.mm(po.ap[:, 0:WC], pv[:, kc, j * 128:(j + 1) * 128], hT.ap[:, kc, :], kc == 0, kc == 15, [ps_, hT.ts[kc]], [po])
                    kb.cp("act" if j else "dve", qTh.ap[:, pj * 2 + j, :], po.ap[:, 0:WC], [po], [qTh])
                st.done(h * 2 + pj)
            for sub in range(WC // 128):
                pss = kb.bank("o")
                for j in range(4):
                    kb.mm(pss.ap[:, 0:256], qTh.ap[:, j, sub * 128:(sub + 1) * 128], kTm.ap[:, h * 4 + j, :], j == 0, j == 3, [qTh, kTm], [pss])
                kb.op("dve", lambda e, o=st4.ap[:, 0:1], i=pss.ap[:, 0:256]: e.reduce_max(o, i, mybir.AxisListType.X), [pss], [st4])
                kb.ts("dve", st4.ap[:, 1:2], st4.ap[:, 0:1], -SCL, None, ALU.mult, None, [st4], [st4])
                kb.act(Pf.ap, pss.ap[:, 0:256], AF.Exp, [pss, st4], [Pf, st4], bias=st4.ap[:, 1:2], scale=SCL, accum=st4.ap[:, 2:3])
                kb.op("dve", lambda e, o=st4.ap[:, 3:4], i=st4.ap[:, 2:3]: e.reciprocal(o, i), [st4], [st4])
                kb.ts("dve", Pb.ap, Pf.ap, st4.ap[:, 3:4], None, ALU.mult, None, [Pf, st4], [Pb])
                pm = kb.bank("m")
                pmb = pm.ap[:, 0:128].bitcast(BF16)
                for mc in range(2):
                    kb.tr(pmb[:, mc * 128:(mc + 1) * 128], Pb.ap[:, mc * 128:(mc + 1) * 128], IDB, [Pb, cstb], [pm])
                kb.cp("act", PT.ap[:, :, sub * 128:(sub + 1) * 128], pmb.rearrange("p (a b) -> p a b", a=2), [pm], [PT])
            for j in range(4):
                po = kb.bank("ab")
                for mc in range(2):
                    kb.mm(po.ap[:, 0:WC], vm.ap[:, mc, (h * 4 + j) * 128:(h * 4 + j + 1) * 128], PT.ap[:, mc, :], mc == 0, mc == 1, [vm, PT], [po])
                kb.cp("act" if j % 2 else "dve", oT.ap[:, h * 4 + j, :], po.ap[:, 0:WC], [po], [oT.ts[h * 4 + j]])
        st = PanelStream(ws8, [("wco", 16, pi * 256, 256) for pi in range(8)])
        for pi in range(8):
            pv, ps_ = st.get(pi)
            for j in range(2):
                dc = pi * 2 + j
                po = kb.bank("o")
                for kc in range(16):
                    kb.mm(po.ap[:, 0:WC], pv[:, kc, j * 128:(j + 1) * 128], oT.ap[:, kc, :], kc == 0, kc == 15, [ps_, oT.ts[kc]], [po])
                kb.tt("dve", xT.ap[:, dc, :], xT.ap[:, dc, :], po.ap[:, 0:WC], ALU.add, [po, xT.ts[dc]], [xT.ts[dc]])
            st.done(pi)
        rmsnorm_fm(xT, hT, ncol(N_FFN2), sqb, rstd, kb.bank("m"))
        ffn(xT, hT, hid, "w1b", "w3b", "w2b", ws8, ws11, sa, ti)
        rmsnorm_fm(xT, yT, ncol(N_FIN), sqb, rstd, kb.bank("m"))
        for sub in range(WC // 128):
            yo = mtok[sub % 2]
            for k4 in range(4):
                pm = kb.bank("m")
                for u in range(4):
                    kc = k4 * 4 + u
                    kb.tr(pm.ap[:, u * 128:(u + 1) * 128], yT.ap[:, kc, sub * 128:(sub + 1) * 128], IDF, [yT.ts[kc], cst], [pm])
                kb.cp("act" if k4 % 2 else "dve", yo.ap[:, k4 * 512:(k4 + 1) * 512], pm.ap, [pm], [yo])
            kb.dma("pool", y_out[t0 + sub * 128:t0 + (sub + 1) * 128, :], yo.ap, [yo], [])
    return finalize(nc, kb, stack)


STOP_AFTER = None
BIS = 0


def finalize(nc, kb, stack):
    kb.barrier()
    with nc.Block() as block:
        @block.tensor
        def _(eng):
            kb.replay("pe", eng)

        @block.scalar
        def _(eng):
            kb.replay("act", eng)

        @block.vector
        def _(eng):
            kb.replay("dve", eng)

        @block.gpsimd
        def _(eng):
            kb.replay("pool", eng)

        @block.sync
        def _(eng):
            kb.replay("sp", eng)
    stack.close()
    return nc


def make_consts():
    p = np.arange(128)[:, None]
    f = np.arange(128)[None, :]
    mats = [p == f, p <= f, p >= f, p > f, p < f, np.ones((128, 128), bool)]
    c = [m.astype(np.float32) for m in mats]
    c += [m.astype(np.float32) * (-1.0 / 16.0) for m in mats[1:5]]
    return np.ascontiguousarray(np.concatenate(c, axis=1))


def colify(v):
    return np.asarray(v, np.float32).reshape(-1, 128).T


def host_inputs(inp, x, mem):
    g = lambda k: np.asarray(inp[k], np.float32)
    cols = np.concatenate(
        [colify(g(k).reshape(-1)) for k in ("ffn1_norm", "mix_norm", "xattn_norm", "mem_norm", "ffn2_norm", "final_norm")]
        + [g("conv_w")[0].T.reshape(24, 128, 5).transpose(1, 0, 2).reshape(128, 120),
           colify(g("conv_b").reshape(-1))], axis=1)
    rows = np.concatenate([
        g("dt_bias_fwd").reshape(-1), g("dt_bias_bwd").reshape(-1),
        g("a_log_fwd").reshape(-1), g("a_log_bwd").reshape(-1),
        g("d_skip").reshape(-1), g("ssd_norm").reshape(-1), g("gla_norm").reshape(-1)])[None, :]
    rows = np.ascontiguousarray(rows, np.float32)
    wg = np.stack([np.concatenate([g("gla_gate_w_fwd")[0], g("gla_gate_b_fwd")], 0),
                   np.concatenate([g("gla_gate_w_bwd")[0], g("gla_gate_b_bwd")], 0)], 0)
    m = {"x": np.ascontiguousarray(x, np.float32), "mem": np.ascontiguousarray(mem, np.float32),
         "consts": make_consts(), "cols": np.ascontiguousarray(cols, np.float32), "rows": rows,
         "wg": np.ascontiguousarray(wg, np.float32)}
    for k in ("ffn1_w1", "ffn1_w3", "ffn1_w2", "w_in", "w_out", "w_cq", "w_ckv", "w_co", "ffn2_w1", "ffn2_w3", "ffn2_w2"):
        m[k] = np.ascontiguousarray(g(k)[0])
    return m


T_CORE = 4096
_CACHE = {}


def tmask_for(T, valid):
    nch = T // 128
    m = (np.arange(nch) * 128 < valid).astype(np.float32)
    return np.ascontiguousarray(np.broadcast_to(m[None, :], (128, nch)))


def lmask_for(c, chain):
    m = np.zeros((128, 32), np.float32)
    i = chain.index(c)
    for r in chain[:i]:
        m[:, r] = 1.0
    for r in chain[i + 1:]:
        m[:, 8 + r] = 1.0
    if i > 0:
        m[:, 16 + chain[i - 1]] = 1.0
    if i + 1 < len(chain):
        m[:, 24 + chain[i + 1]] = 1.0
    return m


def kernel(**inputs):
    inp = {k: np.asarray(v) for k, v in inputs.items()}
    xp, xs_, mp, ms = inp["x_prompt"], inp["x_sample"], inp["mem_prompt"], inp["mem_sample"]
    T = T_CORE
    if T not in _CACHE:
        _CACHE[T] = build(T)
    nc = _CACHE[T]
    in_maps = []
    base = None
    for c in range(8):
        if c < 4:
            x, mem, chain, cl = xp[c], mp[c], [c], c
        else:
            x, mem, chain, cl = xs_[0, (c - 4) * T:(c - 3) * T], ms[0], [0, 1, 2, 3], c - 4
        if base is None:
            base = host_inputs(inp, x, mem)
        m = dict(base)
        m["x"] = np.ascontiguousarray(x, np.float32)
        m["mem"] = np.ascontiguousarray(mem, np.float32)
        m["tmask"] = tmask_for(T, T)
        m["lmask"] = lmask_for(cl, chain)
        in_maps.append(m)
    res = run_bass_kernel_spmd(nc, in_maps, core_ids=list(range(8)))
    ys = [r["y"] for r in res.results]
    y_prompt = np.stack(ys[0:4], 0).astype(np.float32)
    y_sample = np.concatenate(ys[4:8], 0)[None].astype(np.float32)
    return (y_prompt, y_sample)
```

```python
import contextlib
import numpy as np
import concourse.bass as bass
import concourse.mybir as mybir
from concourse.bass_utils import run_bass_kernel_spmd

F32 = mybir.dt.float32
BF16 = mybir.dt.bfloat16
AF = mybir.ActivationFunctionType
ALU = mybir.AluOpType

D = 2048
DFF = 5632
DIN = 11360
NMEM = 256
EPS = 1e-6
OFF_Z, OFF_XBC, OFF_DT, OFF_Q, OFF_K, OFF_V, OFF_LOW, OFF_GO = 0, 2048, 5120, 5184, 6208, 7232, 9280, 9312
ENGS = ("pe", "act", "dve", "pool", "sp")
DEBUG = ()
NRING = 24
NROWS = 64 + 64 + 32 + 2048 + 512


class Trk:
    __slots__ = ("w", "r")

    def __init__(self):
        self.w = None
        self.r = {}


class Tile:
    def __init__(self, ap=None, n=1):
        self.ap = ap
        self.ts = [Trk() for _ in range(n)]


class KB:
    def __init__(self, nc, stack):
        self.nc = nc
        self.q = {e: [] for e in ENGS}
        self.cnt = {e: 0 for e in ENGS}
        self.seen = {e: {} for e in ENGS}
        self.esem = {e: stack.enter_context(nc.semaphore("es_" + e)) for e in ENGS}
        self.ring = {qn: [stack.enter_context(nc.semaphore(f"dq_{qn}{i}")) for i in range(NRING)] for qn in ("sp", "pool", "act")}
        self.ringcnt = {qn: [0] * NRING for qn in self.ring}
        self.ringpos = {qn: 0 for qn in self.ring}
        self.semid = {}
        self.collsem = [stack.enter_context(nc.semaphore(f"coll{i}")) for i in range(12)]
        self.ncoll = 0
        self.colltoks = []
        self.arena = nc.alloc_sbuf_tensor("arena", [128, 52992], F32)
        self.aoff = 0
        self.persist = 0
        self.ps = [Tile(nc.alloc_psum_tensor(f"ps{i}", [128, 512], F32)[:, :]) for i in range(8)]
        self.pp = {}

    def sb(self, shape, dtype, n=1):
        esz = 4 if dtype == F32 else 2
        free = int(np.prod(shape[1:]))
        nbytes = (free * esz + 31) // 32 * 32
        assert self.aoff + nbytes <= 52992 * 4, ("SBUF arena overflow", self.aoff, nbytes)
        ap = self.arena[0:shape[0], self.aoff // 4:(self.aoff + nbytes) // 4]
        if dtype != F32:
            ap = ap.bitcast(dtype)
        ap = ap[:, 0:free]
        if len(shape) == 3:
            ap = ap.rearrange("p (a b) -> p a b", a=shape[1])
        elif len(shape) == 4:
            ap = ap.rearrange("p (a b c) -> p a b c", a=shape[1], b=shape[2])
        self.aoff += nbytes
        return Tile(ap, n)

    def reset(self):
        self.aoff = self.persist

    def bank(self, pool):
        lst, i = self.pp[pool]
        self.pp[pool][1] = (i + 1) % len(lst)
        return self.ps[lst[i]]

    def key(self, sem):
        return id(sem)

    def wait(self, e, tok):
        if tok is None:
            return
        sem, val, src = tok
        if src == "pe" and e == "pe":
            return
        k = id(sem)
        if self.seen[e].get(k, 0) >= val:
            return
        self.seen[e][k] = val
        self.q[e].append(("w", sem, val))

    def _deps(self, e, reads, writes):
        for t in reads:
            self.wait(e, t.w)
        for t in writes:
            if t.w is not None and t.w[2] != e:
                self.wait(e, t.w)
            for src, r in t.r.items():
                if src != e:
                    self.wait(e, r)

    def _mark(self, e, tok, reads, writes):
        for t in reads:
            t.r[tok[2] if tok[2] != "dma" else ("dma", id(tok[0]))] = tok
        for t in writes:
            t.w = tok
            t.r = {}

    @staticmethod
    def _trks(lst):
        out = []
        for x in lst:
            if isinstance(x, Tile):
                out.extend(x.ts)
            elif isinstance(x, (list, tuple)):
                out.extend(x)
            else:
                out.append(x)
        return out

    def op(self, e, fn, reads=(), writes=()):
        reads = self._trks(reads)
        writes = self._trks(writes)
        self._deps(e, reads, writes)
        self.cnt[e] += 1
        tok = (self.esem[e], self.cnt[e], e)
        self.q[e].append(("i", fn, self.esem[e], 1))
        self._mark(e, tok, reads, writes)
        return tok

    def dma(self, qn, out, in_, reads=(), writes=()):
        reads = self._trks(reads)
        writes = self._trks(writes)
        for t in reads:
            self.wait(qn, t.w)
        for t in writes:
            self.wait(qn, t.w)
            for r in t.r.values():
                self.wait(qn, r)
        i = self.ringpos[qn]
        self.ringpos[qn] = (i + 1) % NRING
        sem = self.ring[qn][i]
        c = self.ringcnt[qn][i]
        if c:
            self.wait(qn, (sem, c, "dma"))
        self.ringcnt[qn][i] = c + 16
        tok = (sem, c + 16, "dma")
        self.q[qn].append(("i", (lambda e, o=out, s=in_: e.dma_start(out=o, in_=s)), sem, 16))
        self._mark(qn, tok, reads, writes)
        return tok

    def coll(self, kind, rg, in_ap, out_ap, reads=(), writes=()):
        reads = self._trks(reads)
        writes = self._trks(writes)
        qn = "pool"
        for t in reads:
            self.wait(qn, t.w)
        for t in writes:
            self.wait(qn, t.w)
            for r in t.r.values():
                self.wait(qn, r)
        sem = self.collsem[self.ncoll]
        self.ncoll += 1
        tok = (sem, 1, "dma")
        self.q[qn].append(("i", (lambda e: e.collective_compute(kind, ALU.bypass, replica_groups=rg, ins=[in_ap.opt()], outs=[out_ap.opt()])), sem, 1))
        self._mark(qn, tok, reads, writes)
        self.colltoks.append(tok)
        return tok

    def barrier(self):
        for e in ENGS:
            for e2 in ENGS:
                if e2 != e and self.cnt[e2]:
                    self.wait(e, (self.esem[e2], self.cnt[e2], e2))
            for qn in self.ring:
                for i in range(NRING):
                    if self.ringcnt[qn][i]:
                        self.wait(e, (self.ring[qn][i], self.ringcnt[qn][i], "dma"))
            for tok in self.colltoks:
                self.wait(e, tok)

    def replay(self, e, eng):
        for it in self.q[e]:
            if it[0] == "w":
                eng.wait_ge(it[1], it[2])
            else:
                it[1](eng).then_inc(it[2], it[3])

    def mm(self, out, lhsT, rhs, start, stop, reads, writes):
        return self.op("pe", lambda e: e.matmul(out, lhsT, rhs, start=start, stop=stop), reads, writes)

    def tr(self, out, in_, ident, reads, writes):
        return self.op("pe", lambda e: e.transpose(out, in_, ident), reads, writes)

    def act(self, out, in_, func, reads, writes, bias=None, scale=1.0, accum=None, eng="act"):
        kw = {}
        if bias is not None:
            kw["bias"] = bias
        if accum is not None:
            kw["accum_out"] = accum
        return self.op("act", lambda e: e.activation(out, in_, func, scale=scale, **kw), reads, writes)

    def tt(self, e, out, in0, in1, op, reads, writes):
        return self.op(e, lambda g: g.tensor_tensor(out, in0, in1, op), reads, writes)

    def ts(self, e, out, in0, s1, s2, op0, op1, reads, writes):
        if s2 is None:
            return self.op(e, lambda g: g.tensor_scalar(out, in0, s1, None, op0), reads, writes)
        return self.op(e, lambda g: g.tensor_scalar(out, in0, s1, s2, op0, op1), reads, writes)

    def stt(self, e, out, in0, sc, in1, op0, op1, reads, writes):
        return self.op(e, lambda g: g.scalar_tensor_tensor(out, in0, sc, in1, op0, op1), reads, writes)

    def cp(self, e, out, in_, reads, writes):
        if e == "act":
            return self.op("act", lambda g: g.copy(out, in_), reads, writes)
        return self.op(e, lambda g: g.tensor_copy(out, in_), reads, writes)


def build(T, ncores=8):
    NT = T // 512
    NCH = T // 128
    nc = bass.Bass("TRN2", target_bir_lowering=False)
    stack = contextlib.ExitStack()
    kb = KB(nc, stack)

    def din(name, shape, dt=F32):
        return nc.dram_tensor(name, list(shape), dt, kind="ExternalInput").ap()

    def dscr(name, shape, dt):
        return nc.dram_tensor(name, list(shape), dt, kind=("ExternalOutput" if (DEBUG and name in DEBUG) else "Internal")).ap()

    x_in = din("x", [T, D])
    mem_in = din("mem", [NMEM, D])
    consts_in = din("consts", [128, 10 * 128])
    cols_in = din("cols", [128, 6 * 16 + 24 * 6])
    rows_in = din("rows", [1, NROWS])
    wg_in = din("wg", [2, 17, 1024])
    tmask_in = din("tmask", [128, NCH])
    lmask_in = din("lmask", [128, 32])
    GR = min(ncores, 4)
    RG = [list(range(g0, g0 + GR)) for g0 in range(0, ncores, GR)]
    XF = 2 * 6184
    edge_d = dscr("edge_d", [128, 96], BF16)
    edges_g = dscr("edges_g", [GR * 128, 96], BF16)
    PW = 1546
    xg_dp = [dscr(f"xg_d{k}", [128, PW], F32) for k in range(8)]
    xg_gp = [dscr(f"xg_g{k}", [GR * 128, PW], F32) for k in range(8)]

    def segs(lo, hi):
        out = []
        q = lo
        while q < hi:
            k = q // PW
            a = q - k * PW
            b = min(PW, a + (hi - q))
            out.append((k, a, b, q - lo))
            q += b - a
        return out
    sin_d = dscr("sin_d", [2, 128, 6144], F32)
    wsrc = {
        "w1a": din("ffn1_w1", [D, DFF]), "w3a": din("ffn1_w3", [D, DFF]), "w2a": din("ffn1_w2", [DFF, D]),
        "win": din("w_in", [D, DIN]), "wout": din("w_out", [2 * D, D]),
        "wcq": din("w_cq", [D, D]), "wckv": din("w_ckv", [D, 2 * D]), "wco": din("w_co", [D, D]),
        "w1b": din("ffn2_w1", [D, DFF]), "w3b": din("ffn2_w3", [D, DFF]), "w2b": din("ffn2_w2", [DFF, D]),
    }
    y_out = nc.dram_tensor("y", [T, D], F32, kind="ExternalOutput").ap()

    wb = {k: dscr("wb_" + k, v.shape, BF16) for k, v in wsrc.items()}
    x1T = dscr("x1T", [16, 128, T], F32)
    z_tok = dscr("z_tok", [T, 2048], BF16)
    dt_tok = dscr("dt_tok", [T, 64], F32)
    k_tok = dscr("k_tok", [T, 1024], BF16)
    v_tok = dscr("v_tok", [T, 2048], BF16)
    go_tok = dscr("go_tok", [T, 2048], BF16)
    xbcT = dscr("xbcT", [24, 128, T], BF16)
    qT_d = dscr("qT", [8, 128, T], BF16)
    kT_d = dscr("kT", [8, 128, T], BF16)
    lowT = dscr("lowT", [2, 16, T], F32)
    xs_tok = dscr("xs_tok", [T, 2048], BF16)
    B_tok = dscr("B_tok", [T, 512], BF16)
    BT_d = dscr("BT", [4, 128, T], BF16)
    CT_d = dscr("CT", [4, 128, T], BF16)
    yf_tok = dscr("yf_tok", [T, 4096], F32)
    mix_tok = dscr("mix_tok", [T, 4096], BF16)

    cst = kb.sb([128, 10, 128], F32)
    cstb = kb.sb([128, 10, 128], BF16)
    cols = kb.sb([128, 6 * 16 + 24 * 6], F32)
    epsc = kb.sb([128, 1], F32)
    kb.dma("sp", cst.ap, consts_in.rearrange("p (a b) -> p a b", a=10), [], [cst])
    kb.dma("sp", cols.ap, cols_in, [], [cols])
    kb.cp("dve", cstb.ap, cst.ap, [cst], [cstb])
    kb.op("dve", lambda g: g.memset(epsc.ap, EPS), [], [epsc])
    IDF = cst.ap[:, 0, :]
    IDB = cstb.ap[:, 0, :]
    ONESF = cst.ap[:, 5, :]

    def ncol(i):
        return cols.ap[:, 16 * i:16 * i + 16]
    N_FFN1, N_MIX, N_XA, N_MEM, N_FFN2, N_FIN = range(6)
    convw = cols.ap[:, 96:96 + 120].rearrange("p (c k) -> p c k", c=24)
    convb = cols.ap[:, 216:240]
    kb.persist = kb.aoff

    CW = 4096
    wtmp = [kb.sb([128, CW], F32) for _ in range(2)]
    wtb = [kb.sb([128, CW], BF16) for _ in range(2)]
    it = 0
    cast_engs = ("dve", "pool", "act")
    FIRST = ("w1a", "w3a", "w2a", "win")
    rest_steps = []
    for name, src in wsrc.items():
        K, N = src.shape
        tot = K * N // 128
        s2 = src.rearrange("(p a) n -> p (a n)", p=128)
        d2 = wb[name].rearrange("(p a) n -> p (a n)", p=128)
        if name not in FIRST:
            for c0 in range(0, tot, 1024):
                rest_steps.append((s2, d2, c0, min(1024, tot - c0)))
            continue
        for c0 in range(0, tot, CW):
            cw = min(CW, tot - c0)
            a, b = wtmp[it % 2], wtb[it % 2]
            kb.dma("sp", a.ap[:, 0:cw], s2[:, c0:c0 + cw], [], [a])
            kb.cp(cast_engs[it % 3], b.ap[:, 0:cw], a.ap[:, 0:cw], [a], [b])
            kb.dma("pool", d2[:, c0:c0 + cw], b.ap[:, 0:cw], [b], [])
            it += 1
    kb.barrier()
    kb.reset()
    rest_steps.reverse()
    cvs = {"pending": None, "n": 0, "bufs": None}

    def conv_tick(flush=False):
        if cvs["bufs"] is None:
            return
        new = None
        if rest_steps and not flush:
            s2, d2, c0, cw = rest_steps.pop()
            a, b = cvs["bufs"][cvs["n"] % 2]
            cvs["n"] += 1
            kb.dma("pool", a.ap[:, 0:cw], s2[:, c0:c0 + cw], [], [a])
            new = (a, b, d2, c0, cw)
        p = cvs["pending"]
        if p is not None:
            a, b, d2, c0, cw = p
            kb.cp("pool", b.ap[:, 0:cw], a.ap[:, 0:cw], [a], [b])
            kb.dma("pool", d2[:, c0:c0 + cw], b.ap[:, 0:cw], [b], [])
        cvs["pending"] = new

    def rmsnorm_fm(xT, hT, ncols, sqb, rstd, pbank):
        W = xT.ap.shape[2]
        for kc in range(16):
            s = sqb[kc % 2]
            kb.act(s.ap[:, 0:W], xT.ap[:, kc, :], AF.Square, [xT.ts[kc]], [s])
            kb.mm(pbank.ap[:, 0:W], ONESF, s.ap[:, 0:W], kc == 0, kc == 15, [s, cst], [pbank])
        kb.act(rstd.ap[:, 0:W], pbank.ap[:, 0:W], AF.Sqrt, [pbank, epsc], [rstd], bias=epsc.ap, scale=1.0 / D)
        kb.op("dve", lambda g: g.reciprocal(rstd.ap[:, 0:W], rstd.ap[:, 0:W]), [rstd], [rstd])
        for kc in range(16):
            kb.stt("dve", hT.ap[:, kc, :], xT.ap[:, kc, :], ncols[:, kc:kc + 1], rstd.ap[:, 0:W], ALU.mult, ALU.mult,
                   [xT.ts[kc], rstd, cols], [hT.ts[kc]])

    def load_panel(wname, KC, c0, w, slot):
        src = wb[wname].rearrange("(kc p) n -> p kc n", p=128)[:, :, c0:c0 + w]
        flat = slot.ap.rearrange("p a b -> p (a b)") if len(slot.ap.shape) == 3 else slot.ap
        dst = flat[:, 0:KC * w].rearrange("p (a b) -> p a b", a=KC)
        kb.dma("sp", dst, src, [], [slot])
        return dst

    class PanelStream:
        def __init__(self, slots, reqs):
            self.slots = slots
            self.reqs = reqs
            self.views = {}
            for i in range(min(len(slots), len(reqs))):
                self._issue(i)

        def _issue(self, i):
            slot = self.slots[i % len(self.slots)]
            self.views[i] = (load_panel(*self.reqs[i], slot), slot)
            conv_tick()

        def get(self, i):
            return self.views[i]

        def done(self, i):
            del self.views[i]
            j = i + len(self.slots)
            if j < len(self.reqs):
                self._issue(j)

    def ffn(xT, hT, hid, w1, w3, w2, ws8, ws11, sa, tile_tag):
        W = xT.ap.shape[2]
        reqs = []
        for fp in range(DFF // 256):
            reqs.append((w1, 16, fp * 256, 256))
            reqs.append((w3, 16, fp * 256, 256))
        st = PanelStream(ws8, reqs)
        for fp in range(DFF // 256):
            p1, s1 = st.get(2 * fp)
            p3, s3 = st.get(2 * fp + 1)
            for j in range(2):
                fc = fp * 2 + j
                pa = kb.bank("ab")
                pb = kb.bank("ab")
                for kc in range(16):
                    kb.mm(pa.ap[:, 0:W], p1[:, kc, j * 128:(j + 1) * 128], hT.ap[:, kc, :], kc == 0, kc == 15, [s1, hT.ts[kc]], [pa])
                for kc in range(16):
                    kb.mm(pb.ap[:, 0:W], p3[:, kc, j * 128:(j + 1) * 128], hT.ap[:, kc, :], kc == 0, kc == 15, [s3, hT.ts[kc]], [pb])
                s = sa[fc % 2]
                kb.act(s.ap[:, 0:W], pa.ap[:, 0:W], AF.Silu, [pa], [s])
                kb.tt("dve", hid.ap[:, fc, :], s.ap[:, 0:W], pb.ap[:, 0:W], ALU.mult, [s, pb], [hid.ts[fc]])
            st.done(2 * fp)
            st.done(2 * fp + 1)
        reqs2 = [(w2, 44, dc * 128, 128) for dc in range(16)]
        st2 = PanelStream(ws11, reqs2)
        for dc in range(16):
            p2, s2 = st2.get(dc)
            po = kb.bank("o")
            for fc in range(44):
                kb.mm(po.ap[:, 0:W], p2[:, fc, :], hid.ap[:, fc, :], fc == 0, fc == 43, [s2, hid.ts[fc]], [po])
            kb.stt("dve", xT.ap[:, dc, :], po.ap[:, 0:W], 0.5, xT.ap[:, dc, :], ALU.mult, ALU.add, [po, xT.ts[dc]], [xT.ts[dc]])
            st2.done(dc)

    kb.pp = {"ab": [[0, 1, 2, 3], 0], "o": [[4, 5], 0], "m": [[6, 7], 0]}
    xtok = [kb.sb([128, 2048], F32) for _ in range(2)]
    xT = kb.sb([128, 16, 512], F32, n=16)
    hT = kb.sb([128, 16, 512], BF16, n=16)
    hid = kb.sb([128, 44, 512], BF16, n=44)
    ws8 = [kb.sb([128, 16, 256], BF16) for _ in range(4)]
    ws11 = [kb.sb([128, 44, 128], BF16) for _ in range(2)]
    sqb = [kb.sb([128, 512], F32) for _ in range(2)]
    rstd = kb.sb([128, 512], F32)
    sa = [kb.sb([128, 512], F32) for _ in range(2)]
    ev = [kb.sb([128, 512], BF16) for _ in range(4)]
    evf = [kb.sb([128, 512], F32) for _ in range(2)]
    cvs["bufs"] = [(kb.sb([128, 1024], F32), kb.sb([128, 1024], BF16)) for _ in range(2)]
    evi = [0]

    def nev():
        evi[0] += 1
        return ev[evi[0] % 4]

    for ti in range(NT):
        t0 = ti * 512
        for sub in range(4):
            xt = xtok[sub % 2]
            kb.dma("sp", xt.ap, x_in[t0 + sub * 128:t0 + (sub + 1) * 128, :], [], [xt])
            for k4 in range(4):
                pm = kb.bank("m")
                for u in range(4):
                    kc = k4 * 4 + u
                    kb.tr(pm.ap[:, u * 128:(u + 1) * 128], xt.ap[:, kc * 128:(kc + 1) * 128], IDF, [xt, cst], [pm])
                e = "dve" if k4 % 2 == 0 else "act"
                kb.cp(e, xT.ap[:, k4 * 4:k4 * 4 + 4, sub * 128:(sub + 1) * 128],
                      pm.ap.rearrange("p (a b) -> p a b", a=4), [pm], xT.ts[k4 * 4:k4 * 4 + 4])
        rmsnorm_fm(xT, hT, ncol(N_FFN1), sqb, rstd, kb.bank("m"))
        ffn(xT, hT, hid, "w1a", "w3a", "w2a", ws8, ws11, sa, ti)
        kb.dma("pool", x1T[:, :, t0:t0 + 512].rearrange("k p t -> p k t"), xT.ap, [xT], [])
        rmsnorm_fm(xT, hT, ncol(N_MIX), sqb, rstd, kb.bank("m"))
        tm_jobs = [(z_tok, OFF_Z, 2048), (k_tok, OFF_K, 1024), (v_tok, OFF_V, 2048), (go_tok, OFF_GO, 2048)]
        fm_jobs = [(xbcT, OFF_XBC, 24, 1.0), (qT_d, OFF_Q, 8, 0.0625), (kT_d, OFF_K, 8, 1.0)]
        reqs = []
        for dst, c0, ncs in tm_jobs:
            for pi in range(ncs // 256):
                reqs.append(("win", 16, c0 + pi * 256, 256))
        for dst, c0, nch, sc in fm_jobs:
            for pi in range(nch // 2):
                reqs.append(("win", 16, c0 + pi * 256, 256))
        reqs.append(("win", 16, OFF_DT, 64))
        reqs.append(("win", 16, OFF_LOW, 32))
        st = PanelStream(ws8, reqs)
        ri = 0
        for dst, c0, ncs in tm_jobs:
            for pi in range(ncs // 256):
                pv, ps_ = st.get(ri)
                ri += 1
                for sub in range(4):
                    po = kb.bank("ab")
                    for kc in range(16):
                        kb.mm(po.ap[:, 0:256], hT.ap[:, kc, sub * 128:(sub + 1) * 128], pv[:, kc, :], kc == 0, kc == 15,
                              [ps_, hT.ts[kc]], [po])
                    o = nev()
                    kb.cp("act" if sub % 2 else "dve", o.ap[:, 0:256], po.ap[:, 0:256], [po], [o])
                    kb.dma("pool", dst[t0 + sub * 128:t0 + (sub + 1) * 128, pi * 256:(pi + 1) * 256], o.ap[:, 0:256], [o], [])
                st.done(ri - 1)
        for dst, c0, nch, sc in fm_jobs:
            for pi in range(nch // 2):
                pv, ps_ = st.get(ri)
                ri += 1
                for j in range(2):
                    po = kb.bank("ab")
                    for kc in range(16):
                        kb.mm(po.ap, pv[:, kc, j * 128:(j + 1) * 128], hT.ap[:, kc, :], kc == 0, kc == 15, [ps_, hT.ts[kc]], [po])
                    o = nev()
                    if sc == 1.0:
                        kb.cp("act" if j % 2 else "dve", o.ap, po.ap, [po], [o])
                    else:
                        kb.ts("dve", o.ap, po.ap, sc, None, ALU.mult, None, [po], [o])
                    kb.dma("pool", dst[pi * 2 + j, :, t0:t0 + 512], o.ap, [o], [])
                st.done(ri - 1)
        pv, ps_ = st.get(ri)
        ri += 1
        for sub in range(4):
            po = kb.bank("ab")
            for kc in range(16):
                kb.mm(po.ap[:, 0:64], hT.ap[:, kc, sub * 128:(sub + 1) * 128], pv[:, kc, :], kc == 0, kc == 15, [ps_, hT.ts[kc]], [po])
            o = evf[sub % 2]
            kb.cp("dve", o.ap[:, 0:64], po.ap[:, 0:64], [po], [o])
            kb.dma("pool", dt_tok[t0 + sub * 128:t0 + (sub + 1) * 128, :], o.ap[:, 0:64], [o], [])
        st.done(ri - 1)
        pv, ps_ = st.get(ri)
        ri += 1
        for j in range(2):
            po = kb.bank("ab")
            for kc in range(16):
                kb.mm(po.ap[0:16, :], pv[:, kc, j * 16:(j + 1) * 16], hT.ap[:, kc, :], kc == 0, kc == 15, [ps_, hT.ts[kc]], [po])
            o = evf[j % 2]
            kb.cp("dve", o.ap[0:16, :], po.ap[0:16, :], [po], [o])
            kb.dma("pool", lowT[j, :, t0:t0 + 512], o.ap[0:16, :], [o], [])
    while rest_steps or cvs["pending"] is not None:
        conv_tick()
    cvs["bufs"] = None
    kb.barrier()
    kb.reset()
    if STOP_AFTER == "A":
        return finalize(nc, kb, stack)

    kb.pp = {"a": [[0, 1], 0], "b": [[2, 3], 0], "c": [[4, 5], 0], "d": [[6, 7], 0]}
    lmk = kb.sb([128, 32], F32)
    kb.dma("sp", lmk.ap, lmask_in, [], [lmk])
    xsrc = xbcT.rearrange("c p t -> p c t")
    agE = Tile(None)
    e0 = kb.sb([128, 24, 2], BF16)
    e1 = kb.sb([128, 24, 2], BF16)
    with nc.allow_non_contiguous_dma(reason="tiny conv halo"):
        pass
    kb.dma("sp", e0.ap, xsrc[:, :, 0:2], [], [e0])
    kb.dma("sp", e1.ap, xsrc[:, :, T - 2:T], [], [e1])
    kb.dma("sp", edge_d[:, 0:48], e0.ap.rearrange("p a b -> p (a b)"), [e0], [agE])
    kb.dma("sp", edge_d[:, 48:96], e1.ap.rearrange("p a b -> p (a b)"), [e1], [agE])
    kb.coll("AllGather", RG, edge_d[:, :], edges_g[:, :], [agE], [agE])
    eg = kb.sb([128, GR, 96], BF16)
    kb.dma("sp", eg.ap, edges_g.rearrange("(r p) f -> p r f", p=128), [agE], [eg])
    hl = kb.sb([128, 48], F32)
    hr = kb.sb([128, 48], F32)
    kb.op("dve", lambda g: g.memset(hl.ap, 0.0), [], [hl])
    kb.op("dve", lambda g: g.memset(hr.ap, 0.0), [], [hr])
    for r in range(GR):
        kb.stt("dve", hl.ap, eg.ap[:, r, 48:96], lmk.ap[:, 16 + r:17 + r], hl.ap, ALU.mult, ALU.add, [eg, lmk, hl], [hl])
        kb.stt("dve", hr.ap, eg.ap[:, r, 0:48], lmk.ap[:, 24 + r:25 + r], hr.ap, ALU.mult, ALU.add, [eg, lmk, hr], [hr])
    xh = [kb.sb([128, 24, 516], BF16) for _ in range(2)]
    cacc = [kb.sb([128, 512], F32) for _ in range(3)]
    xc = kb.sb([128, 24, 512], BF16, n=24)
    tko = [kb.sb([128, 2048], BF16) for _ in range(2)]
    tkb = [kb.sb([128, 512], BF16) for _ in range(2)]
    for ti in range(NT):
        t0 = ti * 512
        xt = xh[ti % 2]
        src = xbcT.rearrange("c p t -> p c t")
        lo = 2 if ti == 0 else 0
        hi = 514 if ti == NT - 1 else 516
        if ti == 0:
            kb.cp("dve", xt.ap[:, :, 0:2], hl.ap.rearrange("p (a b) -> p a b", a=24), [hl], [xt])
        if ti == NT - 1:
            kb.cp("dve", xt.ap[:, :, 514:516], hr.ap.rearrange("p (a b) -> p a b", a=24), [hr], [xt])
        kb.dma("sp", xt.ap[:, :, lo:hi], src[:, :, t0 - 2 + lo:t0 - 2 + hi], [], [xt])
        for c in range(24):
            acc = cacc[c % 3]
            e = "dve"
            kb.ts(e, acc.ap, xt.ap[:, c, 0:512], convw[:, c, 0:1], convb[:, c:c + 1], ALU.mult, ALU.add, [xt, cols], [acc])
            for k in range(1, 5):
                kb.stt(e, acc.ap, xt.ap[:, c, k:k + 512], convw[:, c, k:k + 1], acc.ap, ALU.mult, ALU.add, [xt, cols, acc], [acc])
            kb.act(xc.ap[:, c, :], acc.ap, AF.Silu, [acc], [xc.ts[c]])
        for sub in range(4):
            o = tko[sub % 2]
            for c4 in range(4):
                pm = kb.bank("a" if c4 % 2 else "b")
                pmb = pm.ap[:, 0:256].bitcast(BF16)
                for u in range(4):
                    c = c4 * 4 + u
                    kb.tr(pmb[:, u * 128:(u + 1) * 128], xc.ap[:, c, sub * 128:(sub + 1) * 128], IDB, [xc.ts[c], cstb], [pm])
                kb.cp("act" if c4 % 2 else "dve", o.ap[:, c4 * 512:(c4 + 1) * 512], pmb, [pm], [o])
            kb.dma("pool", xs_tok[t0 + sub * 128:t0 + (sub + 1) * 128, :], o.ap, [o], [])
            ob = tkb[sub % 2]
            pm = kb.bank("c")
            pmb = pm.ap[:, 0:256].bitcast(BF16)
            for u in range(4):
                kb.tr(pmb[:, u * 128:(u + 1) * 128], xc.ap[:, 16 + u, sub * 128:(sub + 1) * 128], IDB, [xc.ts[16 + u], cstb], [pm])
            kb.cp("dve", ob.ap, pmb, [pm], [ob])
            kb.dma("pool", B_tok[t0 + sub * 128:t0 + (sub + 1) * 128, :], ob.ap, [ob], [])
        kb.dma("pool", BT_d[:, :, t0:t0 + 512].rearrange("c p t -> p c t"), xc.ap[:, 16:20, :], [xc.ts[16:20]], [])
        kb.dma("pool", CT_d[:, :, t0:t0 + 512].rearrange("c p t -> p c t"), xc.ap[:, 20:24, :], [xc.ts[20:24]], [])
    kb.barrier()
    kb.reset()

    if STOP_AFTER == "B0":
        return finalize(nc, kb, stack)
    rows = kb.sb([128, NROWS], F32)
    kb.dma("sp", rows.ap, rows_in.partition_broadcast(128).rearrange("p a b -> p (a b)"), [], [rows])
    arow = kb.sb([128, 64], F32)
    kb.act(arow.ap, rows.ap[:, 64:128], AF.Exp, [rows], [arow])
    kb.ts("dve", arow.ap, arow.ap, -1.0, None, ALU.mult, None, [arow], [arow])
    tmk = kb.sb([128, NCH], F32)
    kb.dma("sp", tmk.ap, tmask_in, [], [tmk])
    onec = kb.sb([128, 1], F32)
    kb.op("dve", lambda g: g.memset(onec.ap, 1.0), [], [onec])
    R_DSK, R_SSDN, R_GLAN = 128, 160, 160 + 2048
    wgt = kb.sb([17, 1024], F32)
    lowa = [kb.sb([17, 64], F32) for _ in range(2)]
    for l in lowa:
        kb.op("dve", lambda g, a=l.ap: g.memset(a, 1.0), [], [l])
    lmk = kb.sb([128, 32], F32)
    kb.dma("sp", lmk.ap, lmask_in, [], [lmk])
    Dt_s = kb.sb([128, 32], F32)
    Dt_g = kb.sb([128, 8], F32)
    aoff_keep = kb.aoff
    S_ssd = kb.sb([128, 4, 512], F32, n=4)
    S_ssdb = kb.sb([128, 4, 512], BF16, n=4)
    S_gla = kb.sb([128, 8, 512], F32, n=8)
    S_glab = kb.sb([128, 8, 512], BF16, n=8)
    db = lambda shape, dt: [kb.sb(shape, dt) for _ in range(2)]
    xs_b, Bt_b, BT_b, CT_b, dtr_b = db([128, 2048], BF16), db([128, 512], BF16), db([128, 4, 128], BF16), db([128, 4, 128], BF16), db([128, 64], F32)
    kt_b, vt_b, qT_b, kT_b = db([64, 1024], BF16), db([64, 2048], BF16), db([128, 8, 64], BF16), db([128, 8, 64], BF16)
    dts = kb.sb([128, 32], F32)
    av = kb.sb([128, 32], F32)
    acs = kb.sb([128, 64], F32)
    dd = kb.sb([128, 32], F32)
    ea = kb.sb([128, 32], F32)
    dec = kb.sb([128, 32], F32)
    xes = kb.sb([128, 32], F32)
    xdt = kb.sb([128, 2048], BF16)
    xend = kb.sb([128, 2048], BF16)
    LmL = [kb.sb([128, 8, 128], F32) for _ in range(2)]
    EmL = [kb.sb([128, 8, 128], F32) for _ in range(2)]
    MhL = [kb.sb([128, 8, 128], BF16) for _ in range(2)]
    cbmL = [kb.sb([128, 128], F32) for _ in range(2)]
    ysb = kb.sb([128, 2048], F32, n=4)
    ytmpL = [kb.sb([128, 512], F32) for _ in range(2)]
    spL = [kb.sb([64, 1024], F32) for _ in range(2)]
    EGL = [kb.sb([128, 8, 64], F32) for _ in range(2)]
    GsL = [kb.sb([128, 8, 64], F32) for _ in range(2)]
    E1 = kb.sb([128, 8, 64], F32)
    Gm = kb.sb([128, 8, 64], F32)
    E2 = kb.sb([128, 8, 64], F32)
    qp = kb.sb([128, 8, 64], BF16)
    kp = kb.sb([128, 8, 64], BF16)
    qin = kb.sb([128, 8, 64], BF16)
    Ek = kb.sb([64, 1024], F32)
    kend = kb.sb([64, 1024], BF16)
    Am = kb.sb([64, 4, 64], BF16)
    osb = kb.sb([64, 2048], F32, n=4)
    yfb = kb.sb([128, 2048], F32)
    zb = kb.sb([128, 2048], BF16)
    szb = kb.sb([128, 2048], F32)
    mixb = db([128, 2048], BF16)
    ssq = kb.sb([128, 8], F32)
    junk = kb.sb([128, 512], F32)

    def bc(ap2, n):
        return ap2.unsqueeze(2).to_broadcast([ap2.shape[0], ap2.shape[1], n])

    def scan_pass(dirn, full, final, init=False, save=False):
        fwd = dirn == 0
        M_INC = 1 if fwd else 2
        M_STR = 3 if fwd else 4
        G_INC = 6 if fwd else 7
        G_STR = 8 if fwd else 9
        mid = 31 if fwd else 32
        last = 63 if fwd else 0
        kb.dma("sp", wgt.ap, wg_in[dirn], [], [wgt])
        if init:
            kb.dma("sp", S_ssd.ap.rearrange("p a b -> p (a b)"), sin_d[dirn, :, 0:2048], [], [S_ssd])
            kb.dma("sp", S_gla.ap.rearrange("p a b -> p (a b)"), sin_d[dirn, :, 2048:6144], [], [S_gla])
            for g in range(4):
                kb.cp("pool", S_ssdb.ap[:, g, :], S_ssd.ap[:, g, :], [S_ssd.ts[g]], [S_ssdb.ts[g]])
            for g in range(8):
                kb.cp("pool", S_glab.ap[:, g, :], S_gla.ap[:, g, :], [S_gla.ts[g]], [S_glab.ts[g]])
        else:
            for g in range(4):
                kb.op("pool", lambda e, a=S_ssd.ap[:, g, :]: e.memset(a, 0.0), [], [S_ssd.ts[g]])
                kb.op("pool", lambda e, a=S_ssdb.ap[:, g, :]: e.memset(a, 0.0), [], [S_ssdb.ts[g]])
            for g in range(8):
                kb.op("pool", lambda e, a=S_gla.ap[:, g, :]: e.memset(a, 0.0), [], [S_gla.ts[g]])
                kb.op("pool", lambda e, a=S_glab.ap[:, g, :]: e.memset(a, 0.0), [], [S_glab.ts[g]])
        if save:
            kb.op("dve", lambda e: e.memset(Dt_s.ap, 1.0), [], [Dt_s])
            kb.op("dve", lambda e: e.memset(Dt_g.ap, 1.0), [], [Dt_g])
        order = list(range(NCH)) if fwd else list(range(NCH - 1, -1, -1))
        for n_, ci in enumerate(order):
            t0 = ci * 128
            pb_ = n_ % 2
            xs, Bt, BT, CT, dtr = xs_b[pb_], Bt_b[pb_], BT_b[pb_], CT_b[pb_], dtr_b[pb_]
            kb.dma("sp", xs.ap, xs_tok[t0:t0 + 128, :], [], [xs])
            kb.dma("sp", Bt.ap, B_tok[t0:t0 + 128, :], [], [Bt])
            kb.dma("sp", dtr.ap, dt_tok[t0:t0 + 128, :], [], [dtr])
            if full:
                kb.dma("sp", BT.ap, BT_d[:, :, t0:t0 + 128].rearrange("c p t -> p c t"), [], [BT])
                kb.dma("sp", CT.ap, CT_d[:, :, t0:t0 + 128].rearrange("c p t -> p c t"), [], [CT])
            c0 = 32 * dirn
            kb.tt("dve", dts.ap, dtr.ap[:, c0:c0 + 32], rows.ap[:, c0:c0 + 32], ALU.add, [dtr, rows], [dts])
            kb.act(dts.ap, dts.ap, AF.Exp, [dts], [dts])
            kb.act(dts.ap, dts.ap, AF.Ln, [dts, onec], [dts], bias=onec.ap)
            kb.tt("dve", av.ap, dts.ap, arow.ap[:, c0:c0 + 32], ALU.mult, [dts, arow], [av])
            pm = kb.bank("a")
            kb.mm(pm.ap[:, 0:32], cst.ap[:, M_INC, :], av.ap, True, True, [cst, av], [pm])
            kb.mm(pm.ap[:, 32:64], ONESF, av.ap, True, True, [cst, av], [pm])
            kb.cp("dve", acs.ap, pm.ap[:, 0:64], [pm], [acs])
            kb.tt("dve", dd.ap, acs.ap[:, 32:64], acs.ap[:, 0:32], ALU.subtract, [acs], [dd])
            kb.act(xes.ap, dd.ap, AF.Exp, [dd], [xes])
            kb.act(ea.ap, acs.ap[:, 0:32], AF.Exp, [acs], [ea])
            kb.act(dec.ap, acs.ap[:, 32:64], AF.Exp, [acs], [dec])
            if save:
                kb.tt("dve", Dt_s.ap, Dt_s.ap, dec.ap, ALU.mult, [Dt_s, dec], [Dt_s])
            kb.ts("dve", dts.ap, dts.ap, tmk.ap[:, ci:ci + 1], None, ALU.mult, None, [dts, tmk], [dts])
            kb.tt("dve", xes.ap, xes.ap, dts.ap, ALU.mult, [xes, dts], [xes])
            x3 = xs.ap.rearrange("p (h d) -> p h d", h=32)
            kb.tt("dve", xend.ap.rearrange("p (h d) -> p h d", h=32), x3, bc(xes.ap, 64), ALU.mult, [xs, xes], [xend])
            if full:
                kb.tt("pool", xdt.ap.rearrange("p (h d) -> p h d", h=32), x3, bc(dts.ap, 64), ALU.mult, [xs, dts], [xdt])
            for g in range(4):
                gs = slice(g * 512, (g + 1) * 512)
                Lm, Em, Mh, cbm, ytmp = LmL[g % 2], EmL[g % 2], MhL[g % 2], cbmL[g % 2], ytmpL[g % 2]
                if full and BIS != 4:
                    pyo = kb.bank("b")
                    kb.mm(pyo.ap, CT.ap[:, g, :], S_ssdb.ap[:, g, :], True, True, [CT, S_ssdb.ts[g]], [pyo])
                    pcb = kb.bank("a")
                    kb.mm(pcb.ap[:, 0:128], BT.ap[:, g, :], CT.ap[:, g, :], True, True, [BT, CT], [pcb])
                    kb.tt("dve", cbm.ap, pcb.ap[:, 0:128], cst.ap[:, M_INC, :], ALU.mult, [pcb, cst], [cbm])
                    kb.tt("pool", Lm.ap, cst.ap[:, M_STR:M_STR + 1, :].to_broadcast([128, 8, 128]),
                          bc(av.ap[:, g * 8:(g + 1) * 8], 128), ALU.mult, [cst, av], [Lm])
                    pys = kb.bank("d")
                    for hh in range(2):
                        psg = kb.bank("c")
                        for u in range(4):
                            kb.mm(psg.ap[:, u * 128:(u + 1) * 128], Lm.ap[:, hh * 4 + u, :], cst.ap[:, M_INC, :], True, True, [Lm, cst], [psg])
                        kb.act(Em.ap[:, hh * 4:hh * 4 + 4, :], psg.ap.rearrange("p (a b) -> p a b", a=4), AF.Exp, [psg], [Em])
                    kb.tt("dve", Mh.ap, Em.ap, cbm.ap.unsqueeze(1).to_broadcast([128, 8, 128]), ALU.mult, [Em, cbm], [Mh])
                    for h in range(8):
                        hd = (g * 8 + h) * 64
                        kb.mm(pys.ap[:, h * 64:(h + 1) * 64], Mh.ap[:, h, :], xdt.ap[:, hd:hd + 64], True, True, [Mh, xdt], [pys])
                    kb.tt("dve", ytmp.ap.rearrange("p (h d) -> p h d", h=8), pyo.ap.rearrange("p (h d) -> p h d", h=8),
                          bc(ea.ap[:, g * 8:(g + 1) * 8], 64), ALU.mult, [pyo, ea], [ytmp])
                    kb.tt("dve", ysb.ap[:, gs], ytmp.ap, pys.ap, ALU.add, [ytmp, pys], [ysb.ts[g]])
                pu = kb.bank("b")
                kb.mm(pu.ap, Bt.ap[:, g * 128:(g + 1) * 128], xend.ap[:, gs], True, True, [Bt, xend], [pu])
                S3 = S_ssd.ap[:, g, :].rearrange("p (h d) -> p h d", h=8)
                kb.tt("pool", S3, S3, bc(dec.ap[:, g * 8:(g + 1) * 8], 64), ALU.mult, [S_ssd.ts[g], dec], [S_ssd.ts[g]])
                kb.tt("dve", S_ssd.ap[:, g, :], S_ssd.ap[:, g, :], pu.ap, ALU.add, [S_ssd.ts[g], pu], [S_ssd.ts[g]])
                kb.cp("act", S_ssdb.ap[:, g, :], S_ssd.ap[:, g, :], [S_ssd.ts[g]], [S_ssdb.ts[g]])
            if full and not final:
                kb.dma("pool", yf_tok[t0:t0 + 128, 0:2048], ysb.ap, [ysb], [])
            if final:
                kb.dma("sp", yfb.ap, yf_tok[t0:t0 + 128, 0:2048], [], [yfb])
                kb.dma("sp", zb.ap, z_tok[t0:t0 + 128, :], [], [zb])
                kb.tt("dve", ysb.ap, ysb.ap, yfb.ap, ALU.add, [ysb, yfb], [ysb])
                kb.tt("pool", yfb.ap.rearrange("p (h d) -> p h d", h=32), x3, bc(rows.ap[:, R_DSK:R_DSK + 32], 64), ALU.mult,
                      [xs, rows], [yfb])
                kb.tt("dve", ysb.ap, ysb.ap, yfb.ap, ALU.add, [ysb, yfb], [ysb])
                kb.act(szb.ap, zb.ap, AF.Silu, [zb], [szb])
                kb.tt("dve", ysb.ap, ysb.ap, szb.ap, ALU.mult, [ysb, szb], [ysb])
                for g in range(4):
                    kb.act(junk.ap, ysb.ap[:, g * 512:(g + 1) * 512], AF.Square, [ysb], [junk, ssq], accum=ssq.ap[:, g:g + 1])
                kb.act(ssq.ap[:, 0:4], ssq.ap[:, 0:4], AF.Sqrt, [ssq, epsc], [ssq], bias=epsc.ap, scale=1.0 / 512)
                kb.op("dve", lambda e: e.reciprocal(ssq.ap[:, 0:4], ssq.ap[:, 0:4]), [ssq], [ssq])
                mo = mixb[n_ % 2]
                for g in range(4):
                    kb.stt("dve", mo.ap[:, g * 512:(g + 1) * 512], ysb.ap[:, g * 512:(g + 1) * 512], ssq.ap[:, g:g + 1],
                           rows.ap[:, R_SSDN + g * 512:R_SSDN + (g + 1) * 512], ALU.mult, ALU.mult, [ysb, ssq, rows], [mo])
                kb.dma("pool", mix_tok[t0:t0 + 128, 0:2048], mo.ap, [mo], [])
            for sn, sub in enumerate((0, 1) if fwd else (1, 0)):
                s0 = t0 + sub * 64
                qb_ = (n_ * 2 + sn) % 2
                kt, vt, qT, kT, la = kt_b[qb_], vt_b[qb_], qT_b[qb_], kT_b[qb_], lowa[qb_]
                sp_, EG, Gs = spL[qb_], EGL[qb_], GsL[qb_]
                kb.dma("sp", kt.ap, k_tok[s0:s0 + 64, :], [], [kt])
                kb.dma("sp", vt.ap, v_tok[s0:s0 + 64, :], [], [vt])
                kb.dma("sp", la.ap[0:16, :], lowT[dirn, :, s0:s0 + 64], [], [la])
                if full:
                    kb.dma("sp", qT.ap, qT_d[:, :, s0:s0 + 64].rearrange("c p t -> p c t"), [], [qT])
                    kb.dma("sp", kT.ap, kT_d[:, :, s0:s0 + 64].rearrange("c p t -> p c t"), [], [kT])
                for hf in range(2):
                    pg = kb.bank("c")
                    kb.mm(pg.ap[0:64, :], la.ap[0:17, :], wgt.ap[0:17, hf * 512:(hf + 1) * 512], True, True, [la, wgt], [pg])
                    kb.act(sp_.ap[:, hf * 512:(hf + 1) * 512], pg.ap[0:64, :], AF.Exp, [pg], [sp_], scale=-1.0)
                kb.act(sp_.ap, sp_.ap, AF.Ln, [sp_, onec], [sp_], bias=onec.ap[0:64, :])
                pG = kb.bank("a")
                for kc in range(8):
                    kb.mm(pG.ap[:, kc * 64:(kc + 1) * 64], sp_.ap[0:64, kc * 128:(kc + 1) * 128], cst.ap[0:64, G_INC, 0:64], True, True,
                          [sp_, cst], [pG])
                pG3 = pG.ap.rearrange("p (a b) -> p a b", a=8)
                kb.cp("dve", Gs.ap, pG3, [pG], [Gs])
                kb.act(EG.ap, Gs.ap, AF.Exp, [Gs], [EG])
                if save:
                    kb.tt("dve", Dt_g.ap, Dt_g.ap, EG.ap[:, :, last], ALU.mult, [Dt_g, EG], [Dt_g])
                for hf in range(2):
                    pD = kb.bank("d")
                    kb.mm(pD.ap[0:64, :], cst.ap[0:64, G_STR, 0:64], sp_.ap[0:64, hf * 512:(hf + 1) * 512], True, True, [sp_, cst], [pD])
                    kb.act(Ek.ap[:, hf * 512:(hf + 1) * 512], pD.ap[0:64, :], AF.Exp, [pD], [Ek])
                kb.tt("pool", kend.ap, kt.ap, Ek.ap, ALU.mult, [kt, Ek], [kend])
                if full and BIS != 3:
                    kb.tt("dve", Gm.ap, Gs.ap, Gs.ap[:, :, mid:mid + 1].to_broadcast([128, 8, 64]), ALU.subtract, [Gs], [Gm])
                    kb.act(E1.ap, Gm.ap, AF.Exp, [Gm], [E1])
                    kb.act(E2.ap, Gm.ap, AF.Exp, [Gm], [E2], scale=-1.0)
                    kb.tt("dve", qp.ap, qT.ap, E1.ap, ALU.mult, [qT, E1], [qp])
                    kb.tt("pool", kp.ap, kT.ap, E2.ap, ALU.mult, [kT, E2], [kp])
                    kb.tt("dve", qin.ap, qT.ap, EG.ap, ALU.mult, [qT, EG], [qin])
                    pA = kb.bank("a")
                    for h in range(4):
                        for kk in range(2):
                            kb.mm(pA.ap[0:64, h * 64:(h + 1) * 64], kp.ap[:, 2 * h + kk, :], qp.ap[:, 2 * h + kk, :], kk == 0, kk == 1,
                                  [kp, qp], [pA])
                    kb.tt("dve", Am.ap, pA.ap[0:64, 0:256].rearrange("p (a b) -> p a b", a=4),
                          cst.ap[0:64, M_INC:M_INC + 1, 0:64].to_broadcast([64, 4, 64]), ALU.mult, [pA, cst], [Am])
                    for h in range(4 if BIS != 5 else 0):
                        po = kb.bank("b")
                        kb.mm(po.ap[0:64, :], Am.ap[:, h, :], vt.ap[:, h * 512:(h + 1) * 512], True, False, [Am, vt], [po])
                        for kk in range(2):
                            kb.mm(po.ap[0:64, :], qin.ap[:, 2 * h + kk, :], S_glab.ap[:, 2 * h + kk, :], False, kk == 1,
                                  [qin, S_glab.ts[2 * h + kk]], [po])
                        kb.cp("act", osb.ap[:, h * 512:(h + 1) * 512], po.ap[0:64, :], [po], [osb.ts[h]])
                for h in range(4):
                    for kk in range(2):
                        j = 2 * h + kk
                        pu = kb.bank("d" if kk else "c")
                        kb.mm(pu.ap, kend.ap[:, j * 128:(j + 1) * 128], vt.ap[:, h * 512:(h + 1) * 512], True, True, [kend, vt], [pu])
                        kb.stt("dve", S_gla.ap[:, j, :], S_gla.ap[:, j, :], EG.ap[:, j, last:last + 1], pu.ap, ALU.mult, ALU.add,
                               [S_gla.ts[j], EG, pu], [S_gla.ts[j]])
                        kb.cp("pool", S_glab.ap[:, j, :], S_gla.ap[:, j, :], [S_gla.ts[j]], [S_glab.ts[j]])
                if full and not final:
                    kb.dma("pool", yf_tok[s0:s0 + 64, 2048:4096], osb.ap, [osb], [])
                if final:
                    kb.dma("sp", yfb.ap[0:64, :], yf_tok[s0:s0 + 64, 2048:4096], [], [yfb])
                    kb.dma("sp", zb.ap[0:64, :], go_tok[s0:s0 + 64, :], [], [zb])
                    kb.tt("dve", osb.ap, osb.ap, yfb.ap[0:64, :], ALU.add, [osb, yfb], [osb])
                    for h in range(4):
                        kb.act(junk.ap[0:64, :], osb.ap[:, h * 512:(h + 1) * 512], AF.Square, [osb], [junk, ssq],
                               accum=ssq.ap[0:64, 4 + h:5 + h])
                    kb.act(ssq.ap[0:64, 4:8], ssq.ap[0:64, 4:8], AF.Sqrt, [ssq, epsc], [ssq], bias=epsc.ap[0:64, :], scale=1.0 / 512)
                    kb.op("dve", lambda e: e.reciprocal(ssq.ap[0:64, 4:8], ssq.ap[0:64, 4:8]), [ssq], [ssq])
                    for h in range(4):
                        kb.stt("dve", osb.ap[:, h * 512:(h + 1) * 512], osb.ap[:, h * 512:(h + 1) * 512], ssq.ap[0:64, 4 + h:5 + h],
                               rows.ap[0:64, R_GLAN:R_GLAN + 512], ALU.mult, ALU.mult, [osb, ssq, rows], [osb])
                    kb.act(szb.ap[0:64, :], zb.ap[0:64, :], AF.Silu, [zb], [szb])
                    mo = mixb[(n_ * 2 + sn) % 2]
                    kb.tt("dve", mo.ap[0:64, :], osb.ap, szb.ap[0:64, :], ALU.mult, [osb, szb], [mo])
                    kb.dma("pool", mix_tok[s0:s0 + 64, 2048:4096], mo.ap[0:64, :], [mo], [])

    agS = Tile(None)

    def save_states(dirn):
        o = dirn * 6184
        for tl, flat, lo, n in ((S_ssd, S_ssd.ap.rearrange("p a b -> p (a b)"), 0, 2048),
                                (S_gla, S_gla.ap.rearrange("p a b -> p (a b)"), 2048, 4096),
                                (Dt_s, Dt_s.ap, 6144, 32), (Dt_g, Dt_g.ap, 6176, 8)):
            for k, a, b, off in segs(o + lo, o + lo + n):
                kb.dma("pool", xg_dp[k][:, a:b], flat[:, off:off + (b - a)], [tl], [agS])

    scan_pass(0, False, False, save=True)
    save_states(0)
    scan_pass(1, False, False, save=True)
    save_states(1)
    kb.barrier()
    for k in range(8):
        kb.coll("AllGather", RG, xg_dp[k][:, :], xg_gp[k][:, :], [agS], [agS])
    aoff_B = kb.aoff
    kb.aoff = aoff_keep
    acc = kb.sb([128, 6144], F32)
    Sr = [kb.sb([128, 6184], F32) for _ in range(2)]
    Deff = [kb.sb([128, 40], F32) for _ in range(2)]
    tmpc = kb.sb([128, 512], F32)
    for dirn in range(2):
        kb.op("dve", lambda e: e.memset(acc.ap, 0.0), [], [acc])
        rr = list(range(GR)) if dirn == 0 else list(range(GR - 1, -1, -1))
        for n_, r in enumerate(rr):
            sr, de = Sr[n_ % 2], Deff[n_ % 2]
            o = dirn * 6184
            for kk in range(4):
                kb.dma("sp", sr.ap[:, kk * PW:(kk + 1) * PW], xg_gp[dirn * 4 + kk][r * 128:(r + 1) * 128, :], [agS], [sr])
            b = lmk.ap[:, dirn * 8 + r:dirn * 8 + r + 1]
            kb.ts("dve", de.ap, sr.ap[:, 6144:6184], -1.0, b, ALU.add, ALU.mult, [sr, lmk], [de])
            kb.ts("dve", de.ap, de.ap, 1.0, None, ALU.add, None, [de], [de])
            a3 = acc.ap[:, 0:2048].rearrange("p (h d) -> p h d", h=32)
            kb.tt("dve", a3, a3, bc(de.ap[:, 0:32], 64), ALU.mult, [acc, de], [acc])
            kb.stt("dve", acc.ap[:, 0:2048], sr.ap[:, 0:2048], b, acc.ap[:, 0:2048], ALU.mult, ALU.add, [sr, lmk, acc], [acc])
            for j in range(8):
                sl = slice(2048 + j * 512, 2048 + (j + 1) * 512)
                kb.ts("pool", tmpc.ap, sr.ap[:, sl], b, None, ALU.mult, None, [sr, lmk], [tmpc])
                kb.stt("dve", acc.ap[:, sl], acc.ap[:, sl], de.ap[:, 32 + j:33 + j], tmpc.ap, ALU.mult, ALU.add, [acc, de, tmpc], [acc])
        kb.dma("pool", sin_d[dirn], acc.ap, [acc], [])
    kb.barrier()
    kb.aoff = aoff_B
    scan_pass(0, True, False, init=True)
    kb.barrier()
    scan_pass(1, True, True, init=True)
    kb.barrier()
    kb.reset()
    if STOP_AFTER == "B":
        return finalize(nc, kb, stack)
    WC = 512
    NTC = T // WC
    kb.pp = {"ab": [[0, 1, 2, 3], 0], "o": [[4, 5], 0], "m": [[6, 7], 0]}
    xT = kb.sb([128, 16, WC], F32, n=16)
    hT = kb.sb([128, 16, WC], BF16, n=16)
    oT = kb.sb([128, 16, WC], BF16, n=16)
    hid = kb.sb([128, 44, WC], BF16, n=44)
    mixT = Tile(hid.ap.rearrange("p a b -> p (a b)")[:, 0:32 * WC].rearrange("p (a b) -> p a b", a=32), 32)
    yT = Tile(hid.ap.rearrange("p a b -> p (a b)")[:, 0:32 * WC].bitcast(F32).rearrange("p (a b) -> p a b", a=16), 16)
    ws8 = [kb.sb([128, 16, 256], BF16) for _ in range(3)]
    ws11 = [kb.sb([128, 44, 128], BF16) for _ in range(2)]
    sqb = [kb.sb([128, 512], F32) for _ in range(2)]
    rstd = kb.sb([128, 512], F32)
    sa = [kb.sb([128, 512], F32) for _ in range(2)]
    kTm = kb.sb([128, 16, 256], BF16)
    vm = kb.sb([128, 2, 2048], BF16)
    mtok = [kb.sb([128, 2048], F32)] * 2
    qTh = kb.sb([128, 4, WC], BF16)
    PT = kb.sb([128, 2, WC], BF16)
    Pf = kb.sb([128, 256], F32)
    Pb = kb.sb([128, 256], BF16)
    st4 = kb.sb([128, 4], F32)

    mT = Tile(hid.ap.rearrange("p a b -> p (a b)")[:, 0:16 * 256].rearrange("p (a b) -> p a b", a=16), 1)
    for mt in range(2):
        mk = mtok[mt]
        kb.dma("sp", mk.ap, mem_in[mt * 128:(mt + 1) * 128, :], [], [mk])
        for q4 in range(4):
            kb.act(sa[q4 % 2].ap, mk.ap[:, q4 * 512:(q4 + 1) * 512], AF.Square, [mk], [sa[q4 % 2], st4], accum=st4.ap[:, q4:q4 + 1])
        kb.op("dve", lambda e, o=rstd.ap[:, mt:mt + 1], i=st4.ap: e.reduce_sum(o, i, mybir.AxisListType.X), [st4], [rstd])
        kb.act(rstd.ap[:, mt:mt + 1], rstd.ap[:, mt:mt + 1], AF.Sqrt, [rstd, epsc], [rstd], bias=epsc.ap, scale=1.0 / D)
        kb.op("dve", lambda e, a=rstd.ap[:, mt:mt + 1]: e.reciprocal(a, a), [rstd], [rstd])
        kb.ts("dve", mk.ap, mk.ap, rstd.ap[:, mt:mt + 1], None, ALU.mult, None, [mk, rstd], [mk])
        for k4 in range(4):
            pm = kb.bank("m")
            for u in range(4):
                kc = k4 * 4 + u
                kb.tr(pm.ap[:, u * 128:(u + 1) * 128], mk.ap[:, kc * 128:(kc + 1) * 128], IDF, [mk, cst], [pm])
            for u in range(4):
                kc = k4 * 4 + u
                kb.ts("dve", mT.ap[:, kc, mt * 128:(mt + 1) * 128], pm.ap[:, u * 128:(u + 1) * 128], ncol(N_MEM)[:, kc:kc + 1], None,
                      ALU.mult, None, [pm, cols], [mT])
    reqs = [("wckv", 16, pi * 256, 256) for pi in range(16)]
    st = PanelStream(ws8, reqs)
    for pi in range(8):
        pv, ps_ = st.get(pi)
        for j in range(2):
            po = kb.bank("ab")
            for kc in range(16):
                kb.mm(po.ap[:, 0:256], pv[:, kc, j * 128:(j + 1) * 128], mT.ap[:, kc, :], kc == 0, kc == 15, [ps_, mT], [po])
            kb.cp("act" if j else "dve", kTm.ap[:, pi * 2 + j, :], po.ap[:, 0:256], [po], [kTm])
        st.done(pi)
    for pi in range(8, 16):
        pv, ps_ = st.get(pi)
        for mt in range(2):
            po = kb.bank("ab")
            for kc in range(16):
                kb.mm(po.ap[:, 0:256], mT.ap[:, kc, mt * 128:(mt + 1) * 128], pv[:, kc, :], kc == 0, kc == 15, [ps_, mT], [po])
            kb.cp("act" if mt else "dve", vm.ap[:, mt, (pi - 8) * 256:(pi - 7) * 256], po.ap[:, 0:256], [po], [vm])
        st.done(pi)

    SCL = 512.0 ** -0.5
    for ti in range(NTC):
        t0 = ti * WC
        kb.dma("sp", xT.ap, x1T[:, :, t0:t0 + WC].rearrange("k p t -> p k t"), [], [xT])
        for sub in range(WC // 128):
            mk = mtok[sub % 2]
            mkb = mk.ap.bitcast(BF16)
            kb.dma("sp", mkb, mix_tok[t0 + sub * 128:t0 + (sub + 1) * 128, :], [], [mk])
            for c4 in range(8):
                pm = kb.bank("m")
                pmb = pm.ap[:, 0:256].bitcast(BF16)
                for u in range(4):
                    c = c4 * 4 + u
                    kb.tr(pmb[:, u * 128:(u + 1) * 128], mkb[:, c * 128:(c + 1) * 128], IDB, [mk, cstb], [pm])
                kb.cp("act" if c4 % 2 else "dve", mixT.ap[:, c4 * 4:c4 * 4 + 4, sub * 128:(sub + 1) * 128],
                      pmb.rearrange("p (a b) -> p a b", a=4), [pm], mixT.ts[c4 * 4:c4 * 4 + 4])
        st = PanelStream(ws8, [("wout", 32, dc * 128, 128) for dc in range(16)])
        for dc in range(16):
            pv, ps_ = st.get(dc)
            po = kb.bank("o")
            for cc in range(32):
                kb.mm(po.ap[:, 0:WC], pv[:, cc, :], mixT.ap[:, cc, :], cc == 0, cc == 31, [ps_, mixT.ts[cc]], [po])
            kb.tt("dve", xT.ap[:, dc, :], xT.ap[:, dc, :], po.ap[:, 0:WC], ALU.add, [po, xT.ts[dc]], [xT.ts[dc]])
            st.done(dc)
        rmsnorm_fm(xT, hT, ncol(N_XA), sqb, rstd, kb.bank("m"))
        st = PanelStream(ws8, [("wcq", 16, pi * 256, 256) for pi in range(8)])
        for h in range(4):
            for pj in range(2):
                pv, ps_ = st.get(h * 2 + pj)
                for j in range(2):
                    po = kb.bank("ab")
                    for kc in range(16):
                        kb.mm(po.ap[:, 0:WC], pv[:, kc, j * 128:(j + 1) * 128], hT.ap[:, kc, :], kc == 0, kc == 15, [ps_, hT.ts[kc]], [po])
                    kb.cp("act" if j else "dve", qTh.ap[:, pj * 2 + j, :], po.ap[:, 0:WC], [po], [qTh])
                st.done(h * 2 + pj)
            for sub in range(WC // 128):
                pss = kb.bank("o")
                for j in range(4):
                    kb.mm(pss.ap[:, 0:256], qTh.ap[:, j, sub * 128:(sub + 1) * 128], kTm.ap[:, h * 4 + j, :], j == 0, j == 3, [qTh, kTm], [pss])
                kb.op("dve", lambda e, o=st4.ap[:, 0:1], i=pss.ap[:, 0:256]: e.reduce_max(o, i, mybir.AxisListType.X), [pss], [st4])
                kb.ts("dve", st4.ap[:, 1:2], st4.ap[:, 0:1], -SCL, None, ALU.mult, None, [st4], [st4])
                kb.act(Pf.ap, pss.ap[:, 0:256], AF.Exp, [pss, st4], [Pf, st4], bias=st4.ap[:, 1:2], scale=SCL, accum=st4.ap[:, 2:3])
                kb.op("dve", lambda e, o=st4.ap[:, 3:4], i=st4.ap[:, 2:3]: e.reciprocal(o, i), [st4], [st4])
                kb.ts("dve", Pb.ap, Pf.ap, st4.ap[:, 3:4], None, ALU.mult, None, [Pf, st4], [Pb])
                pm = kb.bank("m")
                pmb = pm.ap[:, 0:128].bitcast(BF16)
                for mc in range(2):
                    kb.tr(pmb[:, mc * 128:(mc + 1) * 128], Pb.ap[:, mc * 128:(mc + 1) * 128], IDB, [Pb, cstb], [pm])
                kb.cp("act", PT.ap[:, :, sub * 128:(sub + 1) * 128], pmb.rearrange("p (a b) -> p a b", a=2), [pm], [PT])
            for j in range(4):
                po = kb.bank("ab")
                for mc in range(2):
                    kb.mm(po.ap[:, 0:WC], vm.ap[:, mc, (h * 4 + j) * 128:(h * 4 + j + 1) * 128], PT.ap[:, mc, :], mc == 0, mc == 1, [vm, PT], [po])
                kb.cp("act" if j % 2 else "dve", oT.ap[:, h * 4 + j, :], po.ap[:, 0:WC], [po], [oT.ts[h * 4 + j]])
        st = PanelStream(ws8, [("wco", 16, pi * 256, 256) for pi in range(8)])
        for pi in range(8):
            pv, ps_ = st.get(pi)
            for j in range(2):
                dc = pi * 2 + j
                po = kb.bank("o")
                for kc in range(16):
                    kb.mm(po.ap[:, 0:WC], pv[:, kc, j * 128:(j + 1) * 128], oT.ap[:, kc, :], kc == 0, kc == 15, [ps_, oT.ts[kc]], [po])
                kb.tt("dve", xT.ap[:, dc, :], xT.ap[:, dc, :], po.ap[:, 0:WC], ALU.add, [po, xT.ts[dc]], [xT.ts[dc]])
            st.done(pi)
        rmsnorm_fm(xT, hT, ncol(N_FFN2), sqb, rstd, kb.bank("m"))
        ffn(xT, hT, hid, "w1b", "w3b", "w2b", ws8, ws11, sa, ti)
        rmsnorm_fm(xT, yT, ncol(N_FIN), sqb, rstd, kb.bank("m"))
        for sub in range(WC // 128):
            yo = mtok[sub % 2]
            for k4 in range(4):
                pm = kb.bank("m")
                for u in range(4):
                    kc = k4 * 4 + u
                    kb.tr(pm.ap[:, u * 128:(u + 1) * 128], yT.ap[:, kc, sub * 128:(sub + 1) * 128], IDF, [yT.ts[kc], cst], [pm])
                kb.cp("act" if k4 % 2 else "dve", yo.ap[:, k4 * 512:(k4 + 1) * 512], pm.ap, [pm], [yo])
            kb.dma("pool", y_out[t0 + sub * 128:t0 + (sub + 1) * 128, :], yo.ap, [yo], [])
    return finalize(nc, kb, stack)


STOP_AFTER = None
BIS = 0


def finalize(nc, kb, stack):
    kb.barrier()
    with nc.Block() as block:
        @block.tensor
        def _(eng):
            kb.replay("pe", eng)

        @block.scalar
        def _(eng):
            kb.replay("act", eng)

        @block.vector
        def _(eng):
            kb.replay("dve", eng)

        @block.gpsimd
        def _(eng):
            kb.replay("pool", eng)

        @block.sync
        def _(eng):
            kb.replay("sp", eng)
    stack.close()
    return nc


def make_consts():
    p = np.arange(128)[:, None]
    f = np.arange(128)[None, :]
    mats = [p == f, p <= f, p >= f, p > f, p < f, np.ones((128, 128), bool)]
    c = [m.astype(np.float32) for m in mats]
    c += [m.astype(np.float32) * (-1.0 / 16.0) for m in mats[1:5]]
    return np.ascontiguousarray(np.concatenate(c, axis=1))


def colify(v):
    return np.asarray(v, np.float32).reshape(-1, 128).T


def host_inputs(inp, x, mem):
    g = lambda k: np.asarray(inp[k], np.float32)
    cols = np.concatenate(
        [colify(g(k).reshape(-1)) for k in ("ffn1_norm", "mix_norm", "xattn_norm", "mem_norm", "ffn2_norm", "final_norm")]
        + [g("conv_w")[0].T.reshape(24, 128, 5).transpose(1, 0, 2).reshape(128, 120),
           colify(g("conv_b").reshape(-1))], axis=1)
    rows = np.concatenate([
        g("dt_bias_fwd").reshape(-1), g("dt_bias_bwd").reshape(-1),
        g("a_log_fwd").reshape(-1), g("a_log_bwd").reshape(-1),
        g("d_skip").reshape(-1), g("ssd_norm").reshape(-1), g("gla_norm").reshape(-1)])[None, :]
    rows = np.ascontiguousarray(rows, np.float32)
    wg = np.stack([np.concatenate([g("gla_gate_w_fwd")[0], g("gla_gate_b_fwd")], 0),
                   np.concatenate([g("gla_gate_w_bwd")[0], g("gla_gate_b_bwd")], 0)], 0)
    m = {"x": np.ascontiguousarray(x, np.float32), "mem": np.ascontiguousarray(mem, np.float32),
         "consts": make_consts(), "cols": np.ascontiguousarray(cols, np.float32), "rows": rows,
         "wg": np.ascontiguousarray(wg, np.float32)}
    for k in ("ffn1_w1", "ffn1_w3", "ffn1_w2", "w_in", "w_out", "w_cq", "w_ckv", "w_co", "ffn2_w1", "ffn2_w3", "ffn2_w2"):
        m[k] = np.ascontiguousarray(g(k)[0])
    return m


T_CORE = 4096
_CACHE = {}


def tmask_for(T, valid):
    nch = T // 128
    m = (np.arange(nch) * 128 < valid).astype(np.float32)
    return np.ascontiguousarray(np.broadcast_to(m[None, :], (128, nch)))


def lmask_for(c, chain):
    m = np.zeros((128, 32), np.float32)
    i = chain.index(c)
    for r in chain[:i]:
        m[:, r] = 1.0
    for r in chain[i + 1:]:
        m[:, 8 + r] = 1.0
    if i > 0:
        m[:, 16 + chain[i - 1]] = 1.0
    if i + 1 < len(chain):
        m[:, 24 + chain[i + 1]] = 1.0
    return m


def kernel(**inputs):
    inp = {k: np.asarray(v) for k, v in inputs.items()}
    xp, xs_, mp, ms = inp["x_prompt"], inp["x_sample"], inp["mem_prompt"], inp["mem_sample"]
    T = T_CORE
    if T not in _CACHE:
        _CACHE[T] = build(T)
    nc = _CACHE[T]
    in_maps = []
    base = None
    for c in range(8):
        if c < 4:
            x, mem, chain, cl = xp[c], mp[c], [c], c
        else:
            x, mem, chain, cl = xs_[0, (c - 4) * T:(c - 3) * T], ms[0], [0, 1, 2, 3], c - 4
        if base is None:
            base = host_inputs(inp, x, mem)
        m = dict(base)
        m["x"] = np.ascontiguousarray(x, np.float32)
        m["mem"] = np.ascontiguousarray(mem, np.float32)
        m["tmask"] = tmask_for(T, T)
        m["lmask"] = lmask_for(cl, chain)
        in_maps.append(m)
    res = run_bass_kernel_spmd(nc, in_maps, core_ids=list(range(8)))
    ys = [r["y"] for r in res.results]
    y_prompt = np.stack(ys[0:4], 0).astype(np.float32)
    y_sample = np.concatenate(ys[4:8], 0)[None].astype(np.float32)
    return (y_prompt, y_sample)
```

```python
import contextlib
import numpy as np
import concourse.bass as bass
import concourse.mybir as mybir
from concourse.bass_utils import run_bass_kernel_spmd

F32 = mybir.dt.float32
BF16 = mybir.dt.bfloat16
AF = mybir.ActivationFunctionType
ALU = mybir.AluOpType

D = 2048
DFF = 5632
DIN = 11360
NMEM = 256
EPS = 1e-6
OFF_Z, OFF_XBC, OFF_DT, OFF_Q, OFF_K, OFF_V, OFF_LOW, OFF_GO = 0, 2048, 5120, 5184, 6208, 7232, 9280, 9312
ENGS = ("pe", "act", "dve", "pool", "sp")
DEBUG = ()
NRING = 24
NROWS = 64 + 64 + 32 + 2048 + 512


class Trk:
    __slots__ = ("w", "r")

    def __init__(self):
        self.w = None
        self.r = {}


class Tile:
    def __init__(self, ap=None, n=1):
        self.ap = ap
        self.ts = [Trk() for _ in range(n)]


class KB:
    def __init__(self, nc, stack):
        self.nc = nc
        self.q = {e: [] for e in ENGS}
        self.cnt = {e: 0 for e in ENGS}
        self.seen = {e: {} for e in ENGS}
        self.esem = {e: stack.enter_context(nc.semaphore("es_" + e)) for e in ENGS}
        self.ring = {qn: [stack.enter_context(nc.semaphore(f"dq_{qn}{i}")) for i in range(NRING)] for qn in ("sp", "pool", "act")}
        self.ringcnt = {qn: [0] * NRING for qn in self.ring}
        self.ringpos = {qn: 0 for qn in self.ring}
        self.semid = {}
        self.collsem = [stack.enter_context(nc.semaphore(f"coll{i}")) for i in range(12)]
        self.ncoll = 0
        self.colltoks = []
        self.arena = nc.alloc_sbuf_tensor("arena", [128, 52992], F32)
        self.aoff = 0
        self.persist = 0
        self.ps = [Tile(nc.alloc_psum_tensor(f"ps{i}", [128, 512], F32)[:, :]) for i in range(8)]
        self.pp = {}

    def sb(self, shape, dtype, n=1):
        esz = 4 if dtype == F32 else 2
        free = int(np.prod(shape[1:]))
        nbytes = (free * esz + 31) // 32 * 32
        assert self.aoff + nbytes <= 52992 * 4, ("SBUF arena overflow", self.aoff, nbytes)
        ap = self.arena[0:shape[0], self.aoff // 4:(self.aoff + nbytes) // 4]
        if dtype != F32:
            ap = ap.bitcast(dtype)
        ap = ap[:, 0:free]
        if len(shape) == 3:
            ap = ap.rearrange("p (a b) -> p a b", a=shape[1])
        elif len(shape) == 4:
            ap = ap.rearrange("p (a b c) -> p a b c", a=shape[1], b=shape[2])
        self.aoff += nbytes
        return Tile(ap, n)

    def reset(self):
        self.aoff = self.persist

    def bank(self, pool):
        lst, i = self.pp[pool]
        self.pp[pool][1] = (i + 1) % len(lst)
        return self.ps[lst[i]]

    def key(self, sem):
        return id(sem)

    def wait(self, e, tok):
        if tok is None:
            return
        sem, val, src = tok
        if src == "pe" and e == "pe":
            return
        k = id(sem)
        if self.seen[e].get(k, 0) >= val:
            return
        self.seen[e][k] = val
        self.q[e].append(("w", sem, val))

    def _deps(self, e, reads, writes):
        for t in reads:
            self.wait(e, t.w)
        for t in writes:
            if t.w is not None and t.w[2] != e:
                self.wait(e, t.w)
            for src, r in t.r.items():
                if src != e:
                    self.wait(e, r)

    def _mark(self, e, tok, reads, writes):
        for t in reads:
            t.r[tok[2] if tok[2] != "dma" else ("dma", id(tok[0]))] = tok
        for t in writes:
            t.w = tok
            t.r = {}

    @staticmethod
    def _trks(lst):
        out = []
        for x in lst:
            if isinstance(x, Tile):
                out.extend(x.ts)
            elif isinstance(x, (list, tuple)):
                out.extend(x)
            else:
                out.append(x)
        return out

    def op(self, e, fn, reads=(), writes=()):
        reads = self._trks(reads)
        writes = self._trks(writes)
        self._deps(e, reads, writes)
        self.cnt[e] += 1
        tok = (self.esem[e], self.cnt[e], e)
        self.q[e].append(("i", fn, self.esem[e], 1))
        self._mark(e, tok, reads, writes)
        return tok

    def dma(self, qn, out, in_, reads=(), writes=()):
        reads = self._trks(reads)
        writes = self._trks(writes)
        for t in reads:
            self.wait(qn, t.w)
        for t in writes:
            self.wait(qn, t.w)
            for r in t.r.values():
                self.wait(qn, r)
        i = self.ringpos[qn]
        self.ringpos[qn] = (i + 1) % NRING
        sem = self.ring[qn][i]
        c = self.ringcnt[qn][i]
        if c:
            self.wait(qn, (sem, c, "dma"))
        self.ringcnt[qn][i] = c + 16
        tok = (sem, c + 16, "dma")
        self.q[qn].append(("i", (lambda e, o=out, s=in_: e.dma_start(out=o, in_=s)), sem, 16))
        self._mark(qn, tok, reads, writes)
        return tok

    def coll(self, kind, rg, in_ap, out_ap, reads=(), writes=()):
        reads = self._trks(reads)
        writes = self._trks(writes)
        qn = "pool"
        for t in reads:
            self.wait(qn, t.w)
        for t in writes:
            self.wait(qn, t.w)
            for r in t.r.values():
                self.wait(qn, r)
        sem = self.collsem[self.ncoll]
        self.ncoll += 1
        tok = (sem, 1, "dma")
        self.q[qn].append(("i", (lambda e: e.collective_compute(kind, ALU.bypass, replica_groups=rg, ins=[in_ap.opt()], outs=[out_ap.opt()])), sem, 1))
        self._mark(qn, tok, reads, writes)
        self.colltoks.append(tok)
        return tok

    def barrier(self):
        for e in ENGS:
            for e2 in ENGS:
                if e2 != e and self.cnt[e2]:
                    self.wait(e, (self.esem[e2], self.cnt[e2], e2))
            for qn in self.ring:
                for i in range(NRING):
                    if self.ringcnt[qn][i]:
                        self.wait(e, (self.ring[qn][i], self.ringcnt[qn][i], "dma"))
            for tok in self.colltoks:
                self.wait(e, tok)

    def replay(self, e, eng):
        for it in self.q[e]:
            if it[0] == "w":
                eng.wait_ge(it[1], it[2])
            else:
                it[1](eng).then_inc(it[2], it[3])

    def mm(self, out, lhsT, rhs, start, stop, reads, writes):
        return self.op("pe", lambda e: e.matmul(out, lhsT, rhs, start=start, stop=stop), reads, writes)

    def tr(self, out, in_, ident, reads, writes):
        return self.op("pe", lambda e: e.transpose(out, in_, ident), reads, writes)

    def act(self, out, in_, func, reads, writes, bias=None, scale=1.0, accum=None, eng="act"):
        kw = {}
        if bias is not None:
            kw["bias"] = bias
        if accum is not None:
            kw["accum_out"] = accum
        return self.op("act", lambda e: e.activation(out, in_, func, scale=scale, **kw), reads, writes)

    def tt(self, e, out, in0, in1, op, reads, writes):
        return self.op(e, lambda g: g.tensor_tensor(out, in0, in1, op), reads, writes)

    def ts(self, e, out, in0, s1, s2, op0, op1, reads, writes):
        if s2 is None:
            return self.op(e, lambda g: g.tensor_scalar(out, in0, s1, None, op0), reads, writes)
        return self.op(e, lambda g: g.tensor_scalar(out, in0, s1, s2, op0, op1), reads, writes)

    def stt(self, e, out, in0, sc, in1, op0, op1, reads, writes):
        return self.op(e, lambda g: g.scalar_tensor_tensor(out, in0, sc, in1, op0, op1), reads, writes)

    def cp(self, e, out, in_, reads, writes):
        if e == "act":
            return self.op("act", lambda g: g.copy(out, in_), reads, writes)
        return self.op(e, lambda g: g.tensor_copy(out, in_), reads, writes)


def build(T, ncores=8):
    NT = T // 512
    NCH = T // 128
    nc = bass.Bass("TRN2", target_bir_lowering=False)
    stack = contextlib.ExitStack()
    kb = KB(nc, stack)

    def din(name, shape, dt=F32):
        return nc.dram_tensor(name, list(shape), dt, kind="ExternalInput").ap()

    def dscr(name, shape, dt):
        return nc.dram_tensor(name, list(shape), dt, kind=("ExternalOutput" if (DEBUG and name in DEBUG) else "Internal")).ap()

    x_in = din("x", [T, D])
    mem_in = din("mem", [NMEM, D])
    consts_in = din("consts", [128, 10 * 128])
    cols_in = din("cols", [128, 6 * 16 + 24 * 6])
    rows_in = din("rows", [1, NROWS])
    wg_in = din("wg", [2, 17, 1024])
    tmask_in = din("tmask", [128, NCH])
    lmask_in = din("lmask", [128, 32])
    GR = min(ncores, 4)
    RG = [list(range(g0, g0 + GR)) for g0 in range(0, ncores, GR)]
    XF = 2 * 6184
    edge_d = dscr("edge_d", [128, 96], BF16)
    edges_g = dscr("edges_g", [GR * 128, 96], BF16)
    PW = 1546
    xg_dp = [dscr(f"xg_d{k}", [128, PW], F32) for k in range(8)]
    xg_gp = [dscr(f"xg_g{k}", [GR * 128, PW], F32) for k in range(8)]

    def segs(lo, hi):
        out = []
        q = lo
        while q < hi:
            k = q // PW
            a = q - k * PW
            b = min(PW, a + (hi - q))
            out.append((k, a, b, q - lo))
            q += b - a
        return out
    sin_d = dscr("sin_d", [2, 128, 6144], F32)
    wsrc = {
        "w1a": din("ffn1_w1", [D, DFF]), "w3a": din("ffn1_w3", [D, DFF]), "w2a": din("ffn1_w2", [DFF, D]),
        "win": din("w_in", [D, DIN]), "wout": din("w_out", [2 * D, D]),
        "wcq": din("w_cq", [D, D]), "wckv": din("w_ckv", [D, 2 * D]), "wco": din("w_co", [D, D]),
        "w1b": din("ffn2_w1", [D, DFF]), "w3b": din("ffn2_w3", [D, DFF]), "w2b": din("ffn2_w2", [DFF, D]),
    }
    y_out = nc.dram_tensor("y", [T, D], F32, kind="ExternalOutput").ap()

    wb = {k: dscr("wb_" + k, v.shape, BF16) for k, v in wsrc.items()}
    x1T = dscr("x1T", [16, 128, T], F32)
    z_tok = dscr("z_tok", [T, 2048], BF16)
    dt_tok = dscr("dt_tok", [T, 64], F32)
    k_tok = dscr("k_tok", [T, 1024], BF16)
    v_tok = dscr("v_tok", [T, 2048], BF16)
    go_tok = dscr("go_tok", [T, 2048], BF16)
    xbcT = dscr("xbcT", [24, 128, T], BF16)
    qT_d = dscr("qT", [8, 128, T], BF16)
    kT_d = dscr("kT", [8, 128, T], BF16)
    lowT = dscr("lowT", [2, 16, T], F32)
    xs_tok = dscr("xs_tok", [T, 2048], BF16)
    B_tok = dscr("B_tok", [T, 512], BF16)
    BT_d = dscr("BT", [4, 128, T], BF16)
    CT_d = dscr("CT", [4, 128, T], BF16)
    yf_tok = dscr("yf_tok", [T, 4096], F32)
    mix_tok = dscr("mix_tok", [T, 4096], BF16)

    cst = kb.sb([128, 10, 128], F32)
    cstb = kb.sb([128, 10, 128], BF16)
    cols = kb.sb([128, 6 * 16 + 24 * 6], F32)
    epsc = kb.sb([128, 1], F32)
    kb.dma("sp", cst.ap, consts_in.rearrange("p (a b) -> p a b", a=10), [], [cst])
    kb.dma("sp", cols.ap, cols_in, [], [cols])
    kb.cp("dve", cstb.ap, cst.ap, [cst], [cstb])
    kb.op("dve", lambda g: g.memset(epsc.ap, EPS), [], [epsc])
    IDF = cst.ap[:, 0, :]
    IDB = cstb.ap[:, 0, :]
    ONESF = cst.ap[:, 5, :]

    def ncol(i):
        return cols.ap[:, 16 * i:16 * i + 16]
    N_FFN1, N_MIX, N_XA, N_MEM, N_FFN2, N_FIN = range(6)
    convw = cols.ap[:, 96:96 + 120].rearrange("p (c k) -> p c k", c=24)
    convb = cols.ap[:, 216:240]
    kb.persist = kb.aoff

    CW = 4096
    wtmp = [kb.sb([128, CW], F32) for _ in range(2)]
    wtb = [kb.sb([128, CW], BF16) for _ in range(2)]
    it = 0
    cast_engs = ("dve", "pool", "act")
    FIRST = ("w1a", "w3a", "w2a", "win")
    rest_steps = []
    for name, src in wsrc.items():
        K, N = src.shape
        tot = K * N // 128
        s2 = src.rearrange("(p a) n -> p (a n)", p=128)
        d2 = wb[name].rearrange("(p a) n -> p (a n)", p=128)
        if name not in FIRST:
            for c0 in range(0, tot, 1024):
                rest_steps.append((s2, d2, c0, min(1024, tot - c0)))
            continue
        for c0 in range(0, tot, CW):
            cw = min(CW, tot - c0)
            a, b = wtmp[it % 2], wtb[it % 2]
            kb.dma("sp", a.ap[:, 0:cw], s2[:, c0:c0 + cw], [], [a])
            kb.cp(cast_engs[it % 3], b.ap[:, 0:cw], a.ap[:, 0:cw], [a], [b])
            kb.dma("pool", d2[:, c0:c0 + cw], b.ap[:, 0:cw], [b], [])
            it += 1
    kb.barrier()
    kb.reset()
    rest_steps.reverse()
    cvs = {"pending": None, "n": 0, "bufs": None}

    def conv_tick(flush=False):
        if cvs["bufs"] is None:
            return
        new = None
        if rest_steps and not flush:
            s2, d2, c0, cw = rest_steps.pop()
            a, b = cvs["bufs"][cvs["n"] % 2]
            cvs["n"] += 1
            kb.dma("pool", a.ap[:, 0:cw], s2[:, c0:c0 + cw], [], [a])
            new = (a, b, d2, c0, cw)
        p = cvs["pending"]
        if p is not None:
            a, b, d2, c0, cw = p
            kb.cp("pool", b.ap[:, 0:cw], a.ap[:, 0:cw], [a], [b])
            kb.dma("pool", d2[:, c0:c0 + cw], b.ap[:, 0:cw], [b], [])
        cvs["pending"] = new

    def rmsnorm_fm(xT, hT, ncols, sqb, rstd, pbank):
        W = xT.ap.shape[2]
        for kc in range(16):
            s = sqb[kc % 2]
            kb.act(s.ap[:, 0:W], xT.ap[:, kc, :], AF.Square, [xT.ts[kc]], [s])
            kb.mm(pbank.ap[:, 0:W], ONESF, s.ap[:, 0:W], kc == 0, kc == 15, [s, cst], [pbank])
        kb.act(rstd.ap[:, 0:W], pbank.ap[:, 0:W], AF.Sqrt, [pbank, epsc], [rstd], bias=epsc.ap, scale=1.0 / D)
        kb.op("dve", lambda g: g.reciprocal(rstd.ap[:, 0:W], rstd.ap[:, 0:W]), [rstd], [rstd])
        for kc in range(16):
            kb.stt("dve", hT.ap[:, kc, :], xT.ap[:, kc, :], ncols[:, kc:kc + 1], rstd.ap[:, 0:W], ALU.mult, ALU.mult,
                   [xT.ts[kc], rstd, cols], [hT.ts[kc]])

    def load_panel(wname, KC, c0, w, slot):
        src = wb[wname].rearrange("(kc p) n -> p kc n", p=128)[:, :, c0:c0 + w]
        flat = slot.ap.rearrange("p a b -> p (a b)") if len(slot.ap.shape) == 3 else slot.ap
        dst = flat[:, 0:KC * w].rearrange("p (a b) -> p a b", a=KC)
        kb.dma("sp", dst, src, [], [slot])
        return dst

    class PanelStream:
        def __init__(self, slots, reqs):
            self.slots = slots
            self.reqs = reqs
            self.views = {}
            for i in range(min(len(slots), len(reqs))):
                self._issue(i)

        def _issue(self, i):
            slot = self.slots[i % len(self.slots)]
            self.views[i] = (load_panel(*self.reqs[i], slot), slot)
            conv_tick()

        def get(self, i):
            return self.views[i]

        def done(self, i):
            del self.views[i]
            j = i + len(self.slots)
            if j < len(self.reqs):
                self._issue(j)

    def ffn(xT, hT, hid, w1, w3, w2, ws8, ws11, sa, tile_tag):
        W = xT.ap.shape[2]
        reqs = []
        for fp in range(DFF // 256):
            reqs.append((w1, 16, fp * 256, 256))
            reqs.append((w3, 16, fp * 256, 256))
        st = PanelStream(ws8, reqs)
        for fp in range(DFF // 256):
            p1, s1 = st.get(2 * fp)
            p3, s3 = st.get(2 * fp + 1)
            for j in range(2):
                fc = fp * 2 + j
                pa = kb.bank("ab")
                pb = kb.bank("ab")
                for kc in range(16):
                    kb.mm(pa.ap[:, 0:W], p1[:, kc, j * 128:(j + 1) * 128], hT.ap[:, kc, :], kc == 0, kc == 15, [s1, hT.ts[kc]], [pa])
                for kc in range(16):
                    kb.mm(pb.ap[:, 0:W], p3[:, kc, j * 128:(j + 1) * 128], hT.ap[:, kc, :], kc == 0, kc == 15, [s3, hT.ts[kc]], [pb])
                s = sa[fc % 2]
                kb.act(s.ap[:, 0:W], pa.ap[:, 0:W], AF.Silu, [pa], [s])
                kb.tt("dve", hid.ap[:, fc, :], s.ap[:, 0:W], pb.ap[:, 0:W], ALU.mult, [s, pb], [hid.ts[fc]])
            st.done(2 * fp)
            st.done(2 * fp + 1)
        reqs2 = [(w2, 44, dc * 128, 128) for dc in range(16)]
        st2 = PanelStream(ws11, reqs2)
        for dc in range(16):
            p2, s2 = st2.get(dc)
            po = kb.bank("o")
            for fc in range(44):
                kb.mm(po.ap[:, 0:W], p2[:, fc, :], hid.ap[:, fc, :], fc == 0, fc == 43, [s2, hid.ts[fc]], [po])
            kb.stt("dve", xT.ap[:, dc, :], po.ap[:, 0:W], 0.5, xT.ap[:, dc, :], ALU.mult, ALU.add, [po, xT.ts[dc]], [xT.ts[dc]])
            st2.done(dc)

    kb.pp = {"ab": [[0, 1, 2, 3], 0], "o": [[4, 5], 0], "m": [[6, 7], 0]}
    xtok = [kb.sb([128, 2048], F32) for _ in range(2)]
    xT = kb.sb([128, 16, 512], F32, n=16)
    hT = kb.sb([128, 16, 512], BF16, n=16)
    hid = kb.sb([128, 44, 512], BF16, n=44)
    ws8 = [kb.sb([128, 16, 256], BF16) for _ in range(4)]
    ws11 = [kb.sb([128, 44, 128], BF16) for _ in range(2)]
    sqb = [kb.sb([128, 512], F32) for _ in range(2)]
    rstd = kb.sb([128, 512], F32)
    sa = [kb.sb([128, 512], F32) for _ in range(2)]
    ev = [kb.sb([128, 512], BF16) for _ in range(4)]
    evf = [kb.sb([128, 512], F32) for _ in range(2)]
    cvs["bufs"] = [(kb.sb([128, 1024], F32), kb.sb([128, 1024], BF16)) for _ in range(2)]
    evi = [0]

    def nev():
        evi[0] += 1
        return ev[evi[0] % 4]

    for ti in range(NT):
        t0 = ti * 512
        for sub in range(4):
            xt = xtok[sub % 2]
            kb.dma("sp", xt.ap, x_in[t0 + sub * 128:t0 + (sub + 1) * 128, :], [], [xt])
            for k4 in range(4):
                pm = kb.bank("m")
                for u in range(4):
                    kc = k4 * 4 + u
                    kb.tr(pm.ap[:, u * 128:(u + 1) * 128], xt.ap[:, kc * 128:(kc + 1) * 128], IDF, [xt, cst], [pm])
                e = "dve" if k4 % 2 == 0 else "act"
                kb.cp(e, xT.ap[:, k4 * 4:k4 * 4 + 4, sub * 128:(sub + 1) * 128],
                      pm.ap.rearrange("p (a b) -> p a b", a=4), [pm], xT.ts[k4 * 4:k4 * 4 + 4])
        rmsnorm_fm(xT, hT, ncol(N_FFN1), sqb, rstd, kb.bank("m"))
        ffn(xT, hT, hid, "w1a", "w3a", "w2a", ws8, ws11, sa, ti)
        kb.dma("pool", x1T[:, :, t0:t0 + 512].rearrange("k p t -> p k t"), xT.ap, [xT], [])
        rmsnorm_fm(xT, hT, ncol(N_MIX), sqb, rstd, kb.bank("m"))
        tm_jobs = [(z_tok, OFF_Z, 2048), (k_tok, OFF_K, 1024), (v_tok, OFF_V, 2048), (go_tok, OFF_GO, 2048)]
        fm_jobs = [(xbcT, OFF_XBC, 24, 1.0), (qT_d, OFF_Q, 8, 0.0625), (kT_d, OFF_K, 8, 1.0)]
        reqs = []
        for dst, c0, ncs in tm_jobs:
            for pi in range(ncs // 256):
                reqs.append(("win", 16, c0 + pi * 256, 256))
        for dst, c0, nch, sc in fm_jobs:
            for pi in range(nch // 2):
                reqs.append(("win", 16, c0 + pi * 256, 256))
        reqs.append(("win", 16, OFF_DT, 64))
        reqs.append(("win", 16, OFF_LOW, 32))
        st = PanelStream(ws8, reqs)
        ri = 0
        for dst, c0, ncs in tm_jobs:
            for pi in range(ncs // 256):
                pv, ps_ = st.get(ri)
                ri += 1
                for sub in range(4):
                    po = kb.bank("ab")
                    for kc in range(16):
                        kb.mm(po.ap[:, 0:256], hT.ap[:, kc, sub * 128:(sub + 1) * 128], pv[:, kc, :], kc == 0, kc == 15,
                              [ps_, hT.ts[kc]], [po])
                    o = nev()
                    kb.cp("act" if sub % 2 else "dve", o.ap[:, 0:256], po.ap[:, 0:256], [po], [o])
                    kb.dma("pool", dst[t0 + sub * 128:t0 + (sub + 1) * 128, pi * 256:(pi + 1) * 256], o.ap[:, 0:256], [o], [])
                st.done(ri - 1)
        for dst, c0, nch, sc in fm_jobs:
            for pi in range(nch // 2):
                pv, ps_ = st.get(ri)
                ri += 1
                for j in range(2):
                    po = kb.bank("ab")
                    for kc in range(16):
                        kb.mm(po.ap, pv[:, kc, j * 128:(j + 1) * 128], hT.ap[:, kc, :], kc == 0, kc == 15, [ps_, hT.ts[kc]], [po])
                    o = nev()
                    if sc == 1.0:
                        kb.cp("act" if j % 2 else "dve", o.ap, po.ap, [po], [o])
                    else:
                        kb.ts("dve", o.ap, po.ap, sc, None, ALU.mult, None, [po], [o])
                    kb.dma("pool", dst[pi * 2 + j, :, t0:t0 + 512], o.ap, [o], [])
                st.done(ri - 1)
        pv, ps_ = st.get(ri)
        ri += 1
        for sub in range(4):
            po = kb.bank("ab")
            for kc in range(16):
                kb.mm(po.ap[:, 0:64], hT.ap[:, kc, sub * 128:(sub + 1) * 128], pv[:, kc, :], kc == 0, kc == 15, [ps_, hT.ts[kc]], [po])
            o = evf[sub % 2]
            kb.cp("dve", o.ap[:, 0:64], po.ap[:, 0:64], [po], [o])
            kb.dma("pool", dt_tok[t0 + sub * 128:t0 + (sub + 1) * 128, :], o.ap[:, 0:64], [o], [])
        st.done(ri - 1)
        pv, ps_ = st.get(ri)
        ri += 1
        for j in range(2):
            po = kb.bank("ab")
            for kc in range(16):
                kb.mm(po.ap[0:16, :], pv[:, kc, j * 16:(j + 1) * 16], hT.ap[:, kc, :], kc == 0, kc == 15, [ps_, hT.ts[kc]], [po])
            o = evf[j % 2]
            kb.cp("dve", o.ap[0:16, :], po.ap[0:16, :], [po], [o])
            kb.dma("pool", lowT[j, :, t0:t0 + 512], o.ap[0:16, :], [o], [])
    while rest_steps or cvs["pending"] is not None:
        conv_tick()
    cvs["bufs"] = None
    kb.barrier()
    kb.reset()
    if STOP_AFTER == "A":
        return finalize(nc, kb, stack)

    kb.pp = {"a": [[0, 1], 0], "b": [[2, 3], 0], "c": [[4, 5], 0], "d": [[6, 7], 0]}
    lmk = kb.sb([128, 32], F32)
    kb.dma("sp", lmk.ap, lmask_in, [], [lmk])
    xsrc = xbcT.rearrange("c p t -> p c t")
    agE = Tile(None)
    e0 = kb.sb([128, 24, 2], BF16)
    e1 = kb.sb([128, 24, 2], BF16)
    with nc.allow_non_contiguous_dma(reason="tiny conv halo"):
        pass
    kb.dma("sp", e0.ap, xsrc[:, :, 0:2], [], [e0])
    kb.dma("sp", e1.ap, xsrc[:, :, T - 2:T], [], [e1])
    kb.dma("sp", edge_d[:, 0:48], e0.ap.rearrange("p a b -> p (a b)"), [e0], [agE])
    kb.dma("sp", edge_d[:, 48:96], e1.ap.rearrange("p a b -> p (a b)"), [e1], [agE])
    kb.coll("AllGather", RG, edge_d[:, :], edges_g[:, :], [agE], [agE])
    eg = kb.sb([128, GR, 96], BF16)
    kb.dma("sp", eg.ap, edges_g.rearrange("(r p) f -> p r f", p=128), [agE], [eg])
    hl = kb.sb([128, 48], F32)
    hr = kb.sb([128, 48], F32)
    kb.op("dve", lambda g: g.memset(hl.ap, 0.0), [], [hl])
    kb.op("dve", lambda g: g.memset(hr.ap, 0.0), [], [hr])
    for r in range(GR):
        kb.stt("dve", hl.ap, eg.ap[:, r, 48:96], lmk.ap[:, 16 + r:17 + r], hl.ap, ALU.mult, ALU.add, [eg, lmk, hl], [hl])
        kb.stt("dve", hr.ap, eg.ap[:, r, 0:48], lmk.ap[:, 24 + r:25 + r], hr.ap, ALU.mult, ALU.add, [eg, lmk, hr], [hr])
    xh = [kb.sb([128, 24, 516], BF16) for _ in range(2)]
    cacc = [kb.sb([128, 512], F32) for _ in range(3)]
    xc = kb.sb([128, 24, 512], BF16, n=24)
    tko = [kb.sb([128, 2048], BF16) for _ in range(2)]
    tkb = [kb.sb([128, 512], BF16) for _ in range(2)]
    for ti in range(NT):
        t0 = ti * 512
        xt = xh[ti % 2]
        src = xbcT.rearrange("c p t -> p c t")
        lo = 2 if ti == 0 else 0
        hi = 514 if ti == NT - 1 else 516
        if ti == 0:
            kb.cp("dve", xt.ap[:, :, 0:2], hl.ap.rearrange("p (a b) -> p a b", a=24), [hl], [xt])
        if ti == NT - 1:
            kb.cp("dve", xt.ap[:, :, 514:516], hr.ap.rearrange("p (a b) -> p a b", a=24), [hr], [xt])
        kb.dma("sp", xt.ap[:, :, lo:hi], src[:, :, t0 - 2 + lo:t0 - 2 + hi], [], [xt])
        for c in range(24):
            acc = cacc[c % 3]
            e = "dve"
            kb.ts(e, acc.ap, xt.ap[:, c, 0:512], convw[:, c, 0:1], convb[:, c:c + 1], ALU.mult, ALU.add, [xt, cols], [acc])
            for k in range(1, 5):
                kb.stt(e, acc.ap, xt.ap[:, c, k:k + 512], convw[:, c, k:k + 1], acc.ap, ALU.mult, ALU.add, [xt, cols, acc], [acc])
            kb.act(xc.ap[:, c, :], acc.ap, AF.Silu, [acc], [xc.ts[c]])
        for sub in range(4):
            o = tko[sub % 2]
            for c4 in range(4):
                pm = kb.bank("a" if c4 % 2 else "b")
                pmb = pm.ap[:, 0:256].bitcast(BF16)
                for u in range(4):
                    c = c4 * 4 + u
                    kb.tr(pmb[:, u * 128:(u + 1) * 128], xc.ap[:, c, sub * 128:(sub + 1) * 128], IDB, [xc.ts[c], cstb], [pm])
                kb.cp("act" if c4 % 2 else "dve", o.ap[:, c4 * 512:(c4 + 1) * 512], pmb, [pm], [o])
            kb.dma("pool", xs_tok[t0 + sub * 128:t0 + (sub + 1) * 128, :], o.ap, [o], [])
            ob = tkb[sub % 2]
            pm = kb.bank("c")
            pmb = pm.ap[:, 0:256].bitcast(BF16)
            for u in range(4):
                kb.tr(pmb[:, u * 128:(u + 1) * 128], xc.ap[:, 16 + u, sub * 128:(sub + 1) * 128], IDB, [xc.ts[16 + u], cstb], [pm])
            kb.cp("dve", ob.ap, pmb, [pm], [ob])
            kb.dma("pool", B_tok[t0 + sub * 128:t0 + (sub + 1) * 128, :], ob.ap, [ob], [])
        kb.dma("pool", BT_d[:, :, t0:t0 + 512].rearrange("c p t -> p c t"), xc.ap[:, 16:20, :], [xc.ts[16:20]], [])
        kb.dma("pool", CT_d[:, :, t0:t0 + 512].rearrange("c p t -> p c t"), xc.ap[:, 20:24, :], [xc.ts[20:24]], [])
    kb.barrier()
    kb.reset()

    if STOP_AFTER == "B0":
        return finalize(nc, kb, stack)
    rows = kb.sb([128, NROWS], F32)
    kb.dma("sp", rows.ap, rows_in.partition_broadcast(128).rearrange("p a b -> p (a b)"), [], [rows])
    arow = kb.sb([128, 64], F32)
    kb.act(arow.ap, rows.ap[:, 64:128], AF.Exp, [rows], [arow])
    kb.ts("dve", arow.ap, arow.ap, -1.0, None, ALU.mult, None, [arow], [arow])
    tmk = kb.sb([128, NCH], F32)
    kb.dma("sp", tmk.ap, tmask_in, [], [tmk])
    onec = kb.sb([128, 1], F32)
    kb.op("dve", lambda g: g.memset(onec.ap, 1.0), [], [onec])
    R_DSK, R_SSDN, R_GLAN = 128, 160, 160 + 2048
    wgt = kb.sb([17, 1024], F32)
    lowa = [kb.sb([17, 64], F32) for _ in range(2)]
    for l in lowa:
        kb.op("dve", lambda g, a=l.ap: g.memset(a, 1.0), [], [l])
    lmk = kb.sb([128, 32], F32)
    kb.dma("sp", lmk.ap, lmask_in, [], [lmk])
    Dt_s = kb.sb([128, 32], F32)
    Dt_g = kb.sb([128, 8], F32)
    aoff_keep = kb.aoff
    S_ssd = kb.sb([128, 4, 512], F32, n=4)
    S_ssdb = kb.sb([128, 4, 512], BF16, n=4)
    S_gla = kb.sb([128, 8, 512], F32, n=8)
    S_glab = kb.sb([128, 8, 512], BF16, n=8)
    db = lambda shape, dt: [kb.sb(shape, dt) for _ in range(2)]
    xs_b, Bt_b, BT_b, CT_b, dtr_b = db([128, 2048], BF16), db([128, 512], BF16), db([128, 4, 128], BF16), db([128, 4, 128], BF16), db([128, 64], F32)
    kt_b, vt_b, qT_b, kT_b = db([64, 1024], BF16), db([64, 2048], BF16), db([128, 8, 64], BF16), db([128, 8, 64], BF16)
    dts = kb.sb([128, 32], F32)
    av = kb.sb([128, 32], F32)
    acs = kb.sb([128, 64], F32)
    dd = kb.sb([128, 32], F32)
    ea = kb.sb([128, 32], F32)
    dec = kb.sb([128, 32], F32)
    xes = kb.sb([128, 32], F32)
    xdt = kb.sb([128, 2048], BF16)
    xend = kb.sb([128, 2048], BF16)
    LmL = [kb.sb([128, 8, 128], F32) for _ in range(2)]
    EmL = [kb.sb([128, 8, 128], F32) for _ in range(2)]
    MhL = [kb.sb([128, 8, 128], BF16) for _ in range(2)]
    cbmL = [kb.sb([128, 128], F32) for _ in range(2)]
    ysb = kb.sb([128, 2048], F32, n=4)
    ytmpL = [kb.sb([128, 512], F32) for _ in range(2)]
    spL = [kb.sb([64, 1024], F32) for _ in range(2)]
    EGL = [kb.sb([128, 8, 64], F32) for _ in range(2)]
    GsL = [kb.sb([128, 8, 64], F32) for _ in range(2)]
    E1 = kb.sb([128, 8, 64], F32)
    Gm = kb.sb([128, 8, 64], F32)
    E2 = kb.sb([128, 8, 64], F32)
    qp = kb.sb([128, 8, 64], BF16)
    kp = kb.sb([128, 8, 64], BF16)
    qin = kb.sb([128, 8, 64], BF16)
    Ek = kb.sb([64, 1024], F32)
    kend = kb.sb([64, 1024], BF16)
    Am = kb.sb([64, 4, 64], BF16)
    osb = kb.sb([64, 2048], F32, n=4)
    yfb = kb.sb([128, 2048], F32)
    zb = kb.sb([128, 2048], BF16)
    szb = kb.sb([128, 2048], F32)
    mixb = db([128, 2048], BF16)
    ssq = kb.sb([128, 8], F32)
    junk = kb.sb([128, 512], F32)

    def bc(ap2, n):
        return ap2.unsqueeze(2).to_broadcast([ap2.shape[0], ap2.shape[1], n])

    def scan_pass(dirn, full, final, init=False, save=False):
        fwd = dirn == 0
        M_INC = 1 if fwd else 2
        M_STR = 3 if fwd else 4
        G_INC = 6 if fwd else 7
        G_STR = 8 if fwd else 9
        mid = 31 if fwd else 32
        last = 63 if fwd else 0
        kb.dma("sp", wgt.ap, wg_in[dirn], [], [wgt])
        if init:
            kb.dma("sp", S_ssd.ap.rearrange("p a b -> p (a b)"), sin_d[dirn, :, 0:2048], [], [S_ssd])
            kb.dma("sp", S_gla.ap.rearrange("p a b -> p (a b)"), sin_d[dirn, :, 2048:6144], [], [S_gla])
            for g in range(4):
                kb.cp("pool", S_ssdb.ap[:, g, :], S_ssd.ap[:, g, :], [S_ssd.ts[g]], [S_ssdb.ts[g]])
            for g in range(8):
                kb.cp("pool", S_glab.ap[:, g, :], S_gla.ap[:, g, :], [S_gla.ts[g]], [S_glab.ts[g]])
        else:
            for g in range(4):
                kb.op("pool", lambda e, a=S_ssd.ap[:, g, :]: e.memset(a, 0.0), [], [S_ssd.ts[g]])
                kb.op("pool", lambda e, a=S_ssdb.ap[:, g, :]: e.memset(a, 0.0), [], [S_ssdb.ts[g]])
            for g in range(8):
                kb.op("pool", lambda e, a=S_gla.ap[:, g, :]: e.memset(a, 0.0), [], [S_gla.ts[g]])
                kb.op("pool", lambda e, a=S_glab.ap[:, g, :]: e.memset(a, 0.0), [], [S_glab.ts[g]])
        if save:
            kb.op("dve", lambda e: e.memset(Dt_s.ap, 1.0), [], [Dt_s])
            kb.op("dve", lambda e: e.memset(Dt_g.ap, 1.0), [], [Dt_g])
        order = list(range(NCH)) if fwd else list(range(NCH - 1, -1, -1))
        for n_, ci in enumerate(order):
            t0 = ci * 128
            pb_ = n_ % 2
            xs, Bt, BT, CT, dtr = xs_b[pb_], Bt_b[pb_], BT_b[pb_], CT_b[pb_], dtr_b[pb_]
            kb.dma("sp", xs.ap, xs_tok[t0:t0 + 128, :], [], [xs])
            kb.dma("sp", Bt.ap, B_tok[t0:t0 + 128, :], [], [Bt])
            kb.dma("sp", dtr.ap, dt_tok[t0:t0 + 128, :], [], [dtr])
            if full:
                kb.dma("sp", BT.ap, BT_d[:, :, t0:t0 + 128].rearrange("c p t -> p c t"), [], [BT])
                kb.dma("sp", CT.ap, CT_d[:, :, t0:t0 + 128].rearrange("c p t -> p c t"), [], [CT])
            c0 = 32 * dirn
            kb.tt("dve", dts.ap, dtr.ap[:, c0:c0 + 32], rows.ap[:, c0:c0 + 32], ALU.add, [dtr, rows], [dts])
            kb.act(dts.ap, dts.ap, AF.Exp, [dts], [dts])
            kb.act(dts.ap, dts.ap, AF.Ln, [dts, onec], [dts], bias=onec.ap)
            kb.tt("dve", av.ap, dts.ap, arow.ap[:, c0:c0 + 32], ALU.mult, [dts, arow], [av])
            pm = kb.bank("a")
            kb.mm(pm.ap[:, 0:32], cst.ap[:, M_INC, :], av.ap, True, True, [cst, av], [pm])
            kb.mm(pm.ap[:, 32:64], ONESF, av.ap, True, True, [cst, av], [pm])
            kb.cp("dve", acs.ap, pm.ap[:, 0:64], [pm], [acs])
            kb.tt("dve", dd.ap, acs.ap[:, 32:64], acs.ap[:, 0:32], ALU.subtract, [acs], [dd])
            kb.act(xes.ap, dd.ap, AF.Exp, [dd], [xes])
            kb.act(ea.ap, acs.ap[:, 0:32], AF.Exp, [acs], [ea])
            kb.act(dec.ap, acs.ap[:, 32:64], AF.Exp, [acs], [dec])
            if save:
                kb.tt("dve", Dt_s.ap, Dt_s.ap, dec.ap, ALU.mult, [Dt_s, dec], [Dt_s])
            kb.ts("dve", dts.ap, dts.ap, tmk.ap[:, ci:ci + 1], None, ALU.mult, None, [dts, tmk], [dts])
            kb.tt("dve", xes.ap, xes.ap, dts.ap, ALU.mult, [xes, dts], [xes])
            x3 = xs.ap.rearrange("p (h d) -> p h d", h=32)
            kb.tt("dve", xend.ap.rearrange("p (h d) -> p h d", h=32), x3, bc(xes.ap, 64), ALU.mult, [xs, xes], [xend])
            if full:
                kb.tt("pool", xdt.ap.rearrange("p (h d) -> p h d", h=32), x3, bc(dts.ap, 64), ALU.mult, [xs, dts], [xdt])
            for g in range(4):
                gs = slice(g * 512, (g + 1) * 512)
                Lm, Em, Mh, cbm, ytmp = LmL[g % 2], EmL[g % 2], MhL[g % 2], cbmL[g % 2], ytmpL[g % 2]
                if full and BIS != 4:
                    pyo = kb.bank("b")
                    kb.mm(pyo.ap, CT.ap[:, g, :], S_ssdb.ap[:, g, :], True, True, [CT, S_ssdb.ts[g]], [pyo])
                    pcb = kb.bank("a")
                    kb.mm(pcb.ap[:, 0:128], BT.ap[:, g, :], CT.ap[:, g, :], True, True, [BT, CT], [pcb])
                    kb.tt("dve", cbm.ap, pcb.ap[:, 0:128], cst.ap[:, M_INC, :], ALU.mult, [pcb, cst], [cbm])
                    kb.tt("pool", Lm.ap, cst.ap[:, M_STR:M_STR + 1, :].to_broadcast([128, 8, 128]),
                          bc(av.ap[:, g * 8:(g + 1) * 8], 128), ALU.mult, [cst, av], [Lm])
                    pys = kb.bank("d")
                    for hh in range(2):
                        psg = kb.bank("c")
                        for u in range(4):
                            kb.mm(psg.ap[:, u * 128:(u + 1) * 128], Lm.ap[:, hh * 4 + u, :], cst.ap[:, M_INC, :], True, True, [Lm, cst], [psg])
                        kb.act(Em.ap[:, hh * 4:hh * 4 + 4, :], psg.ap.rearrange("p (a b) -> p a b", a=4), AF.Exp, [psg], [Em])
                    kb.tt("dve", Mh.ap, Em.ap, cbm.ap.unsqueeze(1).to_broadcast([128, 8, 128]), ALU.mult, [Em, cbm], [Mh])
                    for h in range(8):
                        hd = (g * 8 + h) * 64
                        kb.mm(pys.ap[:, h * 64:(h + 1) * 64], Mh.ap[:, h, :], xdt.ap[:, hd:hd + 64], True, True, [Mh, xdt], [pys])
                    kb.tt("dve", ytmp.ap.rearrange("p (h d) -> p h d", h=8), pyo.ap.rearrange("p (h d) -> p h d", h=8),
                          bc(ea.ap[:, g * 8:(g + 1) * 8], 64), ALU.mult, [pyo, ea], [ytmp])
                    kb.tt("dve", ysb.ap[:, gs], ytmp.ap, pys.ap, ALU.add, [ytmp, pys], [ysb.ts[g]])
                pu = kb.bank("b")
                kb.mm(pu.ap, Bt.ap[:, g * 128:(g + 1) * 128], xend.ap[:, gs], True, True, [Bt, xend], [pu])
                S3 = S_ssd.ap[:, g, :].rearrange("p (h d) -> p h d", h=8)
                kb.tt("pool", S3, S3, bc(dec.ap[:, g * 8:(g + 1) * 8], 64), ALU.mult, [S_ssd.ts[g], dec], [S_ssd.ts[g]])
                kb.tt("dve", S_ssd.ap[:, g, :], S_ssd.ap[:, g, :], pu.ap, ALU.add, [S_ssd.ts[g], pu], [S_ssd.ts[g]])
                if full:
                    kb.cp("act", S_ssdb.ap[:, g, :], S_ssd.ap[:, g, :], [S_ssd.ts[g]], [S_ssdb.ts[g]])
            if full and not final:
                kb.dma("pool", yf_tok[t0:t0 + 128, 0:2048], ysb.ap, [ysb], [])
            if final:
                kb.dma("sp", yfb.ap, yf_tok[t0:t0 + 128, 0:2048], [], [yfb])
                kb.dma("sp", zb.ap, z_tok[t0:t0 + 128, :], [], [zb])
                kb.tt("dve", ysb.ap, ysb.ap, yfb.ap, ALU.add, [ysb, yfb], [ysb])
                kb.tt("pool", yfb.ap.rearrange("p (h d) -> p h d", h=32), x3, bc(rows.ap[:, R_DSK:R_DSK + 32], 64), ALU.mult,
                      [xs, rows], [yfb])
                kb.tt("dve", ysb.ap, ysb.ap, yfb.ap, ALU.add, [ysb, yfb], [ysb])
                kb.act(szb.ap, zb.ap, AF.Silu, [zb], [szb])
                kb.tt("dve", ysb.ap, ysb.ap, szb.ap, ALU.mult, [ysb, szb], [ysb])
                for g in range(4):
                    kb.act(junk.ap, ysb.ap[:, g * 512:(g + 1) * 512], AF.Square, [ysb], [junk, ssq], accum=ssq.ap[:, g:g + 1])
                kb.act(ssq.ap[:, 0:4], ssq.ap[:, 0:4], AF.Sqrt, [ssq, epsc], [ssq], bias=epsc.ap, scale=1.0 / 512)
                kb.op("dve", lambda e: e.reciprocal(ssq.ap[:, 0:4], ssq.ap[:, 0:4]), [ssq], [ssq])
                mo = mixb[n_ % 2]
                for g in range(4):
                    kb.stt("dve", mo.ap[:, g * 512:(g + 1) * 512], ysb.ap[:, g * 512:(g + 1) * 512], ssq.ap[:, g:g + 1],
                           rows.ap[:, R_SSDN + g * 512:R_SSDN + (g + 1) * 512], ALU.mult, ALU.mult, [ysb, ssq, rows], [mo])
                kb.dma("pool", mix_tok[t0:t0 + 128, 0:2048], mo.ap, [mo], [])
            for sn, sub in enumerate((0, 1) if fwd else (1, 0)):
                s0 = t0 + sub * 64
                qb_ = (n_ * 2 + sn) % 2
                kt, vt, qT, kT, la = kt_b[qb_], vt_b[qb_], qT_b[qb_], kT_b[qb_], lowa[qb_]
                sp_, EG, Gs = spL[qb_], EGL[qb_], GsL[qb_]
                kb.dma("sp", kt.ap, k_tok[s0:s0 + 64, :], [], [kt])
                kb.dma("sp", vt.ap, v_tok[s0:s0 + 64, :], [], [vt])
                kb.dma("sp", la.ap[0:16, :], lowT[dirn, :, s0:s0 + 64], [], [la])
                if full:
                    kb.dma("sp", qT.ap, qT_d[:, :, s0:s0 + 64].rearrange("c p t -> p c t"), [], [qT])
                    kb.dma("sp", kT.ap, kT_d[:, :, s0:s0 + 64].rearrange("c p t -> p c t"), [], [kT])
                for hf in range(2):
                    pg = kb.bank("c")
                    kb.mm(pg.ap[0:64, :], la.ap[0:17, :], wgt.ap[0:17, hf * 512:(hf + 1) * 512], True, True, [la, wgt], [pg])
                    kb.act(sp_.ap[:, hf * 512:(hf + 1) * 512], pg.ap[0:64, :], AF.Exp, [pg], [sp_], scale=-1.0)
                kb.act(sp_.ap, sp_.ap, AF.Ln, [sp_, onec], [sp_], bias=onec.ap[0:64, :])
                pG = kb.bank("a")
                for kc in range(8):
                    kb.mm(pG.ap[:, kc * 64:(kc + 1) * 64], sp_.ap[0:64, kc * 128:(kc + 1) * 128], cst.ap[0:64, G_INC, 0:64], True, True,
                          [sp_, cst], [pG])
                pG3 = pG.ap.rearrange("p (a b) -> p a b", a=8)
                kb.cp("dve", Gs.ap, pG3, [pG], [Gs])
                kb.act(EG.ap, Gs.ap, AF.Exp, [Gs], [EG])
                if save:
                    kb.tt("dve", Dt_g.ap, Dt_g.ap, EG.ap[:, :, last], ALU.mult, [Dt_g, EG], [Dt_g])
                for hf in range(2):
                    pD = kb.bank("d")
                    kb.mm(pD.ap[0:64, :], cst.ap[0:64, G_STR, 0:64], sp_.ap[0:64, hf * 512:(hf + 1) * 512], True, True, [sp_, cst], [pD])
                    kb.act(Ek.ap[:, hf * 512:(hf + 1) * 512], pD.ap[0:64, :], AF.Exp, [pD], [Ek])
                kb.tt("pool", kend.ap, kt.ap, Ek.ap, ALU.mult, [kt, Ek], [kend])
                if full and BIS != 3:
                    kb.tt("dve", Gm.ap, Gs.ap, Gs.ap[:, :, mid:mid + 1].to_broadcast([128, 8, 64]), ALU.subtract, [Gs], [Gm])
                    kb.act(E1.ap, Gm.ap, AF.Exp, [Gm], [E1])
                    kb.act(E2.ap, Gm.ap, AF.Exp, [Gm], [E2], scale=-1.0)
                    kb.tt("dve", qp.ap, qT.ap, E1.ap, ALU.mult, [qT, E1], [qp])
                    kb.tt("pool", kp.ap, kT.ap, E2.ap, ALU.mult, [kT, E2], [kp])
                    kb.tt("dve", qin.ap, qT.ap, EG.ap, ALU.mult, [qT, EG], [qin])
                    pA = kb.bank("a")
                    for h in range(4):
                        for kk in range(2):
                            kb.mm(pA.ap[0:64, h * 64:(h + 1) * 64], kp.ap[:, 2 * h + kk, :], qp.ap[:, 2 * h + kk, :], kk == 0, kk == 1,
                                  [kp, qp], [pA])
                    kb.tt("dve", Am.ap, pA.ap[0:64, 0:256].rearrange("p (a b) -> p a b", a=4),
                          cst.ap[0:64, M_INC:M_INC + 1, 0:64].to_broadcast([64, 4, 64]), ALU.mult, [pA, cst], [Am])
                    for h in range(4 if BIS != 5 else 0):
                        po = kb.bank("b")
                        kb.mm(po.ap[0:64, :], Am.ap[:, h, :], vt.ap[:, h * 512:(h + 1) * 512], True, False, [Am, vt], [po])
                        for kk in range(2):
                            kb.mm(po.ap[0:64, :], qin.ap[:, 2 * h + kk, :], S_glab.ap[:, 2 * h + kk, :], False, kk == 1,
                                  [qin, S_glab.ts[2 * h + kk]], [po])
                        kb.cp("act", osb.ap[:, h * 512:(h + 1) * 512], po.ap[0:64, :], [po], [osb.ts[h]])
                for h in range(4):
                    for kk in range(2):
                        j = 2 * h + kk
                        pu = kb.bank("d" if kk else "c")
                        kb.mm(pu.ap, kend.ap[:, j * 128:(j + 1) * 128], vt.ap[:, h * 512:(h + 1) * 512], True, True, [kend, vt], [pu])
                        kb.stt("dve", S_gla.ap[:, j, :], S_gla.ap[:, j, :], EG.ap[:, j, last:last + 1], pu.ap, ALU.mult, ALU.add,
                               [S_gla.ts[j], EG, pu], [S_gla.ts[j]])
                        if full:
                            kb.cp("pool", S_glab.ap[:, j, :], S_gla.ap[:, j, :], [S_gla.ts[j]], [S_glab.ts[j]])
                if full and not final:
                    kb.dma("pool", yf_tok[s0:s0 + 64, 2048:4096], osb.ap, [osb], [])
                if final:
                    kb.dma("sp", yfb.ap[0:64, :], yf_tok[s0:s0 + 64, 2048:4096], [], [yfb])
                    kb.dma("sp", zb.ap[0:64, :], go_tok[s0:s0 + 64, :], [], [zb])
                    kb.tt("dve", osb.ap, osb.ap, yfb.ap[0:64, :], ALU.add, [osb, yfb], [osb])
                    for h in range(4):
                        kb.act(junk.ap[0:64, :], osb.ap[:, h * 512:(h + 1) * 512], AF.Square, [osb], [junk, ssq],
                               accum=ssq.ap[0:64, 4 + h:5 + h])
                    kb.act(ssq.ap[0:64, 4:8], ssq.ap[0:64, 4:8], AF.Sqrt, [ssq, epsc], [ssq], bias=epsc.ap[0:64, :], scale=1.0 / 512)
                    kb.op("dve", lambda e: e.reciprocal(ssq.ap[0:64, 4:8], ssq.ap[0:64, 4:8]), [ssq], [ssq])
                    for h in range(4):
                        kb.stt("dve", osb.ap[:, h * 512:(h + 1) * 512], osb.ap[:, h * 512:(h + 1) * 512], ssq.ap[0:64, 4 + h:5 + h],
                               rows.ap[0:64, R_GLAN:R_GLAN + 512], ALU.mult, ALU.mult, [osb, ssq, rows], [osb])
                    kb.act(szb.ap[0:64, :], zb.ap[0:64, :], AF.Silu, [zb], [szb])
                    mo = mixb[(n_ * 2 + sn) % 2]
                    kb.tt("dve", mo.ap[0:64, :], osb.ap, szb.ap[0:64, :], ALU.mult, [osb, szb], [mo])
                    kb.dma("pool", mix_tok[s0:s0 + 64, 2048:4096], mo.ap[0:64, :], [mo], [])

    agS = Tile(None)

    def save_states(dirn):
        o = dirn * 6184
        for tl, flat, lo, n in ((S_ssd, S_ssd.ap.rearrange("p a b -> p (a b)"), 0, 2048),
                                (S_gla, S_gla.ap.rearrange("p a b -> p (a b)"), 2048, 4096),
                                (Dt_s, Dt_s.ap, 6144, 32), (Dt_g, Dt_g.ap, 6176, 8)):
            for k, a, b, off in segs(o + lo, o + lo + n):
                kb.dma("pool", xg_dp[k][:, a:b], flat[:, off:off + (b - a)], [tl], [agS])

    scan_pass(0, False, False, save=True)
    save_states(0)
    scan_pass(1, False, False, save=True)
    save_states(1)
    kb.barrier()
    for k in range(8):
        kb.coll("AllGather", RG, xg_dp[k][:, :], xg_gp[k][:, :], [agS], [agS])
    aoff_B = kb.aoff
    kb.aoff = aoff_keep
    acc = kb.sb([128, 6144], F32)
    Sr = [kb.sb([128, 6184], F32) for _ in range(2)]
    Deff = [kb.sb([128, 40], F32) for _ in range(2)]
    tmpc = kb.sb([128, 512], F32)
    for dirn in range(2):
        kb.op("dve", lambda e: e.memset(acc.ap, 0.0), [], [acc])
        rr = list(range(GR)) if dirn == 0 else list(range(GR - 1, -1, -1))
        for n_, r in enumerate(rr):
            sr, de = Sr[n_ % 2], Deff[n_ % 2]
            o = dirn * 6184
            for kk in range(4):
                kb.dma("sp", sr.ap[:, kk * PW:(kk + 1) * PW], xg_gp[dirn * 4 + kk][r * 128:(r + 1) * 128, :], [agS], [sr])
            b = lmk.ap[:, dirn * 8 + r:dirn * 8 + r + 1]
            kb.ts("dve", de.ap, sr.ap[:, 6144:6184], -1.0, b, ALU.add, ALU.mult, [sr, lmk], [de])
            kb.ts("dve", de.ap, de.ap, 1.0, None, ALU.add, None, [de], [de])
            a3 = acc.ap[:, 0:2048].rearrange("p (h d) -> p h d", h=32)
            kb.tt("dve", a3, a3, bc(de.ap[:, 0:32], 64), ALU.mult, [acc, de], [acc])
            kb.stt("dve", acc.ap[:, 0:2048], sr.ap[:, 0:2048], b, acc.ap[:, 0:2048], ALU.mult, ALU.add, [sr, lmk, acc], [acc])
            for j in range(8):
                sl = slice(2048 + j * 512, 2048 + (j + 1) * 512)
                kb.ts("pool", tmpc.ap, sr.ap[:, sl], b, None, ALU.mult, None, [sr, lmk], [tmpc])
                kb.stt("dve", acc.ap[:, sl], acc.ap[:, sl], de.ap[:, 32 + j:33 + j], tmpc.ap, ALU.mult, ALU.add, [acc, de, tmpc], [acc])
        kb.dma("pool", sin_d[dirn], acc.ap, [acc], [])
    kb.barrier()
    kb.aoff = aoff_B
    scan_pass(0, True, False, init=True)
    kb.barrier()
    scan_pass(1, True, True, init=True)
    kb.barrier()
    kb.reset()
    if STOP_AFTER == "B":
        return finalize(nc, kb, stack)
    WC = 512
    NTC = T // WC
    kb.pp = {"ab": [[0, 1, 2, 3], 0], "o": [[4, 5], 0], "m": [[6, 7], 0]}
    xT = kb.sb([128, 16, WC], F32, n=16)
    hT = kb.sb([128, 16, WC], BF16, n=16)
    oT = kb.sb([128, 16, WC], BF16, n=16)
    hid = kb.sb([128, 44, WC], BF16, n=44)
    mixT = Tile(hid.ap.rearrange("p a b -> p (a b)")[:, 0:32 * WC].rearrange("p (a b) -> p a b", a=32), 32)
    yT = Tile(hid.ap.rearrange("p a b -> p (a b)")[:, 0:32 * WC].bitcast(F32).rearrange("p (a b) -> p a b", a=16), 16)
    ws8 = [kb.sb([128, 16, 256], BF16) for _ in range(3)]
    ws11 = [kb.sb([128, 44, 128], BF16) for _ in range(2)]
    sqb = [kb.sb([128, 512], F32) for _ in range(2)]
    rstd = kb.sb([128, 512], F32)
    sa = [kb.sb([128, 512], F32) for _ in range(2)]
    kTm = kb.sb([128, 16, 256], BF16)
    vm = kb.sb([128, 2, 2048], BF16)
    mtok = [kb.sb([128, 2048], F32)] * 2
    qTh = kb.sb([128, 4, WC], BF16)
    PT = kb.sb([128, 2, WC], BF16)
    Pf = kb.sb([128, 256], F32)
    Pb = kb.sb([128, 256], BF16)
    st4 = kb.sb([128, 4], F32)

    mT = Tile(hid.ap.rearrange("p a b -> p (a b)")[:, 0:16 * 256].rearrange("p (a b) -> p a b", a=16), 1)
    for mt in range(2):
        mk = mtok[mt]
        kb.dma("sp", mk.ap, mem_in[mt * 128:(mt + 1) * 128, :], [], [mk])
        for q4 in range(4):
            kb.act(sa[q4 % 2].ap, mk.ap[:, q4 * 512:(q4 + 1) * 512], AF.Square, [mk], [sa[q4 % 2], st4], accum=st4.ap[:, q4:q4 + 1])
        kb.op("dve", lambda e, o=rstd.ap[:, mt:mt + 1], i=st4.ap: e.reduce_sum(o, i, mybir.AxisListType.X), [st4], [rstd])
        kb.act(rstd.ap[:, mt:mt + 1], rstd.ap[:, mt:mt + 1], AF.Sqrt, [rstd, epsc], [rstd], bias=epsc.ap, scale=1.0 / D)
        kb.op("dve", lambda e, a=rstd.ap[:, mt:mt + 1]: e.reciprocal(a, a), [rstd], [rstd])
        kb.ts("dve", mk.ap, mk.ap, rstd.ap[:, mt:mt + 1], None, ALU.mult, None, [mk, rstd], [mk])
        for k4 in range(4):
            pm = kb.bank("m")
            for u in range(4):
                kc = k4 * 4 + u
                kb.tr(pm.ap[:, u * 128:(u + 1) * 128], mk.ap[:, kc * 128:(kc + 1) * 128], IDF, [mk, cst], [pm])
            for u in range(4):
                kc = k4 * 4 + u
                kb.ts("dve", mT.ap[:, kc, mt * 128:(mt + 1) * 128], pm.ap[:, u * 128:(u + 1) * 128], ncol(N_MEM)[:, kc:kc + 1], None,
                      ALU.mult, None, [pm, cols], [mT])
    reqs = [("wckv", 16, pi * 256, 256) for pi in range(16)]
    st = PanelStream(ws8, reqs)
    for pi in range(8):
        pv, ps_ = st.get(pi)
        for j in range(2):
            po = kb.bank("ab")
            for kc in range(16):
                kb.mm(po.ap[:, 0:256], pv[:, kc, j * 128:(j + 1) * 128], mT.ap[:, kc, :], kc == 0, kc == 15, [ps_, mT], [po])
            kb.cp("act" if j else "dve", kTm.ap[:, pi * 2 + j, :], po.ap[:, 0:256], [po], [kTm])
        st.done(pi)
    for pi in range(8, 16):
        pv, ps_ = st.get(pi)
        for mt in range(2):
            po = kb.bank("ab")
            for kc in range(16):
                kb.mm(po.ap[:, 0:256], mT.ap[:, kc, mt * 128:(mt + 1) * 128], pv[:, kc, :], kc == 0, kc == 15, [ps_, mT], [po])
            kb.cp("act" if mt else "dve", vm.ap[:, mt, (pi - 8) * 256:(pi - 7) * 256], po.ap[:, 0:256], [po], [vm])
        st.done(pi)

    SCL = 512.0 ** -0.5
    for ti in range(NTC):
        t0 = ti * WC
        kb.dma("sp", xT.ap, x1T[:, :, t0:t0 + WC].rearrange("k p t -> p k t"), [], [xT])
        for sub in range(WC // 128):
            mk = mtok[sub % 2]
            mkb = mk.ap.bitcast(BF16)
            kb.dma("sp", mkb, mix_tok[t0 + sub * 128:t0 + (sub + 1) * 128, :], [], [mk])
            for c4 in range(8):
                pm = kb.bank("m")
                pmb = pm.ap[:, 0:256].bitcast(BF16)
                for u in range(4):
                    c = c4 * 4 + u
                    kb.tr(pmb[:, u * 128:(u + 1) * 128], mkb[:, c * 128:(c + 1) * 128], IDB, [mk, cstb], [pm])
                kb.cp("act" if c4 % 2 else "dve", mixT.ap[:, c4 * 4:c4 * 4 + 4, sub * 128:(sub + 1) * 128],
                      pmb.rearrange("p (a b) -> p a b", a=4), [pm], mixT.ts[c4 * 4:c4 * 4 + 4])
        st = PanelStream(ws8, [("wout", 32, dc * 128, 128) for dc in range(16)])
        for dc in range(16):
            pv, ps_ = st.get(dc)
            po = kb.bank("o")
            for cc in range(32):
                kb.mm(po.ap[:, 0:WC], pv[:, cc, :], mixT.ap[:, cc, :], cc == 0, cc == 31, [ps_, mixT.ts[cc]], [po])
            kb.tt("dve", xT.ap[:, dc, :], xT.ap[:, dc, :], po.ap[:, 0:WC], ALU.add, [po, xT.ts[dc]], [xT.ts[dc]])
            st.done(dc)
        rmsnorm_fm(xT, hT, ncol(N_XA), sqb, rstd, kb.bank("m"))
        st = PanelStream(ws8, [("wcq", 16, pi * 256, 256) for pi in range(8)])
        for h in range(4):
            for pj in range(2):
                pv, ps_ = st.get(h * 2 + pj)
                for j in range(2):
                    po = kb.bank("ab")
                    for kc in range(16):
                        kb.mm(po.ap[:, 0:WC], pv[:, kc, j * 128:(j + 1) * 128], hT.ap[:, kc, :], kc == 0, kc == 15, [ps_, hT.ts[kc]], [po])
                    kb.cp("act" if j else "dve", qTh.ap[:, pj * 2 + j, :], po.ap[:, 0:WC], [po], [qTh])
                st.done(h * 2 + pj)
            for sub in range(WC // 128):
                pss = kb.bank("o")
                for j in range(4):
                    kb.mm(pss.ap[:, 0:256], qTh.ap[:, j, sub * 128:(sub + 1) * 128], kTm.ap[:, h * 4 + j, :], j == 0, j == 3, [qTh, kTm], [pss])
                kb.op("dve", lambda e, o=st4.ap[:, 0:1], i=pss.ap[:, 0:256]: e.reduce_max(o, i, mybir.AxisListType.X), [pss], [st4])
                kb.ts("dve", st4.ap[:, 1:2], st4.ap[:, 0:1], -SCL, None, ALU.mult, None, [st4], [st4])
                kb.act(Pf.ap, pss.ap[:, 0:256], AF.Exp, [pss, st4], [Pf, st4], bias=st4.ap[:, 1:2], scale=SCL, accum=st4.ap[:, 2:3])
                kb.op("dve", lambda e, o=st4.ap[:, 3:4], i=st4.ap[:, 2:3]: e.reciprocal(o, i), [st4], [st4])
                kb.ts("dve", Pb.ap, Pf.ap, st4.ap[:, 3:4], None, ALU.mult, None, [Pf, st4], [Pb])
                pm = kb.bank("m")
                pmb = pm.ap[:, 0:128].bitcast(BF16)
                for mc in range(2):
                    kb.tr(pmb[:, mc * 128:(mc + 1) * 128], Pb.ap[:, mc * 128:(mc + 1) * 128], IDB, [Pb, cstb], [pm])
                kb.cp("act", PT.ap[:, :, sub * 128:(sub + 1) * 128], pmb.rearrange("p (a b) -> p a b", a=2), [pm], [PT])
            for j in range(4):
                po = kb.bank("ab")
                for mc in range(2):
                    kb.mm(po.ap[:, 0:WC], vm.ap[:, mc, (h * 4 + j) * 128:(h * 4 + j + 1) * 128], PT.ap[:, mc, :], mc == 0, mc == 1, [vm, PT], [po])
                kb.cp("act" if j % 2 else "dve", oT.ap[:, h * 4 + j, :], po.ap[:, 0:WC], [po], [oT.ts[h * 4 + j]])
        st = PanelStream(ws8, [("wco", 16, pi * 256, 256) for pi in range(8)])
        for pi in range(8):
            pv, ps_ = st.get(pi)
            for j in range(2):
                dc = pi * 2 + j
                po = kb.bank("o")
                for kc in range(16):
                    kb.mm(po.ap[:, 0:WC], pv[:, kc, j * 128:(j + 1) * 128], oT.ap[:, kc, :], kc == 0, kc == 15, [ps_, oT.ts[kc]], [po])
                kb.tt("dve", xT.ap[:, dc, :], xT.ap[:, dc, :], po.ap[:, 0:WC], ALU.add, [po, xT.ts[dc]], [xT.ts[dc]])
            st.done(pi)
        rmsnorm_fm(xT, hT, ncol(N_FFN2), sqb, rstd, kb.bank("m"))
        ffn(xT, hT, hid, "w1b", "w3b", "w2b", ws8, ws11, sa, ti)
        rmsnorm_fm(xT, yT, ncol(N_FIN), sqb, rstd, kb.bank("m"))
        for sub in range(WC // 128):
            yo = mtok[sub % 2]
            for k4 in range(4):
                pm = kb.bank("m")
                for u in range(4):
                    kc = k4 * 4 + u
                    kb.tr(pm.ap[:, u * 128:(u + 1) * 128], yT.ap[:, kc, sub * 128:(sub + 1) * 128], IDF, [yT.ts[kc], cst], [pm])
                kb.cp("act" if k4 % 2 else "dve", yo.ap[:, k4 * 512:(k4 + 1) * 512], pm.ap, [pm], [yo])
            kb.dma("pool", y_out[t0 + sub * 128:t0 + (sub + 1) * 128, :], yo.ap, [yo], [])
    return finalize(nc, kb, stack)


STOP_AFTER = None
BIS = 0


def finalize(nc, kb, stack):
    kb.barrier()
    with nc.Block() as block:
        @block.tensor
        def _(eng):
            kb.replay("pe", eng)

        @block.scalar
        def _(eng):
            kb.replay("act", eng)

        @block.vector
        def _(eng):
            kb.replay("dve", eng)

        @block.gpsimd
        def _(eng):
            kb.replay("pool", eng)

        @block.sync
        def _(eng):
            kb.replay("sp", eng)
    stack.close()
    return nc


def make_consts():
    p = np.arange(128)[:, None]
    f = np.arange(128)[None, :]
    mats = [p == f, p <= f, p >= f, p > f, p < f, np.ones((128, 128), bool)]
    c = [m.astype(np.float32) for m in mats]
    c += [m.astype(np.float32) * (-1.0 / 16.0) for m in mats[1:5]]
    return np.ascontiguousarray(np.concatenate(c, axis=1))


def colify(v):
    return np.asarray(v, np.float32).reshape(-1, 128).T


def host_inputs(inp, x, mem):
    g = lambda k: np.asarray(inp[k], np.float32)
    cols = np.concatenate(
        [colify(g(k).reshape(-1)) for k in ("ffn1_norm", "mix_norm", "xattn_norm", "mem_norm", "ffn2_norm", "final_norm")]
        + [g("conv_w")[0].T.reshape(24, 128, 5).transpose(1, 0, 2).reshape(128, 120),
           colify(g("conv_b").reshape(-1))], axis=1)
    rows = np.concatenate([
        g("dt_bias_fwd").reshape(-1), g("dt_bias_bwd").reshape(-1),
        g("a_log_fwd").reshape(-1), g("a_log_bwd").reshape(-1),
        g("d_skip").reshape(-1), g("ssd_norm").reshape(-1), g("gla_norm").reshape(-1)])[None, :]
    rows = np.ascontiguousarray(rows, np.float32)
    wg = np.stack([np.concatenate([g("gla_gate_w_fwd")[0], g("gla_gate_b_fwd")], 0),
                   np.concatenate([g("gla_gate_w_bwd")[0], g("gla_gate_b_bwd")], 0)], 0)
    m = {"x": np.ascontiguousarray(x, np.float32), "mem": np.ascontiguousarray(mem, np.float32),
         "consts": make_consts(), "cols": np.ascontiguousarray(cols, np.float32), "rows": rows,
         "wg": np.ascontiguousarray(wg, np.float32)}
    for k in ("ffn1_w1", "ffn1_w3", "ffn1_w2", "w_in", "w_out", "w_cq", "w_ckv", "w_co", "ffn2_w1", "ffn2_w3", "ffn2_w2"):
        m[k] = np.ascontiguousarray(g(k)[0])
    return m


T_CORE = 4096
_CACHE = {}


def tmask_for(T, valid):
    nch = T // 128
    m = (np.arange(nch) * 128 < valid).astype(np.float32)
    return np.ascontiguousarray(np.broadcast_to(m[None, :], (128, nch)))


def lmask_for(c, chain):
    m = np.zeros((128, 32), np.float32)
    i = chain.index(c)
    for r in chain[:i]:
        m[:, r] = 1.0
    for r in chain[i + 1:]:
        m[:, 8 + r] = 1.0
    if i > 0:
        m[:, 16 + chain[i - 1]] = 1.0
    if i + 1 < len(chain):
        m[:, 24 + chain[i + 1]] = 1.0
    return m


def kernel(**inputs):
    inp = {k: np.asarray(v) for k, v in inputs.items()}
    xp, xs_, mp, ms = inp["x_prompt"], inp["x_sample"], inp["mem_prompt"], inp["mem_sample"]
    T = T_CORE
    if T not in _CACHE:
        _CACHE[T] = build(T)
    nc = _CACHE[T]
    in_maps = []
    base = None
    for c in range(8):
        if c < 4:
            x, mem, chain, cl = xp[c], mp[c], [c], c
        else:
            x, mem, chain, cl = xs_[0, (c - 4) * T:(c - 3) * T], ms[0], [0, 1, 2, 3], c - 4
        if base is None:
            base = host_inputs(inp, x, mem)
        m = dict(base)
        m["x"] = np.ascontiguousarray(x, np.float32)
        m["mem"] = np.ascontiguousarray(mem, np.float32)
        m["tmask"] = tmask_for(T, T)
        m["lmask"] = lmask_for(cl, chain)
        in_maps.append(m)
    res = run_bass_kernel_spmd(nc, in_maps, core_ids=list(range(8)))
    ys = [r["y"] for r in res.results]
    y_prompt = np.stack(ys[0:4], 0).astype(np.float32)
    y_sample = np.concatenate(ys[4:8], 0)[None].astype(np.float32)
    return (y_prompt, y_sample)
```
